# Optimizing a Trainium2 kernel written in Bass

```python
import jax, jax.numpy as jnp
from jax import lax
import numpy as np

D_MODEL = 1024
BATCH = 16
SEQ = 2048
DEPTH = 2

CTX_LEN = 256
GRID_W = 64
HEAD_DIM = 64
ROPE_THETA = 10000.0
NORM_EPS = 1e-6
Q_BLOCK = 128
NEG_INF = -1e30
MIX_HEADS = D_MODEL // HEAD_DIM

A_HEADS = MIX_HEADS // 2
A_KV_HEADS = 2
A_GROUP = A_HEADS // A_KV_HEADS
A_Q = A_HEADS * HEAD_DIM
A_KV = A_KV_HEADS * HEAD_DIM
B_HEADS = MIX_HEADS // 2
B_Q_RANK = D_MODEL // 4
B_KV_RANK = D_MODEL // 8
B_NOPE_DIM = 64
B_ROPE_DIM = 32
B_V_DIM = 64
B_QK_DIM = B_NOPE_DIM + B_ROPE_DIM
EVEN_SPLITS = (A_Q, A_Q + A_KV, A_Q + 2 * A_KV, A_Q + 2 * A_KV + B_Q_RANK,
               A_Q + 2 * A_KV + B_Q_RANK + B_KV_RANK)
EVEN_IN = EVEN_SPLITS[-1] + B_ROPE_DIM
EVEN_OUT = A_HEADS * HEAD_DIM + B_HEADS * B_V_DIM

C_GROUPS = MIX_HEADS // 2
C_WIDTH = C_GROUPS * HEAD_DIM
C_CHUNK = 128
D_HEADS = MIX_HEADS // 2
D_KV_HEADS = 2
D_GROUP = D_HEADS // D_KV_HEADS
D_Q = D_HEADS * HEAD_DIM
D_KV = D_KV_HEADS * HEAD_DIM
WINDOW = 128
ODD_SPLITS = (C_WIDTH, 2 * C_WIDTH, 2 * C_WIDTH + D_Q, 2 * C_WIDTH + D_Q + D_KV)
ODD_IN = ODD_SPLITS[-1] + D_KV
ODD_OUT = C_WIDTH + D_Q

D_FF = 2816
CONV_W = 3

N_EVEN = (DEPTH + 1) // 2
N_ODD = DEPTH // 2

kernel_name = "hybrid_diffusion_gqa_mla_gmlp_swa_convffn"


def rms_norm(x, g):
    xf = x.astype(jnp.float32)
    y = xf * lax.rsqrt(jnp.mean(xf * xf, axis=-1, keepdims=True) + NORM_EPS)
    return (y * g.astype(jnp.float32)).astype(x.dtype)


def layer_norm(x, g, b):
    xf = x.astype(jnp.float32)
    mu = jnp.mean(xf, axis=-1, keepdims=True)
    xc = xf - mu
    y = xc * lax.rsqrt(jnp.mean(xc * xc, axis=-1, keepdims=True) + NORM_EPS)
    return (y * g.astype(jnp.float32) + b.astype(jnp.float32)).astype(x.dtype)


def modulate(x, shift, scale):
    return x * (1 + scale) + shift


def axial_rope_tables(n_tok, dim):
    n_rows = n_tok // GRID_W
    rows = jnp.repeat(jnp.arange(n_rows), GRID_W).astype(jnp.float32)
    cols = jnp.tile(jnp.arange(GRID_W), n_rows).astype(jnp.float32)
    quarter = dim // 4
    inv_freq = ROPE_THETA ** (-jnp.arange(quarter, dtype=jnp.float32) / quarter)
    ang = jnp.concatenate([rows[:, None] * inv_freq, cols[:, None] * inv_freq], axis=-1)
    return jnp.cos(ang), jnp.sin(ang)


def apply_rope(x, cos, sin):
    half = x.shape[-1] // 2
    x1, x2 = x[..., :half], x[..., half:]
    cos = cos.astype(x.dtype)
    sin = sin.astype(x.dtype)
    return jnp.concatenate([x1 * cos - x2 * sin, x1 * sin + x2 * cos], axis=-1)


def to_q_heads(x, n_kv, group, d):
    b, t, _ = x.shape
    return x.reshape(b, t, n_kv, group, d).transpose(0, 2, 3, 1, 4)


def to_kv_heads(x, n_kv, d):
    b, t, _ = x.shape
    return x.reshape(b, t, n_kv, d).transpose(0, 2, 1, 3)


def merge_heads(o):
    b, hkv, g, t, d = o.shape
    return o.transpose(0, 3, 1, 2, 4).reshape(b, t, hkv * g * d)


def attend(q, k, v, scale, mask=None, sink=None):
    s = jnp.einsum('bhgqd,bhkd->bhgqk', q, k).astype(jnp.float32) * scale
    if mask is not None:
        s = jnp.where(mask, s, NEG_INF)
    if sink is not None:
        hkv, g = q.shape[1], q.shape[2]
        sink_col = jnp.broadcast_to(sink.astype(jnp.float32).reshape(1, hkv, g, 1, 1), s.shape[:-1] + (1,))
        s = jnp.concatenate([s, sink_col], axis=-1)
    p = jax.nn.softmax(s, axis=-1)
    if sink is not None:
        p = p[..., :-1]
    return jnp.einsum('bhgqk,bhkd->bhgqd', p.astype(v.dtype), v)


def blocked_attend(q, k, v, scale):
    b, hkv, g, s, dk = q.shape
    nb = s // Q_BLOCK
    qb = jnp.moveaxis(q.reshape(b, hkv, g, nb, Q_BLOCK, dk), 3, 0)
    out = lax.map(lambda qi: attend(qi, k, v, scale), qb)
    return jnp.moveaxis(out, 0, 3).reshape(b, hkv, g, s, v.shape[-1])


def window_attend(q, k, v, kc, vc, sink, scale):
    b, hkv, g, s, d = q.shape
    nb = s // Q_BLOCK
    n_ctx = kc.shape[2]
    pad = ((0, 0), (0, 0), (Q_BLOCK, Q_BLOCK), (0, 0))
    kp, vp = jnp.pad(k, pad), jnp.pad(v, pad)
    qb = jnp.moveaxis(q.reshape(b, hkv, g, nb, Q_BLOCK, d), 3, 0)
    offset = jnp.arange(Q_BLOCK)[:, None] + Q_BLOCK - jnp.arange(3 * Q_BLOCK)[None, :]
    band = jnp.abs(offset) <= WINDOW
    ctx_ok = jnp.ones((Q_BLOCK, n_ctx), dtype=bool)

    def one_block(args):
        n, qi = args
        start = n * Q_BLOCK
        kw = lax.dynamic_slice_in_dim(kp, start, 3 * Q_BLOCK, axis=2)
        vw = lax.dynamic_slice_in_dim(vp, start, 3 * Q_BLOCK, axis=2)
        kpos = start - Q_BLOCK + jnp.arange(3 * Q_BLOCK)
        local_ok = band & ((kpos >= 0) & (kpos < s))[None, :]
        mask = jnp.concatenate([ctx_ok, local_ok], axis=-1)
        return attend(qi, jnp.concatenate([kc, kw], axis=2), jnp.concatenate([vc, vw], axis=2),
                      scale, mask=mask, sink=sink)

    out = lax.map(one_block, (jnp.arange(nb), qb))
    return jnp.moveaxis(out, 0, 3).reshape(b, hkv, g, s, d)


def even_project(z, w_in, qa_g, ka_g, qlat_g, w_q_up, kvlat_g, w_kv_up, rope):
    h = z @ w_in
    qa, ka, va, cq, ckv, kr = jnp.split(h, EVEN_SPLITS, axis=-1)
    qa = rms_norm(to_q_heads(qa, A_KV_HEADS, A_GROUP, HEAD_DIM), qa_g)
    ka = rms_norm(to_kv_heads(ka, A_KV_HEADS, HEAD_DIM), ka_g)
    va = to_kv_heads(va, A_KV_HEADS, HEAD_DIM)
    qb = to_q_heads(rms_norm(cq, qlat_g) @ w_q_up, B_HEADS, 1, B_QK_DIM)
    kvb = to_kv_heads(rms_norm(ckv, kvlat_g) @ w_kv_up, B_HEADS, B_NOPE_DIM + B_V_DIM)
    kb_nope, vb = kvb[..., :B_NOPE_DIM], kvb[..., B_NOPE_DIM:]
    kr = kr[:, None]
    if rope is not None:
        cos_h, sin_h, cos_r, sin_r = rope
        qa = apply_rope(qa, cos_h, sin_h)
        ka = apply_rope(ka, cos_h, sin_h)
        qb = jnp.concatenate([qb[..., :B_NOPE_DIM], apply_rope(qb[..., B_NOPE_DIM:], cos_r, sin_r)], axis=-1)
        kr = apply_rope(kr, cos_r, sin_r)
    kb = jnp.concatenate([kb_nope, jnp.broadcast_to(kr, kb_nope.shape[:-1] + (B_ROPE_DIM,))], axis=-1)
    return qa, ka, va, qb, kb, vb


def even_mixer(zc, zl, w_in, qa_g, ka_g, qlat_g, w_q_up, kvlat_g, w_kv_up, w_out, rope, need_ctx):
    qa_c, ka_c, va_c, qb_c, kb_c, vb_c = even_project(zc, w_in, qa_g, ka_g, qlat_g, w_q_up, kvlat_g, w_kv_up, None)
    qa_l, ka_l, va_l, qb_l, kb_l, vb_l = even_project(zl, w_in, qa_g, ka_g, qlat_g, w_q_up, kvlat_g, w_kv_up, rope)
    scale_a = HEAD_DIM ** -0.5
    scale_b = B_QK_DIM ** -0.5
    ya = blocked_attend(qa_l, jnp.concatenate([ka_c, ka_l], axis=2), jnp.concatenate([va_c, va_l], axis=2), scale_a)
    yb = blocked_attend(qb_l, jnp.concatenate([kb_c, kb_l], axis=2), jnp.concatenate([vb_c, vb_l], axis=2), scale_b)
    yl = jnp.concatenate([merge_heads(ya), merge_heads(yb)], axis=-1) @ w_out
    yc = None
    if need_ctx:
        yac = attend(qa_c, ka_c, va_c, scale_a)
        ybc = attend(qb_c, kb_c, vb_c, scale_b)
        yc = jnp.concatenate([merge_heads(yac), merge_heads(ybc)], axis=-1) @ w_out
    return yc, yl


def chunk_sgu(u, v, ln_g, ln_b, w_s, b_s):
    bsz, t, _ = v.shape
    nc = t // C_CHUNK
    v = layer_norm(jax.nn.gelu(v), ln_g, ln_b)
    vg = v.reshape(bsz, nc, C_CHUNK, C_GROUPS, HEAD_DIM)
    mixed = jnp.einsum('gpq,bnqgd->bnpgd', w_s, vg) + b_s.T[:, :, None]
    return jax.nn.gelu(u) * mixed.reshape(bsz, t, C_WIDTH)


def odd_project(z, w_in, rope):
    h = z @ w_in
    u, v, qd, kd, vd = jnp.split(h, ODD_SPLITS, axis=-1)
    qd = to_q_heads(qd, D_KV_HEADS, D_GROUP, HEAD_DIM)
    kd = to_kv_heads(kd, D_KV_HEADS, HEAD_DIM)
    vd = to_kv_heads(vd, D_KV_HEADS, HEAD_DIM)
    if rope is not None:
        cos_h, sin_h = rope
        qd = apply_rope(qd, cos_h, sin_h)
        kd = apply_rope(kd, cos_h, sin_h)
    return u, v, qd, kd, vd


def odd_mixer(zc, zl, w_in, ln_g, ln_b, w_s, b_s, sink, w_out, rope, need_ctx):
    u_c, v_c, qd_c, kd_c, vd_c = odd_project(zc, w_in, None)
    u_l, v_l, qd_l, kd_l, vd_l = odd_project(zl, w_in, rope)
    scale = HEAD_DIM ** -0.5
    yc_l = chunk_sgu(u_l, v_l, ln_g, ln_b, w_s, b_s)
    yd_l = window_attend(qd_l, kd_l, vd_l, kd_c, vd_c, sink, scale)
    yl = jnp.concatenate([yc_l, merge_heads(yd_l)], axis=-1) @ w_out
    yc = None
    if need_ctx:
        yc_c = chunk_sgu(u_c, v_c, ln_g, ln_b, w_s, b_s)
        yd_c = attend(qd_c, kd_c, vd_c, scale, sink=sink)
        yc = jnp.concatenate([yc_c, merge_heads(yd_c)], axis=-1) @ w_out
    return yc, yl


def conv_ffn(z, w_up, conv_w, conv_b, w_down):
    h = z @ w_up
    t = h.shape[1]
    half = CONV_W // 2
    hp = jnp.pad(h, ((0, 0), (half, half), (0, 0)))
    h = sum(hp[:, i:i + t] * conv_w[i] for i in range(CONV_W)) + conv_b
    a, g = jnp.split(h, 2, axis=-1)
    return (jax.nn.silu(g) * a) @ w_down


def setup_inputs(seed: int = 0) -> dict:
    key = jax.random.key(seed)
    ks = iter(jax.random.split(key, 40))

    def nrm(shape, scale):
        return jax.random.normal(next(ks), shape, jnp.float32) * scale

    def gain(shape):
        return 1.0 + nrm(shape, 0.02)

    d = D_MODEL
    return {
        "x": nrm((BATCH, SEQ, d), 1.0),
        "c": nrm((BATCH, d), 1.0),
        "ctx": nrm((BATCH, CTX_LEN, d), 1.0),
        "c_ctx": nrm((d,), 1.0),
        "mod_w": nrm((DEPTH, d, 6 * d), 0.5 * d ** -0.5),
        "mod_b": nrm((DEPTH, 6 * d), 0.02),
        "norm1_g": gain((DEPTH, d)),
        "norm2_g": gain((DEPTH, d)),
        "ev_w_in": nrm((N_EVEN, d, EVEN_IN), d ** -0.5),
        "ev_qa_g": gain((N_EVEN, HEAD_DIM)),
        "ev_ka_g": gain((N_EVEN, HEAD_DIM)),
        "ev_qlat_g": gain((N_EVEN, B_Q_RANK)),
        "ev_w_q_up": nrm((N_EVEN, B_Q_RANK, B_HEADS * B_QK_DIM), B_Q_RANK ** -0.5),
        "ev_kvlat_g": gain((N_EVEN, B_KV_RANK)),
        "ev_w_kv_up": nrm((N_EVEN, B_KV_RANK, B_HEADS * (B_NOPE_DIM + B_V_DIM)), B_KV_RANK ** -0.5),
        "ev_w_out": nrm((N_EVEN, EVEN_OUT, d), EVEN_OUT ** -0.5),
        "od_w_in": nrm((N_ODD, d, ODD_IN), d ** -0.5),
        "od_ln_g": gain((N_ODD, C_WIDTH)),
        "od_ln_b": nrm((N_ODD, C_WIDTH), 0.02),
        "od_sgu_w": nrm((N_ODD, C_GROUPS, C_CHUNK, C_CHUNK), C_CHUNK ** -0.5),
        "od_sgu_b": 1.0 + nrm((N_ODD, C_GROUPS, C_CHUNK), 0.02),
        "od_sink": nrm((N_ODD, D_HEADS), 0.5),
        "od_w_out": nrm((N_ODD, ODD_OUT, d), ODD_OUT ** -0.5),
        "ffn_up": nrm((DEPTH, d, 2 * D_FF), d ** -0.5),
        "ffn_conv_w": nrm((DEPTH, CONV_W, 2 * D_FF), CONV_W ** -0.5),
        "ffn_conv_b": nrm((DEPTH, 2 * D_FF), 0.02),
        "ffn_down": nrm((DEPTH, D_FF, d), D_FF ** -0.5),
        "final_g": gain((d,)),
    }


def reference(x, c, ctx, c_ctx, mod_w, mod_b, norm1_g, norm2_g,
              ev_w_in, ev_qa_g, ev_ka_g, ev_qlat_g, ev_w_q_up, ev_kvlat_g, ev_w_kv_up, ev_w_out,
              od_w_in, od_ln_g, od_ln_b, od_sgu_w, od_sgu_b, od_sink, od_w_out,
              ffn_up, ffn_conv_w, ffn_conv_b, ffn_down, final_g):
    n_tok = x.shape[1]
    cos_h, sin_h = axial_rope_tables(n_tok, HEAD_DIM)
    cos_r, sin_r = axial_rope_tables(n_tok, B_ROPE_DIM)
    hl, hc = x, ctx
    for layer in range(DEPTH):
        need_ctx = layer != DEPTH - 1
        w_m, b_m = mod_w[layer], mod_b[layer]
        mod_l = [m[:, None, :] for m in jnp.split(jax.nn.silu(c) @ w_m + b_m, 6, axis=-1)]
        mod_c = jnp.split(jax.nn.silu(c_ctx) @ w_m + b_m, 6, axis=-1)
        zl = modulate(rms_norm(hl, norm1_g[layer]), mod_l[0], mod_l[1])
        zc = modulate(rms_norm(hc, norm1_g[layer]), mod_c[0], mod_c[1])
        if layer % 2 == 0:
            e = layer // 2
            yc, yl = even_mixer(zc, zl, ev_w_in[e], ev_qa_g[e], ev_ka_g[e], ev_qlat_g[e], ev_w_q_up[e],
                                ev_kvlat_g[e], ev_w_kv_up[e], ev_w_out[e],
                                (cos_h, sin_h, cos_r, sin_r), need_ctx)
        else:
            o = layer // 2
            yc, yl = odd_mixer(zc, zl, od_w_in[o], od_ln_g[o], od_ln_b[o], od_sgu_w[o], od_sgu_b[o],
                               od_sink[o], od_w_out[o], (cos_h, sin_h), need_ctx)
        hl = hl + mod_l[2] * yl
        hl = hl + mod_l[5] * conv_ffn(modulate(rms_norm(hl, norm2_g[layer]), mod_l[3], mod_l[4]),
                                      ffn_up[layer], ffn_conv_w[layer], ffn_conv_b[layer], ffn_down[layer])
        if need_ctx:
            hc = hc + mod_c[2] * yc
            hc = hc + mod_c[5] * conv_ffn(modulate(rms_norm(hc, norm2_g[layer]), mod_c[3], mod_c[4]),
                                          ffn_up[layer], ffn_conv_w[layer], ffn_conv_b[layer], ffn_down[layer])
    return rms_norm(hl, final_g)
```

```python
import os
import numpy as np
from contextlib import ExitStack
import concourse.bass as bass
import concourse.mybir as mybir
from concourse.bass_utils import run_bass_kernel_spmd

F32 = mybir.dt.float32
BF16 = mybir.dt.bfloat16
AF = mybir.ActivationFunctionType
ALU = mybir.AluOpType
PE, ACT, DVE, POOL, SP = "pe", "act", "dve", "pool", "sp"
ENGS = (PE, ACT, DVE, POOL, SP)
SEM_LIM = 30000
STRICT = os.environ.get('K_STRICT', '1') == '1'


class Tok:
    __slots__ = ("name", "w", "r", "rd", "pre", "frozen")

    def __init__(self, name="", pre=None):
        self.name = name
        self.w = None
        self.r = {}
        self.rd = []
        self.pre = pre
        self.frozen = False


class Op:
    __slots__ = ("eng", "fn", "deps", "odeps", "sem", "val", "dma_key", "n_dma", "waited", "idx", "is_dma",
                 "cost", "gidx", "pos", "fin", "done", "st", "tag")


class FW:
    def __init__(self, nc, n_phase=2):
        self.nc = nc
        self.ops = {e: [] for e in ENGS}
        self.n_phase = n_phase
        self.dma_keys = {}
        self.last_dma = {}
        self.nops = 0

    def op(self, eng, fn, reads=(), writes=(), dma_key=None, n_dma=1, cost=0.3):
        o = Op()
        o.eng = eng
        o.fn = fn
        o.is_dma = dma_key is not None
        o.dma_key = dma_key
        o.n_dma = n_dma
        o.waited = False
        o.sem = None
        o.val = 0
        o.cost = cost
        o.idx = len(self.ops[eng])
        o.gidx = self.nops
        self.nops += 1
        deps = {}
        odeps = {}

        def add(d, raw):
            if d is None or d is o:
                return
            if (not d.is_dma) and (not o.is_dma) and d.eng == eng:
                if eng == PE or (not raw and not STRICT):
                    odeps[id(d)] = d
                    return
            deps[id(d)] = d

        def add_all(t):
            add(t.w, False)
            for r in t.r.values():
                add(r, False)
            for r in t.rd:
                add(r, False)

        for t in reads:
            add(t.w, True)
        for t in writes:
            add_all(t)
            if t.pre:
                for p in t.pre:
                    add_all(p)
                t.pre = None
        for t in reads:
            if t.frozen:
                continue
            if o.is_dma:
                t.rd.append(o)
            else:
                prev = t.r.get(eng)
                if prev is not None and prev is not o:
                    odeps[id(prev)] = prev
                t.r[eng] = o
        for t in writes:
            assert not t.frozen, t.name
            t.w = o
            t.r = {}
            t.rd = []
        if o.is_dma:
            prev = self.last_dma.get(dma_key)
            if prev is not None:
                deps[id(prev)] = prev
            self.last_dma[dma_key] = o
        if os.environ.get("K_SERIAL") and getattr(self, "last", None) is not None:
            lo = self.last
            if lo.is_dma or lo.eng != eng or o.is_dma:
                deps[id(lo)] = lo
            else:
                odeps[id(lo)] = lo
        self.last = o
        o.deps = list(deps.values())
        o.odeps = list(odeps.values())
        for d in o.deps:
            d.waited = True
        self.ops[eng].append(o)
        if o.is_dma:
            self.dma_keys[dma_key] = self.dma_keys.get(dma_key, 0) + 16 * n_dma
            o.val = self.dma_keys[dma_key]
        return o

    def schedule(self, window=24):
        for e in ENGS:
            for o in self.ops[e]:
                o.done = False
                o.fin = 0.0
        use_bl = os.environ.get("K_BL", "1") == "1"
        if use_bl:
            allops = sorted([o for e in ENGS for o in self.ops[e]], key=lambda o: o.gidx)
            succ = {id(o): [] for o in allops}
            for o in allops:
                for d in o.deps:
                    succ[id(d)].append(o)
                for d in o.odeps:
                    succ[id(d)].append(o)
            bl = {}
            for o in reversed(allops):
                m = 0.0
                for s_ in succ[id(o)]:
                    v = bl[id(s_)]
                    if v > m:
                        m = v
                bl[id(o)] = m + o.cost + (2.0 if o.is_dma else 0.15)
            self.bl = bl
        pending = {e: list(self.ops[e]) for e in ENGS}
        free = {e: 0.0 for e in ENGS}
        new = {e: [] for e in ENGS}
        dma_lat = 2.0
        remaining = sum(len(v) for v in pending.values())
        eps = float(os.environ.get('K_EPS', '0.0'))
        while remaining:
            best = None
            cands = []
            for e in ENGS:
                pl = pending[e]
                if not pl:
                    continue
                lim = min(window, len(pl))
                if e in (SP, POOL):
                    lim = min(int(os.environ.get('K_SPW', '8')), len(pl)) if e == SP else min(int(os.environ.get('K_PLW', '24')), len(pl))
                for i in range(lim):
                    o = pl[i]
                    ok = True
                    st = free[e]
                    for d in o.deps:
                        if not d.done:
                            ok = False
                            break
                        if d.fin > st:
                            st = d.fin
                    if not ok:
                        continue
                    for d in o.odeps:
                        if not d.done:
                            ok = False
                            break
                    if not ok:
                        continue
                    if o.is_dma:
                        blocked = False
                        for j in range(i):
                            if pl[j].is_dma and pl[j].dma_key == o.dma_key:
                                blocked = True
                                break
                        if blocked:
                            continue
                    key = (st, -self.bl[id(o)] if use_bl else o.gidx, o.gidx) if use_bl else (st, o.gidx)
                    if use_bl:
                        cands.append((st, -self.bl[id(o)], o.gidx, e, i, o))
                    if best is None or key < best[0]:
                        best = (key, e, i, o, st)
                    if (not use_bl) and i == 0 and st <= free[e]:
                        break
            assert best is not None, "scheduler deadlock"
            _, e, i, o, st = best
            if use_bl and eps > 0:
                lim_st = st + eps
                c2 = min((c for c in cands if c[0] <= lim_st), key=lambda c: (c[1], c[0], c[2]))
                st, _, _, e, i, o = c2
            pending[e].pop(i)
            o.done = True
            o.st = st
            if o.is_dma:
                free[e] = st + 0.1
                o.fin = st + dma_lat + o.cost
            else:
                free[e] = st + o.cost
                o.fin = st + o.cost + float(os.environ.get('K_LAT', '0.05'))
            new[e].append(o)
            remaining -= 1
        self.ops = new
        self.est = max(free.values())

    def emit(self, stack):
        nc = self.nc
        if os.environ.get("K_SCHED", "1") != "0":
            self.schedule()
        engsems = {e: [stack.enter_context(nc.semaphore(f"s_{e}_{p}")) for p in range(self.n_phase)]
                   for e in ENGS}
        dmasems = {k: stack.enter_context(nc.semaphore(f"d_{k}")) for k in self.dma_keys}
        cum = {}
        for e in ENGS:
            cnt = 0
            for pos, o in enumerate(self.ops[e]):
                o.pos = pos
                if o.is_dma:
                    o.sem = dmasems[o.dma_key]
                    cum[o.dma_key] = cum.get(o.dma_key, 0) + 16 * o.n_dma
                    o.val = cum[o.dma_key]
                elif o.waited:
                    ph = cnt // SEM_LIM
                    assert ph < self.n_phase, f"too many waited ops on {e}"
                    o.sem = engsems[e][ph]
                    o.val = cnt % SEM_LIM + 1
                    cnt += 1
        assert cum == self.dma_keys
        for e in ENGS:
            for o in self.ops[e]:
                best = {}
                keep = []
                for d in o.deps:
                    if d.is_dma:
                        keep.append(d)
                    else:
                        b = best.get(d.eng)
                        if b is None or b.pos < d.pos:
                            best[d.eng] = d
                o.deps = keep + list(best.values())
        final = dict(self.dma_keys)
        self.stats = {}

        def run(eng, h):
            seen = {}
            nwait = 0
            for o in self.ops[eng]:
                for d in o.deps:
                    sid = id(d.sem)
                    if seen.get(sid, 0) < d.val:
                        h.wait_ge(d.sem, d.val)
                        seen[sid] = d.val
                        nwait += 1
                res = o.fn(h)
                if o.is_dma:
                    if not isinstance(res, (list, tuple)):
                        res = [res]
                    assert len(res) == o.n_dma, (len(res), o.n_dma)
                    for ins in res:
                        ins.then_inc(o.sem, 16)
                elif o.waited:
                    res.then_inc(o.sem, 1)
            if eng == SP:
                for k, v in final.items():
                    h.wait_ge(dmasems[k], v)
            self.stats[eng] = (len(self.ops[eng]), nwait)

        with nc.Block() as block:
            @block.tensor
            def _(h):
                run(PE, h)

            @block.scalar
            def _(h):
                run(ACT, h)

            @block.vector
            def _(h):
                run(DVE, h)

            @block.gpsimd
            def _(h):
                run(POOL, h)

            @block.sync
            def _(h):
                run(SP, h)


class Arena:
    def __init__(self, nc, stack, name, ncols):
        self.t = stack.enter_context(nc.sbuf_tensor(name, [128, ncols], BF16))
        self.ncols = ncols
        self.views = []

    def alloc(self, c0, n, ntok=1, name=""):
        assert c0 + n <= self.ncols, (name, c0, n, self.ncols)
        pre = []
        keep = []
        for (a, b, toks) in self.views:
            if a < c0 + n and c0 < b:
                pre.extend(toks)
                if a >= c0 and b <= c0 + n:
                    continue
            keep.append((a, b, toks))
        toks = [Tok(f"{name}{i}", pre=list(pre)) for i in range(ntok)]
        keep.append((c0, c0 + n, toks))
        self.views = keep
        return self.t[:, c0:c0 + n], toks


D = 1024
S = 2048
CTX = 256
T = S + CTX
DFF = 2816
NFF = 22
EPS = 1e-6
TILES = [(0, 256, True), (256, 768, False), (768, 1280, False), (1280, 1792, False), (1792, 2304, False)]
C_C64, C_S64, C_CR, C_SR = 0, 2048, 4096, 6144
C_P64, C_PR, C_ID, C_MA, C_MB = 8192, 8320, 8448, 8576, 8704
NCONST = 8832


def make_consts():
    theta = 10000.0
    n_rows = S // 64
    rows = np.repeat(np.arange(n_rows), 64).astype(np.float32)
    cols = np.tile(np.arange(64), n_rows).astype(np.float32)

    def tab(dim):
        q = dim // 4
        inv = (theta ** (-np.arange(q, dtype=np.float32) / q)).astype(np.float32)
        ang = np.concatenate([rows[:, None] * inv, cols[:, None] * inv], axis=-1).astype(np.float32)
        return np.cos(ang).astype(np.float32), np.sin(ang).astype(np.float32)

    c = np.zeros((128, NCONST), np.float32)
    ch, sh = tab(64)
    cr, sr = tab(32)
    for p in range(128):
        i = (p % 64) % 32
        c[p, C_C64:C_C64 + S] = ch[:, i]
        c[p, C_S64:C_S64 + S] = sh[:, i]
    for p in range(64, 96):
        i = (p - 64) % 16
        c[p, C_CR:C_CR + S] = cr[:, i]
        c[p, C_SR:C_SR + S] = sr[:, i]
    for d in range(128):
        dd = d % 64
        base = d - dd
        if dd < 32:
            c[base + dd + 32, C_P64 + d] = -1.0
        else:
            c[base + dd - 32, C_P64 + d] = 1.0
    for d in range(64, 96):
        dd = d - 64
        if dd < 16:
            c[64 + dd + 16, C_PR + d] = -1.0
        else:
            c[64 + dd - 16, C_PR + d] = 1.0
    c[:, C_ID:C_ID + 128] = np.eye(128, dtype=np.float32)
    jj = np.arange(128)[:, None]
    ii = np.arange(128)[None, :]
    c[:, C_MA:C_MA + 128] = (ii <= jj).astype(np.float32)
    c[:, C_MB:C_MB + 128] = (jj <= ii).astype(np.float32)
    return c


def build_nc(nseq=2, stage=0):
    nc = bass.Bass("TRN2", target_bir_lowering=False)
    dt = lambda name, shape: nc.dram_tensor(name, shape, F32, kind="ExternalInput").ap()
    x_d = dt("x", [2, S, D]); c_d = dt("c", [2, D]); ctx_d = dt("ctx", [2, CTX, D]); cctx_d = dt("c_ctx", [D])
    modw_d = dt("mod_w", [2, D, 6 * D]); modb_d = dt("mod_b", [2, 6 * D])
    n1g_d = dt("norm1_g", [2, D]); n2g_d = dt("norm2_g", [2, D])
    evwin_d = dt("ev_w_in", [1, D, 1184]); evqag_d = dt("ev_qa_g", [1, 64]); evkag_d = dt("ev_ka_g", [1, 64])
    evqlg_d = dt("ev_qlat_g", [1, 256]); evwqup_d = dt("ev_w_q_up", [1, 256, 768]); evkvg_d = dt("ev_kvlat_g", [1, 128])
    evwkv_d = dt("ev_w_kv_up", [1, 128, 1024]); evwout_d = dt("ev_w_out", [1, 1024, D])
    odwin_d = dt("od_w_in", [1, D, 1792]); odlng_d = dt("od_ln_g", [1, 512]); odlnb_d = dt("od_ln_b", [1, 512])
    odsw_d = dt("od_sgu_w", [1, 8, 128, 128]); odsb_d = dt("od_sgu_b", [1, 8, 128]); odsink_d = dt("od_sink", [1, 8])
    odwout_d = dt("od_w_out", [1, 1024, D])
    fup_d = dt("ffn_up", [2, D, 2 * DFF]); fcw_d = dt("ffn_conv_w", [2, 3, 2 * DFF]); fcb_d = dt("ffn_conv_b", [2, 2 * DFF])
    fdn_d = dt("ffn_down", [2, DFF, D]); fing_d = dt("final_g", [D])
    consts_d = dt("consts", [128, NCONST])
    out_d = nc.dram_tensor("out", [2, S, D], F32, kind="ExternalOutput").ap()

    fw = FW(nc)
    st = ExitStack()
    with st:
        sbt = lambda name, shape, d=F32: st.enter_context(nc.sbuf_tensor(name, shape, d))
        resid = sbt("resid", [128, 8, T])
        zbuf = sbt("zbuf", [128, 8, T], BF16)
        WA = Arena(nc, st, "warena", 9216)
        AR = Arena(nc, st, "arena", 21248)
        SC = Arena(nc, st, "scr", 8192)
        sq = sbt("sq", [128, 2, 512], BF16)
        rstd = sbt("rstd", [128, 2, 512])
        tmp32 = sbt("tmp32", [128, 3, 512])
        rd = sbt("rd", [128, 512])
        cst = sbt("cst", [128, 640])
        msk = sbt("msk", [128, 256], BF16)
        onesm = sbt("onesm", [128, 5, 128], BF16)
        epst = sbt("epst", [128, 1])
        modT = sbt("modT", [128, 2, 48, 3])
        scv = sbt("scv", [128, 2, 2, 8, 3])
        csb = sbt("csb", [128, 8, 3])
        gsm = sbt("gsm", [128, 5, 8])
        gq = sbt("gq", [128, 4])
        gql = sbt("gql", [128, 2])
        cvw = sbt("cvw", [128, 2, 4, 44])
        esink = sbt("esink", [128, 8])
        ps = st.enter_context(nc.psum_tensor("ps", [128, 8, 512], F32))
        PB = [Tok(f"pb{i}") for i in range(8)]

        class Rot:
            def __init__(self, idx):
                self.idx = list(idx); self.i = 0
            def next(self):
                v = self.idx[self.i % len(self.idx)]; self.i += 1
                return v
        PAIR_EXP = os.environ.get('K_PAIR', '0') == '1'
        SB_ = Rot([0, 1, 2, 3] if PAIR_EXP else [0, 1, 2]); OB_ = Rot([4, 5] if PAIR_EXP else [3, 4]); GB_ = Rot([6, 7] if PAIR_EXP else [5, 6, 7])
        t_sq = [Tok("sq0"), Tok("sq1")]; t_rstd = [Tok("rstd0"), Tok("rstd1")]
        t_tmp = [Tok("tmp0"), Tok("tmp1"), Tok("tmp2")]; t_rd = Tok("rd")
        sq_i = Rot([0, 1]); rstd_i = Rot([0, 1]); tmp_i = Rot([0, 1, 2])
        t_cst = Tok("cst"); t_small = Tok("small"); t_mod = Tok("mod")
        t_res = [[Tok(f"res{k}_{t}") for t in range(5)] for k in range(8)]
        t_z = [Tok(f"z{t}") for t in range(5)]
        dk = [0]

        def dkey(p="k"):
            dk[0] += 1
            return f"{p}{dk[0] % 8}"

        def fsz(ap):
            n = 1
            for d_ in ap.shape[1:]:
                n *= d_
            return n

        def mmcost(r):
            return fsz(r) / 2400.0 * (4 if r.dtype == F32 else 1) + 0.005

        def dma(eng, out, in_, reads=(), writes=(), key=None, **kw):
            c = fsz(out) * 128 * 4 / 2.0e5
            fw.op(eng, lambda h: h.dma_start(out=out, in_=in_, **kw), reads=reads, writes=writes, dma_key=key or dkey('w' if eng == POOL else 'k'), cost=c)

        def mm(out, pairs, reads, writes):
            def fn(h):
                n = len(pairs); ins = None
                for i, (l, r) in enumerate(pairs):
                    ins = h.matmul(out, lhsT=l, rhs=r, start=(i == 0), stop=(i == n - 1))
                return ins
            fw.op(PE, fn, reads, writes, cost=sum(mmcost(r) for _, r in pairs))

        def ecost(eng, out):
            n = fsz(out)
            if eng == ACT:
                return n / 1200.0 + 0.22
            if eng == DVE:
                return n / 800.0 + 0.1
            return n / 500.0 + 0.2

        def act(out, in_, func, reads, writes, **kw):
            fw.op(ACT, lambda h: h.activation(out=out, in_=in_, func=func, **kw), reads, writes, cost=ecost(ACT, out))

        def stt(eng, out, in0, scalar, in1, op0, op1, reads, writes):
            fw.op(eng, lambda h: h.scalar_tensor_tensor(out=out, in0=in0, scalar=scalar, in1=in1, op0=op0, op1=op1), reads, writes, cost=ecost(eng, out))

        def tt(eng, out, in0, in1, op, reads, writes):
            fw.op(eng, lambda h: h.tensor_tensor(out=out, in0=in0, in1=in1, op=op), reads, writes, cost=ecost(eng, out))

        def cp(eng, out, in_, reads, writes):
            if eng == ACT:
                fw.op(ACT, lambda h: h.copy(out=out, in_=in_), reads, writes, cost=ecost(ACT, out))
            else:
                fw.op(eng, lambda h: h.tensor_copy(out=out, in_=in_), reads, writes, cost=ecost(eng, out))

        dma(SP, cst[:, 0:384], consts_d[:, C_P64:C_P64 + 384], writes=[t_cst])
        dma(POOL, msk[:], consts_d[:, C_MA:C_MA + 256], writes=[t_cst])
        P64 = cst[:, 0:128]; PR = cst[:, 128:256]; IDN = cst[:, 256:384]

        def init_small(h):
            h.memset(onesm[:, 0, :], 1.0 / 1024)
            h.memset(onesm[0:64, 1, 64:128], 0.0)
            h.memset(onesm[64:128, 1, 0:64], 0.0)
            h.memset(onesm[0:64, 1, 0:64], 1.0 / 64)
            h.memset(onesm[64:128, 1, 64:128], 1.0 / 64)
            h.memset(onesm[:, 2, :], 1.0 / 256)
            h.memset(onesm[:, 3, :], 1.0 / 128)
            h.memset(onesm[:, 4, :], 1.0)
            return h.memset(epst[:], EPS)
        fw.op(DVE, init_small, writes=[t_small])
        fm = lambda v: v.rearrange("(k p) -> p k", p=128)
        SKIP = os.environ.get('K_SKIP', '')
        for i, src in enumerate([n1g_d[0], n1g_d[1], n2g_d[0], n2g_d[1], fing_d]):
            dma(SP, gsm[:, i, :], fm(src), writes=[t_small], allow_slow_non_contiguous=True)
        for half in (range(2) if 'a' not in SKIP else []):
            dma(SP, gq[64 * half:64 * half + 64, 0:1], evqag_d[0].rearrange("(p o) -> p o", o=1), writes=[t_small], allow_slow_non_contiguous=True)
            dma(SP, gq[64 * half:64 * half + 64, 1:2], evkag_d[0].rearrange("(p o) -> p o", o=1), writes=[t_small], allow_slow_non_contiguous=True)
        dma(SP, gq[:, 2:3], evkvg_d[0].rearrange("(p o) -> p o", o=1), writes=[t_small], allow_slow_non_contiguous=True)
        dma(SP, gql[:], fm(evqlg_d[0]), writes=[t_small], allow_slow_non_contiguous=True)
        for l in (range(2) if 'b' not in SKIP else []):
            for i in range(3):
                dma(SP, cvw[:, l, i, :], fcw_d[l, i].rearrange("(k p) -> p k", p=128), writes=[t_small], allow_slow_non_contiguous=True)
            dma(SP, cvw[:, l, 3, :], fcb_d[l].rearrange("(k p) -> p k", p=128), writes=[t_small], allow_slow_non_contiguous=True)
        dma(SP, esink[:], odsink_d[0].partition_broadcast(128), writes=[t_small])
        act(esink[:], esink[:], AF.Exp, [t_small], [t_small])
        for b in (range(2) if 'c' not in SKIP else []):
            dma(SP, csb[:, :, b], fm(c_d[b]), writes=[t_mod], allow_slow_non_contiguous=True)
        dma(SP, csb[:, :, 2], fm(cctx_d), writes=[t_mod], allow_slow_non_contiguous=True)
        act(csb[:], csb[:], AF.Silu, [t_mod], [t_mod])

        def load_seq(s):
            stg2_ap, stg2_t = SC.alloc(0, 4096, 2, "instg")
            for i in (range(18) if 'e' not in SKIP else []):
                src = ctx_d[s, i * 128:(i + 1) * 128, :] if i < 2 else x_d[s, (i - 2) * 128:(i - 1) * 128, :]
                bi = i % 2
                sbuf = stg2_ap[:, bi * 2048:(bi + 1) * 2048].bitcast(F32)
                dma(SP, sbuf, src, writes=[stg2_t[bi]])
                col = i * 128
                ti = 0 if i < 2 else 1 + (i - 2) // 4
                for hk in range(2):
                    g = GB_.next()

                    def trf(h, g=g, hk=hk, sbuf=sbuf):
                        ins = None
                        for kk in range(4):
                            k = hk * 4 + kk
                            ins = h.transpose(ps[:, g, kk * 128:(kk + 1) * 128], sbuf[:, k * 128:(k + 1) * 128], IDN)
                        return ins
                    fw.op(PE, trf, [stg2_t[bi], t_cst], [PB[g]], cost=1.2)
                    cp(ACT if hk == 0 else DVE, resid[:, hk * 4:hk * 4 + 4, col:col + 128], ps[:, g, :].rearrange("p (a b) -> p a b", a=4),
                       [PB[g]], [t_res[k][ti] for k in range(hk * 4, hk * 4 + 4)])


        load_seq(0)

        stg_ap, stg_t = AR.alloc(0, 16384, 2, "modstg")
        modv_ap, modv_t = AR.alloc(16384, 4864, 1, "modv")
        AR.views = []
        stg_ap, stg_t = AR.alloc(0, 8192, 1, "modstg")
        modv_ap, modv_t = AR.alloc(8192, 12288, 1, "modv")
        stg = stg_ap.bitcast(F32).rearrange("p (k n) -> p k n", k=8)
        stgb_ap, stgb_t = WA.alloc(0, 8192, 1, "modstg2")
        stgb = stgb_ap.bitcast(F32).rearrange("p (k n) -> p k n", k=8)
        stgs = [(stg, stg_t), (stgb, stgb_t)]
        modv = modv_ap.bitcast(F32)
        modb_ap, modb_t = SC.alloc(0, 8192, 1, "modb")
        for l in (range(2) if 'd' not in SKIP else []):
            for nt in range(12):
                stg_c, stg_ct = stgs[nt % 2]
                dma(SP, stg_c, modw_d[l].rearrange("(k p) n -> p k n", p=128)[:, :, nt * 512:(nt + 1) * 512], writes=stg_ct)
                g = GB_.next()
                mm(ps[0:3, g, :], [(csb[:, k, :], stg_c[:, k, :]) for k in range(8)], [t_mod] + stg_ct, [PB[g]])
                cp(DVE, modv[0:3, nt * 512:(nt + 1) * 512], ps[0:3, g, :], [PB[g]], modv_t)
            g = GB_.next()

            def tr(h, g=g):
                ins = None
                for j in range(48):
                    ins = h.transpose(ps[:, g, j * 3:j * 3 + 3], modv[0:3, j * 128:(j + 1) * 128], IDN[0:3, 0:3])
                return ins
            fw.op(PE, tr, modv_t + [t_cst], [PB[g]])
            cp(DVE, modT[:, l, :, :], ps[:, g, 0:144].rearrange("p (j v) -> p j v", v=3), [PB[g]], [t_mod])
            mbT = modb_ap.bitcast(F32)[:, 0:48]
            dma(SP, mbT, fm(modb_d[l]), writes=modb_t, allow_slow_non_contiguous=True)
            for v in range(3):
                tt(DVE, modT[:, l, :, v], modT[:, l, :, v], mbT, ALU.add, [t_mod] + modb_t, [t_mod])
            for n in range(2):
                for v in range(3):
                    stt(DVE, scv[:, l, n, :, v], modT[:, l, (8 + 24 * n):(16 + 24 * n), v], 1.0, gsm[:, 2 * n + l, :], ALU.add, ALU.mult,
                        [t_mod, t_small], [t_mod])

        def norm_mod(l, n, tiles, vb, gain_only=None, out_fn=None):
            for ti in tiles:
                c0, c1, isc = TILES[ti]
                N = c1 - c0
                v = 2 if isc else vb
                g = GB_.next()
                for k in range(8):
                    si = sq_i.next()
                    act(sq[:, si, 0:N], resid[:, k, c0:c1], AF.Square, [t_res[k][ti]], [t_sq[si]])
                    fw.op(PE, (lambda h, g=g, si=si, k=k, N=N: h.matmul(ps[:, g, 0:N], lhsT=onesm[:, 0, :], rhs=sq[:, si, 0:N], start=(k == 0), stop=(k == 7))),
                          [t_sq[si], t_small], [PB[g]], cost=N / 2400.0 + 0.005)
                ri = rstd_i.next()
                act(rstd[:, ri, 0:N], ps[:, g, 0:N], AF.Ln, [PB[g], t_small], [t_rstd[ri]], bias=epst[:, 0:1])
                act(rstd[:, ri, 0:N], rstd[:, ri, 0:N], AF.Exp, [t_rstd[ri]], [t_rstd[ri]], scale=-0.5)
                for k in range(8):
                    if out_fn is not None:
                        out_fn(ti, k, c0, c1, N, ri)
                        continue
                    mi = tmp_i.next()
                    stt(DVE, tmp32[:, mi, 0:N], resid[:, k, c0:c1], scv[:, l, n, k, v:v + 1], rstd[:, ri, 0:N], ALU.mult, ALU.mult,
                        [t_res[k][ti], t_mod, t_rstd[ri]], [t_tmp[mi]])
                    act(zbuf[:, k, c0:c1], tmp32[:, mi, 0:N], AF.Identity, [t_tmp[mi], t_mod], [t_z[ti]],
                        bias=modT[:, l, 24 * n + k, v:v + 1])

        def load_w(dst3, src2, nk, ncol, toks):
            key = dkey("w")
            fw.op(POOL, lambda h: [h.dma_start(out=dst3[:, k, :], in_=src2[k * 128:(k + 1) * 128, :]) for k in range(nk)],
                  writes=toks, dma_key=key, n_dma=nk)

        def gated_add(pb, mo, c0, c1, ti_list, l, gidx, v):
            N = c1 - c0
            stt(DVE, resid[:, mo, c0:c1], ps[:, pb, 0:N], modT[:, l, gidx + mo, v:v + 1], resid[:, mo, c0:c1], ALU.mult, ALU.add,
                [PB[pb], t_mod] + [t_res[mo][ti] for ti in ti_list], [t_res[mo][ti] for ti in ti_list])

        def rope_out(src32, mi, dst, N, tcol0, Cap, Sap, t_tab, prot, prange, rd_toks, wr_toks):
            p0, p1 = prange
            g = GB_.next()
            fw.op(PE, lambda h: h.matmul(ps[p0:p1, g, 0:N], lhsT=prot[p0:p1, p0:p1], rhs=src32[p0:p1, 0:N], start=True, stop=True),
                  [t_tmp[mi], t_cst], [PB[g]], cost=4 * N / 2400.0 + 0.005)
            m2 = tmp_i.next()
            tt(DVE, tmp32[p0:p1, m2, 0:N], ps[p0:p1, g, 0:N], Sap[p0:p1, tcol0:tcol0 + N], ALU.mult, [PB[g]] + t_tab, [t_tmp[m2]])
            tt(DVE, src32[p0:p1, 0:N], src32[p0:p1, 0:N], Cap[p0:p1, tcol0:tcol0 + N], ALU.mult, [t_tmp[mi]] + t_tab, [t_tmp[mi]])
            tt(POOL, dst, src32[p0:p1, 0:N], tmp32[p0:p1, m2, 0:N], ALU.add, [t_tmp[mi], t_tmp[m2]] + rd_toks, wr_toks)

        def attention(qtiles, nheads, K, q_ap, k_ap, v_ap, scale, den_mode, out_w, out_w_toks, nchunk_o, l, vb,
                      q_toks, k_toks, v_toks, window=False, sink=False, ot_view=None, pt_view=None, k_extra=()):
            OT, ot_t = ot_view
            PT, pt_t = pt_view
            pt_i = Rot(list(range(len(pt_t))))
            pt2_i = Rot([0, 2, 4])
            SP2 = Rot([0, 2])
            for ti in qtiles:
                c0, c1, isc = TILES[ti]
                N = c1 - c0
                v = 2 if isc else vb
                oi = (ti % 2)
                for h in range(nheads):
                    e = h % 2
                    ob = OB_.next()
                    if window:
                        tq = ti - 1
                        kcs = [(0, 0, N, None), (1, 0, N, None)]
                        for c in range(4 * tq - 1, 4 * tq + 5):
                            if c < 0 or c > 15:
                                continue
                            b0 = max(4 * tq, c - 1); b1 = min(4 * tq + 3, c + 1)
                            kcs.append((2 + c, (b0 - 4 * tq) * 128, (b1 - 4 * tq + 1) * 128, c))
                    else:
                        kcs = [(kc, 0, N, None) for kc in (range(2) if isc else range(18))]
                    nk = len(kcs)

                    def emit_s(idx, h=h, ti=ti, c0=c0, N=N):
                        kc, a0, a1, cblk = kcs[idx]
                        sb = SB_.next()
                        kt = 0 if kc < 2 else 1 + (kc - 2) // 4
                        fw.op(PE, (lambda h_, sb=sb, kc=kc, a0=a0, a1=a1, h=h, c0=c0: h_.matmul(ps[:, sb, a0:a1], lhsT=k_ap(h, kc), rhs=q_ap(h, c0 + a0, c0 + a1), start=True, stop=True)),
                              [q_toks[ti], k_toks[kt]] + list(k_extra), [PB[sb]], cost=(a1 - a0) / 2400.0 + 0.005)
                        pi = pt_i.next()
                        act(PT[:, pi, a0:a1], ps[:, sb, a0:a1], AF.Exp, [PB[sb]], [pt_t[pi]], scale=scale)
                        if cblk is not None:
                            tq = ti - 1
                            for qb in range(a0 // 128, a1 // 128):
                                qblk = 4 * tq + qb
                                if qblk == cblk:
                                    continue
                                mcol = 0 if qblk > cblk else 128
                                tt(POOL, PT[:, pi, qb * 128:(qb + 1) * 128], PT[:, pi, qb * 128:(qb + 1) * 128], msk[:, mcol:mcol + 128], ALU.mult,
                                   [pt_t[pi], t_cst], [pt_t[pi]])
                        return pi

                    def emit_s2(idx, h=h, ti=ti, c0=c0, N=N):
                        sb = SP2.next()
                        pi = pt2_i.next()
                        for j_ in range(2):
                            kc = kcs[idx + j_][0]
                            kt = 0 if kc < 2 else 1 + (kc - 2) // 4
                            fw.op(PE, (lambda h_, sb=sb, j_=j_, kc=kc, h=h, c0=c0, N=N: h_.matmul(ps[:, sb + j_, 0:N], lhsT=k_ap(h, kc), rhs=q_ap(h, c0, c0 + N), start=True, stop=True)),
                                  [q_toks[ti], k_toks[kt]] + list(k_extra), [PB[sb + j_]], cost=N / 2400.0 + 0.005)
                        act(PT[:, pi:pi + 2, 0:N], ps[:, sb:sb + 2, 0:N], AF.Exp, [PB[sb], PB[sb + 1]], [pt_t[pi], pt_t[pi + 1]], scale=scale)
                        return pi

                    def emit_pv(idx, pi, h=h, ob=ob, nk=nk):
                        kc, a0, a1, cblk = kcs[idx]
                        kt = 0 if kc < 2 else 1 + (kc - 2) // 4

                        def pv(h_, ob=ob, kc=kc, pi=pi, a0=a0, a1=a1, idx=idx, h=h, nk=nk):
                            if den_mode == "aug":
                                return h_.matmul(ps[:, ob, a0:a1], lhsT=v_ap(h, kc), rhs=PT[:, pi, a0:a1], start=(idx == 0), stop=(idx == nk - 1))
                            h_.matmul(ps[0:64, ob, a0:a1], lhsT=v_ap(h, kc), rhs=PT[:, pi, a0:a1], start=(idx == 0), stop=(idx == nk - 1))
                            return h_.matmul(ps[64:128, ob, a0:a1], lhsT=onesm[:, 4, 0:64], rhs=PT[:, pi, a0:a1], start=(idx == 0), stop=(idx == nk - 1))
                        fw.op(PE, pv, [pt_t[pi], v_toks[kt], t_small], [PB[ob]], cost=((a1 - a0) / 2400.0 + 0.005) * (1 if den_mode == 'aug' else 2))
                    if not window and PAIR_EXP:
                        assert nk % 2 == 0
                        np_ = nk // 2
                        pis = {0: emit_s2(0)}
                        for ip in range(np_):
                            if ip + 1 < np_:
                                pis[ip + 1] = emit_s2(2 * (ip + 1))
                            emit_pv(2 * ip, pis[ip])
                            emit_pv(2 * ip + 1, pis[ip] + 1)
                    else:
                        LA = 2
                        pis = {}
                        for idx in range(min(LA, nk)):
                            pis[idx] = emit_s(idx)
                        for idx in range(nk):
                            if idx + LA < nk:
                                pis[idx + LA] = emit_s(idx + LA)
                            emit_pv(idx, pis[idx])
                    if sink:
                        fw.op(DVE, lambda h_, ob=ob, h=h, N=N: h_.tensor_scalar(out=rd[64:128, 0:N], in0=ps[64:128, ob, 0:N], scalar1=esink[64:128, h:h + 1], scalar2=None, op0=ALU.add),
                              [PB[ob], t_small], [t_rd])
                        fw.op(DVE, lambda h_, N=N: h_.reciprocal(out=rd[0:64, 0:N], in_=rd[64:128, 0:N]), [t_rd], [t_rd])
                    else:
                        fw.op(DVE, lambda h_, ob=ob, N=N: h_.reciprocal(out=rd[0:64, 0:N], in_=ps[64:128, ob, 0:N]), [PB[ob]], [t_rd])
                    if e == 0:
                        tt(DVE, OT[0:64, oi, h // 2, 0:N], ps[0:64, ob, 0:N], rd[0:64, 0:N], ALU.mult, [PB[ob], t_rd], [ot_t[oi]])
                    else:
                        mi = tmp_i.next()
                        tt(DVE, tmp32[0:64, mi, 0:N], ps[0:64, ob, 0:N], rd[0:64, 0:N], ALU.mult, [PB[ob], t_rd], [t_tmp[mi]])
                        cp(ACT, OT[64:128, oi, h // 2, 0:N], tmp32[0:64, mi, 0:N], [t_tmp[mi]], [ot_t[oi]])
                for mo in range(8):
                    g = GB_.next()
                    mm(ps[:, g, 0:N], [(out_w(m)[:, mo * 128:(mo + 1) * 128], OT[:, oi, m, 0:N]) for m in range(nchunk_o)],
                       [ot_t[oi]] + out_w_toks, [PB[g]])
                    gated_add(g, mo, c0, c1, [ti], l, 16, v)

        for t_ in (t_small, t_cst, t_mod):
            t_.frozen = True
        for s in range(nseq):
            vb = s
            if s > 0:
                load_seq(s)
            if stage != 1:
                l = 0
                norm_mod(l, 0, range(5), vb)
                wA_ap, wA_t = WA.alloc(0, 6144, 1, "wA")
                wA = wA_ap.rearrange("p (k n) -> p k n", k=8)
                load_w(wA, evwin_d[0][:, 0:768], 8, 768, wA_t)
                tab_ap, tab_t = SC.alloc(0, 4096, 1, "tabA")
                dma(POOL, tab_ap[:, 0:2048], consts_d[:, C_C64:C_C64 + 2048], writes=tab_t)
                dma(POOL, tab_ap[:, 2048:4096], consts_d[:, C_S64:C_S64 + 2048], writes=tab_t)
                Ct = tab_ap[:, 0:2048]; St = tab_ap[:, 2048:4096]
                QA_ap, QA_t = AR.alloc(0, 9216, 5, "QA"); QA = QA_ap.rearrange("p (m t) -> p m t", m=4)
                KA_ap, KA_t = AR.alloc(9216, 4608, 5, "KA"); KA = KA_ap.rearrange("p (j t) -> p j t", j=2)
                VA_ap, VA_t = AR.alloc(13824, 4608, 5, "VA"); VA = VA_ap.rearrange("p (c j d) -> p c j d", c=18, j=2)
                fw.op(POOL, lambda h: h.memset(VA_ap, 1.0), writes=VA_t)

                def qk_chunk(ti, pairs_fn, gcol, dst, dst_toks, rope, ncost=8):
                    c0, c1, isc = TILES[ti]; N = c1 - c0
                    g = GB_.next()
                    fw.op(PE, lambda h: pairs_fn(h, g, c0, c1, N), [t_z[ti]] + wA_t, [PB[g]], cost=ncost * (N / 2400.0 + 0.005))
                    mi = tmp_i.next()
                    if gcol is not None:
                        si = sq_i.next()
                        act(sq[:, si, 0:N], ps[:, g, 0:N], AF.Square, [PB[g]], [t_sq[si]])
                        g2 = GB_.next()
                        mm(ps[:, g2, 0:N], [(onesm[:, 1, :], sq[:, si, 0:N])], [t_sq[si], t_small], [PB[g2]])
                        ri = rstd_i.next()
                        act(rstd[:, ri, 0:N], ps[:, g2, 0:N], AF.Ln, [PB[g2], t_small], [t_rstd[ri]], bias=epst[:, 0:1])
                        act(rstd[:, ri, 0:N], rstd[:, ri, 0:N], AF.Exp, [t_rstd[ri]], [t_rstd[ri]], scale=-0.5)
                        stt(DVE, tmp32[:, mi, 0:N], ps[:, g, 0:N], gq[:, gcol:gcol + 1], rstd[:, ri, 0:N], ALU.mult, ALU.mult,
                            [PB[g], t_small, t_rstd[ri]], [t_tmp[mi]])
                    else:
                        cp(ACT, tmp32[:, mi, 0:N], ps[:, g, 0:N], [PB[g]], [t_tmp[mi]])
                    if rope and not isc:
                        rope_out(tmp32[:, mi, :], mi, dst, N, c0 - 256, Ct, St, tab_t, P64, (0, 128), [], dst_toks)
                    else:
                        cp(DVE, dst, tmp32[:, mi, 0:N], [t_tmp[mi]], dst_toks)

                def proj_qkv(w3, wtoks, tiles, qnorm, Q, Q_t, Kd, K_t, V, V_t, qtiles):
                    for ti in tiles:
                        c0, c1, isc = TILES[ti]; N = c1 - c0
                        if ti in qtiles:
                            for m in range(4):
                                def pf(h, g, c0, c1, N, m=m):
                                    ins = None
                                    for k in range(8):
                                        ins = h.matmul(ps[:, g, 0:N], lhsT=w3[:, k, m * 128:(m + 1) * 128], rhs=zbuf[:, k, c0:c1], start=(k == 0), stop=(k == 7))
                                    return ins
                                qk_chunk(ti, pf, 0 if qnorm else None, Q[:, m, c0:c1], [Q_t[ti]], True)
                        for j in range(2):
                            def pf(h, g, c0, c1, N, j=j):
                                ins = None
                                for k in range(8):
                                    h.matmul(ps[0:64, g, 0:N], lhsT=w3[:, k, 512 + 64 * j:576 + 64 * j], rhs=zbuf[:, k, c0:c1], start=(k == 0), stop=(k == 7))
                                    ins = h.matmul(ps[64:128, g, 0:N], lhsT=w3[:, k, 512 + 64 * j:576 + 64 * j], rhs=zbuf[:, k, c0:c1], start=(k == 0), stop=(k == 7))
                                return ins
                            qk_chunk(ti, pf, 1 if qnorm else None, Kd[:, j, c0:c1], [K_t[ti]], True, ncost=16)
                        for sc_ in range(N // 128):
                            g = GB_.next()
                            cc0 = c0 + sc_ * 128
                            mm(ps[:, g, 0:128], [(zbuf[:, k, cc0:cc0 + 128], w3[:, k, 640:768]) for k in range(8)], [t_z[ti]] + wtoks, [PB[g]])
                            cp(ACT, V[:, cc0 // 128, :, 0:64], ps[:, g, 0:128].rearrange("p (j d) -> p j d", j=2), [PB[g]], [V_t[ti]])


                def pad_k(KA_ap, KA_t):
                    kz_ap, kz_t = WA.alloc(4608, 4608, 1, "KZ1")
                    fw.op(POOL, lambda h: h.tensor_copy(out=kz_ap[64:128, :], in_=KA_ap[64:128, :]), KA_t, kz_t, cost=10.0)
                    fw.op(DVE, lambda h: h.memset(kz_ap[0:64, :], 0.0), (), kz_t, cost=3.0)
                    fw.op(DVE, lambda h: h.memset(KA_ap[64:128, :], 0.0), (), KA_t, cost=3.0)
                    return kz_ap.rearrange("p (j t) -> p j t", j=2), kz_t
                proj_qkv(wA, wA_t, range(5), True, QA, QA_t, KA, KA_t, VA, VA_t, range(5))
                KZ, KZ_t = pad_k(KA_ap, KA_t)
                woA_ap, woA_t = WA.alloc(0, 4096, 1, "woA")
                woA = woA_ap.rearrange("p (k n) -> p k n", k=4)
                load_w(woA, evwout_d[0][0:512, :], 4, 1024, woA_t)
                OT_ap, OT_t = SC.alloc(0, 4096, 2, "OT"); OT = OT_ap.rearrange("p (o m n) -> p o m n", o=2, m=4)
                PT_ap, PT_t = SC.alloc(4096, 3072, 6, "PT"); PT = PT_ap.rearrange("p (i n) -> p i n", i=6)
                attention(range(5), 8, 64,
                          lambda h, a, b: QA[:, h // 2, a:b],
                          lambda h, kc: (KA if h % 2 == 0 else KZ)[:, h // 4, kc * 128:(kc + 1) * 128],
                          lambda h, kc: VA[:, kc, h // 4, :], 64 ** -0.5, "aug",
                          lambda m: woA[:, m, :], woA_t, 4, l, vb, QA_t, KA_t, VA_t, ot_view=(OT, OT_t), pt_view=(PT, PT_t), k_extra=KZ_t)

                if stage != 2:
                    wB_ap, wB_t = WA.alloc(4608, 3328, 1, "wB"); wB = wB_ap.rearrange("p (k n) -> p k n", k=8)
                    load_w(wB, evwin_d[0][:, 768:1184], 8, 416, wB_t)
                    wq_ap, wq_t = WA.alloc(0, 1536, 1, "wq"); wq = wq_ap.rearrange("p (k n) -> p k n", k=2)
                    load_w(wq, evwqup_d[0], 2, 768, wq_t)
                    wkv_ap, wkv_t = WA.alloc(1536, 1024, 1, "wkv"); wkv = wkv_ap.rearrange("p (k n) -> p k n", k=1)
                    load_w(wkv, evwkv_d[0], 1, 1024, wkv_t)
                    tab_ap, tab_t = SC.alloc(0, 4096, 1, "tabR")
                    dma(POOL, tab_ap[:, 0:2048], consts_d[:, C_CR:C_CR + 2048], writes=tab_t)
                    dma(POOL, tab_ap[:, 2048:4096], consts_d[:, C_SR:C_SR + 2048], writes=tab_t)
                    Cr = tab_ap[:, 0:2048]; Sr = tab_ap[:, 2048:4096]
                    cqn_ap, cqn_t = AR.alloc(0, 4608, 5, "cqn"); cqn = cqn_ap.rearrange("p (m t) -> p m t", m=2)
                    ckv_ap, ckv_t = AR.alloc(4608, 2304, 5, "ckv")
                    KBs = []
                    for bi_ in range(2):
                        kb_ap, kb_t = AR.alloc(6912 + 4608 * bi_, 4608, 5, f"KB{bi_}")
                        fw.op(POOL, lambda h, kb_ap=kb_ap: h.memset(kb_ap[96:128, :], 0.0), writes=kb_t, cost=3.0)
                        KBs.append((kb_ap.rearrange("p (e t) -> p e t", e=2), kb_t))
                    for ti in range(5):
                        c0, c1, isc = TILES[ti]; N = c1 - c0
                        gs = []
                        si_l = []
                        for m in range(2):
                            g = GB_.next(); gs.append(g)
                            mm(ps[:, g, 0:N], [(wB[:, k, m * 128:(m + 1) * 128], zbuf[:, k, c0:c1]) for k in range(8)], [t_z[ti]] + wB_t, [PB[g]])
                        g2 = GB_.next()
                        for m in range(2):
                            si = sq_i.next()
                            act(sq[:, si, 0:N], ps[:, gs[m], 0:N], AF.Square, [PB[gs[m]]], [t_sq[si]])
                            fw.op(PE, (lambda h, g2=g2, si=si, m=m, N=N: h.matmul(ps[:, g2, 0:N], lhsT=onesm[:, 2, :], rhs=sq[:, si, 0:N], start=(m == 0), stop=(m == 1))),
                                  [t_sq[si], t_small], [PB[g2]])
                        ri = rstd_i.next()
                        act(rstd[:, ri, 0:N], ps[:, g2, 0:N], AF.Ln, [PB[g2], t_small], [t_rstd[ri]], bias=epst[:, 0:1])
                        act(rstd[:, ri, 0:N], rstd[:, ri, 0:N], AF.Exp, [t_rstd[ri]], [t_rstd[ri]], scale=-0.5)
                        for m in range(2):
                            stt(DVE, cqn[:, m, c0:c1], ps[:, gs[m], 0:N], gql[:, m:m + 1], rstd[:, ri, 0:N], ALU.mult, ALU.mult,
                                [PB[gs[m]], t_small, t_rstd[ri]], [cqn_t[ti]])
                        g = GB_.next()
                        mm(ps[:, g, 0:N], [(wB[:, k, 256:384], zbuf[:, k, c0:c1]) for k in range(8)], [t_z[ti]] + wB_t, [PB[g]])
                        si = sq_i.next()
                        act(sq[:, si, 0:N], ps[:, g, 0:N], AF.Square, [PB[g]], [t_sq[si]])
                        g2 = GB_.next()
                        mm(ps[:, g2, 0:N], [(onesm[:, 3, :], sq[:, si, 0:N])], [t_sq[si], t_small], [PB[g2]])
                        ri = rstd_i.next()
                        act(rstd[:, ri, 0:N], ps[:, g2, 0:N], AF.Ln, [PB[g2], t_small], [t_rstd[ri]], bias=epst[:, 0:1])
                        act(rstd[:, ri, 0:N], rstd[:, ri, 0:N], AF.Exp, [t_rstd[ri]], [t_rstd[ri]], scale=-0.5)
                        stt(DVE, ckv_ap[:, c0:c1], ps[:, g, 0:N], gq[:, 2:3], rstd[:, ri, 0:N], ALU.mult, ALU.mult,
                            [PB[g], t_small, t_rstd[ri]], [ckv_t[ti]])
                        g = GB_.next()
                        mm(ps[64:96, g, 0:N], [(wB[:, k, 384:416], zbuf[:, k, c0:c1]) for k in range(8)], [t_z[ti]] + wB_t, [PB[g]])
                        kr0 = KBs[0][0][64:96, 0, c0:c1]
                        if isc:
                            cp(ACT, kr0, ps[64:96, g, 0:N], [PB[g]], [KBs[0][1][ti]])
                        else:
                            mi = tmp_i.next()
                            cp(ACT, tmp32[64:96, mi, 0:N], ps[64:96, g, 0:N], [PB[g]], [t_tmp[mi]])
                            rope_out(tmp32[:, mi, :], mi, kr0, N, c0 - 256, Cr, Sr, tab_t, PR, (64, 96), [], [KBs[0][1][ti]])
                        cp(POOL, KBs[0][0][64:96, 1, c0:c1], kr0, [KBs[0][1][ti]], [KBs[0][1][ti]])
                        for e_ in range(2):
                            cp(POOL, KBs[1][0][64:96, e_, c0:c1], kr0, [KBs[0][1][ti]], [KBs[1][1][ti]])
                    QB_ap, QB_t = AR.alloc(16128, 4608, 5, "QB"); QB = QB_ap.rearrange("p (e t) -> p e t", e=2)
                    VB_ap, VB_t = WA.alloc(4608, 4608, 5, "VBa"); VB = VB_ap.rearrange("p (c e d) -> p c e d", c=18, e=2)
                    fw.op(POOL, lambda h, VB_ap=VB_ap: h.memset(VB_ap, 1.0), writes=VB_t, cost=10.0)
                    fw.op(POOL, lambda h, QB_ap=QB_ap: h.memset(QB_ap[96:128, :], 0.0), writes=QB_t, cost=3.0)
                    woB_ap, woB_t = WA.alloc(2560, 2048, 2, "woB"); woB = woB_ap.rearrange("p (i n) -> p i n", i=2)
                    OT_ap, OT_t = SC.alloc(4096, 1024, 2, "OTb"); OTb = OT_ap.rearrange("p (o m n) -> p o m n", o=2, m=1)
                    PT_ap, PT_t = SC.alloc(5120, 3072, 6, "PTb"); PTb = PT_ap.rearrange("p (i n) -> p i n", i=6)
                    for sp in range(4):
                        wi = sp % 2
                        KB, KB_t = KBs[sp % 2]
                        fw.op(POOL, lambda h, sp=sp, wi=wi: [h.dma_start(out=woB[:, wi, :], in_=evwout_d[0][512 + 128 * sp:512 + 128 * (sp + 1), :])],
                              writes=[woB_t[wi]], dma_key=dkey("w"))
                        for ti in range(5):
                            c0, c1, isc = TILES[ti]; N = c1 - c0
                            for e in range(2):
                                hh = 2 * sp + e
                                g = GB_.next()
                                mm(ps[0:96, g, 0:N], [(wq[:, k, hh * 96:(hh + 1) * 96], cqn[:, k, c0:c1]) for k in range(2)], [cqn_t[ti]] + wq_t, [PB[g]])
                                cp(ACT, QB[0:64, e, c0:c1], ps[0:64, g, 0:N], [PB[g]], [QB_t[ti]])
                                if isc:
                                    cp(DVE, QB[64:96, e, c0:c1], ps[64:96, g, 0:N], [PB[g]], [QB_t[ti]])
                                else:
                                    mi = tmp_i.next()
                                    cp(DVE, tmp32[64:96, mi, 0:N], ps[64:96, g, 0:N], [PB[g]], [t_tmp[mi]])
                                    rope_out(tmp32[:, mi, :], mi, QB[64:96, e, c0:c1], N, c0 - 256, Cr, Sr, tab_t, PR, (64, 96), [], [QB_t[ti]])
                                g = GB_.next()
                                mm(ps[0:64, g, 0:N], [(wkv[:, 0, hh * 128:hh * 128 + 64], ckv_ap[:, c0:c1])], [ckv_t[ti]] + wkv_t, [PB[g]])
                                cp(ACT, KB[0:64, e, c0:c1], ps[0:64, g, 0:N], [PB[g]], [KB_t[ti]])
                            for sc_ in range(N // 128):
                                g = GB_.next()
                                cc0 = c0 + sc_ * 128
                                mm(ps[:, g, 0:256], [(ckv_ap[:, cc0:cc0 + 128], wkv[:, 0, sp * 256:(sp + 1) * 256])], [ckv_t[ti]] + wkv_t, [PB[g]])
                                cp(DVE, VB[:, cc0 // 128, :, 0:64], ps[:, g, 0:256].rearrange("p (e x d) -> p e x d", e=2, x=2)[:, :, 1, :], [PB[g]], [VB_t[ti]])
                        attention(range(5), 2, 96,
                                  lambda h, a, b: QB[:, h, a:b],
                                  lambda h, kc, KB=KB: KB[:, h, kc * 128:(kc + 1) * 128],
                                  lambda h, kc: VB[:, kc, h, :], 96 ** -0.5, "aug",
                                  lambda m, wi=wi: woB[:, wi, :], [woB_t[wi]], 1, l, vb, QB_t, KB_t, VB_t, ot_view=(OTb, OT_t), pt_view=(PTb, PT_t))

            def ffn(l, tiles, vb):
                norm_mod(l, 1, tiles, vb)
                GB_ = Rot([0, 1, 2, 3, 4, 5, 6, 7])
                wins = []
                if 0 in tiles:
                    wins.append((0, 256, 0, 256, 2, [0]))
                b = [0, 410, 820, 1230, 1639, 2048]
                for i in range(5):
                    h0 = max(b[i] - 1, 0) + 256; h1 = min(b[i + 1] + 1, 2048) + 256
                    tl = sorted(set([1 + (c - 256) // 512 for c in (b[i] + 256, b[i + 1] + 255)]))
                    wins.append((h0, h1, b[i] + 256, b[i + 1] + 256, vb, tl))
                passes = [list(range(0, 6)), list(range(6, 12)), list(range(12, 17)), list(range(17, 22))]
                act_ap, act_t = AR.alloc(0, 6 * T, 6, "ffact"); actb = act_ap.rearrange("p (c t) -> p c t", c=6)
                dn_ap, dn_t = AR.alloc(6 * T, 6144, 1, "ffdn"); dn = dn_ap.rearrange("p (c n) -> p c n", c=6)
                ub = []
                ub.append(WA.alloc(0, 4096, 1, "ffu0")); ub.append(WA.alloc(4096, 4096, 1, "ffu1"))
                cv_ap, cv_t = SC.alloc(0, 8192, 8, "ffcv"); cvt = cv_ap.bitcast(F32).rearrange("p (i n) -> p i n", i=8)
                cv_i = Rot([0, 1, 2]); sl_i = Rot([6, 7]); blk_i = 0
                pend = []

                def flush():
                    while pend:
                        (ta, xa, tg, xg, ci, o0, o1, No) = pend.pop(0)
                        xs = sl_i.next()
                        act(cvt[:, xs, 0:No], tg[:, 0:No], AF.Silu, [cv_t[xg]], [cv_t[xs]])
                        tt(POOL, actb[:, ci, o0:o1], ta[:, 0:No], cvt[:, xs, 0:No], ALU.mult, [cv_t[xa], cv_t[xs]], [act_t[ci]])
                for pl in passes:
                    npc = len(pl)
                    fw.op(POOL, lambda h, pl=pl, npc=npc: [h.dma_start(out=dn[:, i, :], in_=fdn_d[l][pl[i] * 128:(pl[i] + 1) * 128, :]) for i in range(npc)],
                          writes=dn_t, dma_key=dkey("w"), n_dma=npc)
                    for bi in range(0, npc, 2):
                        pcs = pl[bi:bi + 2]
                        u_ap, u_t = ub[blk_i % 2]; blk_i += 1
                        u3 = u_ap.rearrange("p (k n) -> p k n", k=8)
                        npb = len(pcs)
                        def ld(h, pcs=pcs, npb=npb, u3=u3):
                            r = []
                            for k in range(8):
                                r.append(h.dma_start(out=u3[:, k, 0:128 * npb], in_=fup_d[l][k * 128:(k + 1) * 128, pcs[0] * 128:(pcs[0] + npb) * 128]))
                                r.append(h.dma_start(out=u3[:, k, 256:256 + 128 * npb], in_=fup_d[l][k * 128:(k + 1) * 128, DFF + pcs[0] * 128:DFF + (pcs[0] + npb) * 128]))
                            return r
                        fw.op(POOL, ld, writes=u_t, dma_key=dkey("w"), n_dma=16)
                        for j, c in enumerate(pcs):
                            ci = pl.index(c)
                            for (h0, h1, o0, o1, v, tl) in wins:
                                Nh = h1 - h0; No = o1 - o0; d = o0 - h0
                                seq0 = 0 if o0 < 256 else 256; seq1 = 256 if o0 < 256 else T
                                res = []
                                for half in range(2):
                                    g = GB_.next()
                                    col = half * 256 + j * 128
                                    mm(ps[:, g, 0:Nh], [(u3[:, k, col:col + 128], zbuf[:, k, h0:h1]) for k in range(8)], [t_z[t] for t in tl] + u_t, [PB[g]])
                                    fi = half * NFF + c
                                    xi = (cv_i.next() if half == 0 else xi_a) + 3 * half
                                    xi_a = xi
                                    tb = cvt[:, xi, :]
                                    act(tb[:, 0:No], ps[:, g, d:d + No], AF.Identity, [PB[g], t_small], [cv_t[xi]],
                                        scale=cvw[:, l, 1, fi:fi + 1], bias=cvw[:, l, 3, fi:fi + 1])
                                    lo = 1 if o0 == seq0 else 0
                                    stt(DVE, tb[:, lo:No], ps[:, g, d - 1 + lo:d - 1 + No], cvw[:, l, 0, fi:fi + 1], tb[:, lo:No], ALU.mult, ALU.add,
                                        [PB[g], t_small, cv_t[xi]], [cv_t[xi]])
                                    hi = No - 1 if o1 == seq1 else No
                                    stt(DVE, tb[:, 0:hi], ps[:, g, d + 1:d + 1 + hi], cvw[:, l, 2, fi:fi + 1], tb[:, 0:hi], ALU.mult, ALU.add,
                                        [PB[g], t_small, cv_t[xi]], [cv_t[xi]])
                                    res.append((tb, xi))
                                (ta, xa), (tg, xg) = res
                                flush()
                                pend.append((ta, xa, tg, xg, ci, o0, o1, No))
                    flush()
                    for (h0, h1, o0, o1, v, tl) in wins:
                        No = o1 - o0
                        for mo in range(8):
                            g = GB_.next()
                            mm(ps[:, g, 0:No], [(dn[:, i, mo * 128:(mo + 1) * 128], actb[:, i, o0:o1]) for i in range(npc)],
                               [act_t[i] for i in range(npc)] + dn_t, [PB[g]])
                            gated_add(g, mo, o0, o1, tl, l, 40, v)

            if stage not in (1, 2, 3):
                ffn(0, range(5), vb)

            if stage == 0 or stage >= 5:
                l = 1
                norm_mod(l, 0, range(5), vb)
                wD_ap, wD_t = WA.alloc(0, 6144, 1, "wD")
                wD = wD_ap.rearrange("p (k n) -> p k n", k=8)
                load_w(wD, odwin_d[0][:, 1024:1792], 8, 768, wD_t)
                tab_ap, tab_t = SC.alloc(0, 4096, 1, "tabA")
                dma(POOL, tab_ap[:, 0:2048], consts_d[:, C_C64:C_C64 + 2048], writes=tab_t)
                dma(POOL, tab_ap[:, 2048:4096], consts_d[:, C_S64:C_S64 + 2048], writes=tab_t)
                Ct = tab_ap[:, 0:2048]; St = tab_ap[:, 2048:4096]
                wA_t = wD_t
                QA_ap, QA_t = AR.alloc(0, 9216, 5, "QD"); QA = QA_ap.rearrange("p (m t) -> p m t", m=4)
                KA_ap, KA_t = AR.alloc(9216, 4608, 5, "KD"); KA = KA_ap.rearrange("p (j t) -> p j t", j=2)
                VA_ap, VA_t = AR.alloc(13824, 4608, 5, "VD"); VA = VA_ap.rearrange("p (c j d) -> p c j d", c=18, j=2)
                fw.op(POOL, lambda h: h.memset(VA_ap, 1.0), writes=VA_t)
                proj_qkv(wD, wD_t, range(5), False, QA, QA_t, KA, KA_t, VA, VA_t, range(1, 5))
                KZ, KZ_t = pad_k(KA_ap, KA_t)
                woA_ap, woA_t = WA.alloc(0, 4096, 1, "woD")
                woA = woA_ap.rearrange("p (k n) -> p k n", k=4)
                load_w(woA, odwout_d[0][512:1024, :], 4, 1024, woA_t)
                OT_ap, OT_t = SC.alloc(0, 4096, 2, "OT"); OT = OT_ap.rearrange("p (o m n) -> p o m n", o=2, m=4)
                PT_ap, PT_t = SC.alloc(4096, 3072, 6, "PT"); PT = PT_ap.rearrange("p (i n) -> p i n", i=6)
                attention(range(1, 5), 8, 64,
                          lambda h, a, b: QA[:, h // 2, a:b],
                          lambda h, kc: (KA if h % 2 == 0 else KZ)[:, h // 4, kc * 128:(kc + 1) * 128],
                          lambda h, kc: VA[:, kc, h // 4, :], 64 ** -0.5, "aug",
                          lambda m: woA[:, m, :], woA_t, 4, l, vb, QA_t, KA_t, VA_t, window=True, sink=True,
                          ot_view=(OT, OT_t), pt_view=(PT, PT_t), k_extra=KZ_t)
                if stage != 5:
                    wC_ap, wC_t = WA.alloc(0, 8192, 1, "wC")
                    wC = wC_ap.rearrange("p (k n) -> p k n", k=8)
                    load_w(wC, odwin_d[0][:, 0:1024], 8, 1024, wC_t)
                    ug_ap, ug_t = AR.alloc(0, 8192, 4, "ug"); ug = ug_ap.rearrange("p (m t) -> p m t", m=4)
                    vl_ap, vl_t = AR.alloc(8192, 8192, 16, "vln"); vln = vl_ap.rearrange("p (n c) -> p n c", n=16)
                    wst_ap, wst_t = AR.alloc(16384, 1024, 1, "wst"); wst = wst_ap.rearrange("p (g q) -> p g q", g=8)
                    bt_ap, bt_t = AR.alloc(17408, 1024, 1, "btab"); btab = bt_ap.bitcast(F32).rearrange("p (c q) -> p c q", c=4)
                    lg_ap, lg_t = AR.alloc(18432, 2048, 1, "lgb"); lgb = lg_ap.bitcast(F32).rearrange("p (i c) -> p i c", i=2)
                    st_ap, st_t = SC.alloc(0, 2048, 2, "sgstg")
                    dma(SP, lgb[:, 0, :], odlng_d[0].partition_broadcast(128), writes=lg_t)
                    dma(SP, lgb[:, 1, :], odlnb_d[0].partition_broadcast(128), writes=lg_t)
                    for g_ in range(8):
                        dma(SP, btab[64 * (g_ % 2):64 * (g_ % 2) + 64, g_ // 2, :], odsb_d[0, g_].partition_broadcast(64), writes=bt_t)
                        wstg = st_ap[:, (g_ % 2) * 1024:(g_ % 2) * 1024 + 256].bitcast(F32)
                        dma(SP, wstg, odsw_d[0, g_], writes=[st_t[g_ % 2]])
                        g = GB_.next()
                        fw.op(PE, lambda h, g=g, wstg=wstg: h.transpose(ps[:, g, 0:128], wstg, IDN), [st_t[g_ % 2], t_cst], [PB[g]])
                        cp(ACT, wst[:, g_, :], ps[:, g, 0:128], [PB[g]], wst_t)
                    stat = rd
                    for ti in range(1, 5):
                        c0, c1, isc = TILES[ti]; N = c1 - c0
                        for m in range(4):
                            g = GB_.next()
                            mm(ps[:, g, 0:N], [(wC[:, k, m * 128:(m + 1) * 128], zbuf[:, k, c0:c1]) for k in range(8)], [t_z[ti]] + wC_t, [PB[g]])
                            act(ug[:, m, c0 - 256:c1 - 256], ps[:, g, 0:N], AF.Gelu, [PB[g]], [ug_t[ti - 1]])
                        for sc_ in range(4):
                            n = (ti - 1) * 4 + sc_
                            cc0 = c0 + sc_ * 128
                            g = GB_.next()
                            mm(ps[:, g, :], [(zbuf[:, k, cc0:cc0 + 128], wC[:, k, 512:1024]) for k in range(8)], [t_z[ti]] + wC_t, [PB[g]])
                            mi = tmp_i.next(); m2 = tmp_i.next()
                            act(tmp32[:, mi, :], ps[:, g, :], AF.Gelu, [PB[g]], [t_tmp[mi], t_rd], accum_out=stat[:, 0:1])
                            act(tmp32[:, m2, :], tmp32[:, mi, :], AF.Square, [t_tmp[mi]], [t_tmp[m2], t_rd], accum_out=stat[:, 1:2])

                            def stats(h):
                                h.tensor_scalar(out=stat[:, 2:3], in0=stat[:, 0:1], scalar1=1.0 / 512, scalar2=None, op0=ALU.mult)
                                h.tensor_tensor(out=stat[:, 3:4], in0=stat[:, 2:3], in1=stat[:, 2:3], op=ALU.mult)
                                return h.scalar_tensor_tensor(out=stat[:, 4:5], in0=stat[:, 1:2], scalar=1.0 / 512, in1=stat[:, 3:4], op0=ALU.mult, op1=ALU.subtract)
                            fw.op(DVE, lambda h: h.tensor_scalar(out=stat[:, 2:3], in0=stat[:, 0:1], scalar1=1.0 / 512, scalar2=None, op0=ALU.mult), [t_rd], [t_rd])
                            fw.op(DVE, lambda h: h.tensor_tensor(out=stat[:, 3:4], in0=stat[:, 2:3], in1=stat[:, 2:3], op=ALU.mult), [t_rd], [t_rd])
                            fw.op(DVE, lambda h: h.scalar_tensor_tensor(out=stat[:, 4:5], in0=stat[:, 1:2], scalar=1.0 / 512, in1=stat[:, 3:4], op0=ALU.mult, op1=ALU.subtract), [t_rd], [t_rd])
                            act(stat[:, 5:6], stat[:, 4:5], AF.Ln, [t_rd, t_small], [t_rd], bias=epst[:, 0:1])
                            act(stat[:, 5:6], stat[:, 5:6], AF.Exp, [t_rd], [t_rd], scale=-0.5)
                            fw.op(DVE, lambda h, mi=mi: h.tensor_scalar(out=tmp32[:, mi, :], in0=tmp32[:, mi, :], scalar1=stat[:, 2:3], scalar2=stat[:, 5:6], op0=ALU.subtract, op1=ALU.mult),
                                  [t_tmp[mi], t_rd], [t_tmp[mi]])
                            tt(DVE, tmp32[:, mi, :], tmp32[:, mi, :], lgb[:, 0, :], ALU.mult, [t_tmp[mi]] + lg_t, [t_tmp[mi]])
                            tt(POOL, vln[:, n, :], tmp32[:, mi, :], lgb[:, 1, :], ALU.add, [t_tmp[mi]] + lg_t, [vl_t[n]])
                        for cc in range(4):
                            g = GB_.next()

                            def sp_(h, g=g, cc=cc, ti=ti):
                                ins = None
                                for e in range(2):
                                    for sc_ in range(4):
                                        n = (ti - 1) * 4 + sc_
                                        gg = 2 * cc + e
                                        ins = h.matmul(ps[64 * e:64 * e + 64, g, sc_ * 128:(sc_ + 1) * 128], lhsT=vln[:, n, gg * 64:(gg + 1) * 64], rhs=wst[:, gg, :], start=True, stop=True)
                                return ins
                            fw.op(PE, sp_, [vl_t[(ti - 1) * 4 + i] for i in range(4)] + wst_t, [PB[g]], cost=8 * 0.1)
                            mi = tmp_i.next()
                            for sc_ in range(4):
                                tt(DVE, tmp32[:, mi, sc_ * 128:(sc_ + 1) * 128], ps[:, g, sc_ * 128:(sc_ + 1) * 128], btab[:, cc, :], ALU.add, [PB[g]] + bt_t, [t_tmp[mi]])
                            tt(POOL, ug[:, cc, c0 - 256:c1 - 256], ug[:, cc, c0 - 256:c1 - 256], tmp32[:, mi, :], ALU.mult, [t_tmp[mi], ug_t[ti - 1]], [ug_t[ti - 1]])
                    woC_ap, woC_t = WA.alloc(0, 4096, 1, "woC")
                    woC = woC_ap.rearrange("p (k n) -> p k n", k=4)
                    load_w(woC, odwout_d[0][0:512, :], 4, 1024, woC_t)
                    for ti in range(1, 5):
                        c0, c1, isc = TILES[ti]; N = c1 - c0
                        for mo in range(8):
                            g = GB_.next()
                            mm(ps[:, g, 0:N], [(woC[:, m, mo * 128:(mo + 1) * 128], ug[:, m, c0 - 256:c1 - 256]) for m in range(4)], [ug_t[ti - 1]] + woC_t, [PB[g]])
                            gated_add(g, mo, c0, c1, [ti], l, 16, vb)
                    if stage != 6:
                        ffn(1, range(1, 5), vb)

            ost_ap, ost_t = SC.alloc(0, 4096, 2, "ostg")
            fin_ap, fin_t = SC.alloc(4096, 4096, 1, "fin")
            for ti in (range(1, 5) if 'f' not in SKIP else []):
                c0, c1, isc = TILES[ti]; N = c1 - c0
                if stage == 0:
                    g = GB_.next()
                    for k in range(8):
                        si = sq_i.next()
                        act(sq[:, si, 0:N], resid[:, k, c0:c1], AF.Square, [t_res[k][ti]], [t_sq[si]])
                        fw.op(PE, (lambda h, g=g, si=si, k=k, N=N: h.matmul(ps[:, g, 0:N], lhsT=onesm[:, 0, :], rhs=sq[:, si, 0:N], start=(k == 0), stop=(k == 7))),
                              [t_sq[si], t_small], [PB[g]])
                    ri = rstd_i.next()
                    act(rstd[:, ri, 0:N], ps[:, g, 0:N], AF.Ln, [PB[g], t_small], [t_rstd[ri]], bias=epst[:, 0:1])
                    act(rstd[:, ri, 0:N], rstd[:, ri, 0:N], AF.Exp, [t_rstd[ri]], [t_rstd[ri]], scale=-0.5)
                    for k in range(8):
                        stt(DVE, resid[:, k, c0:c1], resid[:, k, c0:c1], gsm[:, 4, k:k + 1], rstd[:, ri, 0:N], ALU.mult, ALU.mult,
                            [t_res[k][ti], t_small, t_rstd[ri]], [t_res[k][ti]])
                for sc_ in range(4):
                    cc0 = c0 + sc_ * 128
                    bi = sc_ % 2
                    obuf = ost_ap[:, bi * 2048:(bi + 1) * 2048].bitcast(F32)
                    for hk in range(2):
                        g = GB_.next()

                        def trb(h, g=g, hk=hk, cc0=cc0):
                            ins = None
                            for kk in range(4):
                                k = hk * 4 + kk
                                ins = h.transpose(ps[:, g, kk * 128:(kk + 1) * 128], resid[:, k, cc0:cc0 + 128], IDN)
                            return ins
                        fw.op(PE, trb, [t_res[k][ti] for k in range(hk * 4, hk * 4 + 4)] + [t_cst], [PB[g]], cost=1.2)
                        cp(ACT if hk == 0 else DVE, obuf[:, hk * 512:(hk + 1) * 512], ps[:, g, :], [PB[g]], [ost_t[bi]])
                    dma(SP, out_d[s, cc0 - 256:cc0 - 128, :], obuf, reads=[ost_t[bi]], writes=[Tok("o")], key=f"o{bi}")
        fw.emit(st)
    return nc, fw


_CACHE = {}


def kernel(**inputs):
    nseq = int(os.environ.get("K_NSEQ", "2"))
    stage = int(os.environ.get("K_STAGE", "0"))
    key = (nseq, stage)
    if key not in _CACHE:
        _CACHE[key] = build_nc(nseq, stage)
    nc, fw = _CACHE[key]
    consts = make_consts()
    f = lambda a: np.ascontiguousarray(np.asarray(a, dtype=np.float32))
    shared = {k: f(v) for k, v in inputs.items() if k not in ("x", "c", "ctx")}
    shared["consts"] = consts
    x = f(inputs["x"]); c = f(inputs["c"]); ctx = f(inputs["ctx"])
    in_maps = []
    for i in range(8):
        m = dict(shared)
        m["x"] = x[2 * i:2 * i + 2]
        m["c"] = c[2 * i:2 * i + 2]
        m["ctx"] = ctx[2 * i:2 * i + 2]
        in_maps.append(m)
    res = run_bass_kernel_spmd(nc, in_maps, core_ids=list(range(8)))
    out = np.concatenate([np.asarray(r["out"], dtype=np.float32) for r in res.results], axis=0)
    return out
```

```python
import os
import numpy as np
from contextlib import ExitStack
import concourse.bass as bass
import concourse.mybir as mybir
from concourse.bass_utils import run_bass_kernel_spmd

F32 = mybir.dt.float32
BF16 = mybir.dt.bfloat16
AF = mybir.ActivationFunctionType
ALU = mybir.AluOpType
PE, ACT, DVE, POOL, SP = "pe", "act", "dve", "pool", "sp"
ENGS = (PE, ACT, DVE, POOL, SP)
SEM_LIM = 30000
STRICT = os.environ.get('K_STRICT', '1') == '1'


class Tok:
    __slots__ = ("name", "w", "r", "rd", "pre", "frozen")

    def __init__(self, name="", pre=None):
        self.name = name
        self.w = None
        self.r = {}
        self.rd = []
        self.pre = pre
        self.frozen = False


class Op:
    __slots__ = ("eng", "fn", "deps", "odeps", "sem", "val", "dma_key", "n_dma", "waited", "idx", "is_dma",
                 "cost", "gidx", "pos", "fin", "done", "st", "tag")


class FW:
    def __init__(self, nc, n_phase=2):
        self.nc = nc
        self.ops = {e: [] for e in ENGS}
        self.n_phase = n_phase
        self.dma_keys = {}
        self.last_dma = {}
        self.nops = 0

    def op(self, eng, fn, reads=(), writes=(), dma_key=None, n_dma=1, cost=0.3):
        o = Op()
        o.eng = eng
        o.fn = fn
        o.is_dma = dma_key is not None
        o.dma_key = dma_key
        o.n_dma = n_dma
        o.waited = False
        o.sem = None
        o.val = 0
        o.cost = cost
        o.idx = len(self.ops[eng])
        o.gidx = self.nops
        self.nops += 1
        deps = {}
        odeps = {}

        def add(d, raw):
            if d is None or d is o:
                return
            if (not d.is_dma) and (not o.is_dma) and d.eng == eng:
                if eng == PE or (not raw and not STRICT):
                    odeps[id(d)] = d
                    return
            deps[id(d)] = d

        def add_all(t):
            add(t.w, False)
            for r in t.r.values():
                add(r, False)
            for r in t.rd:
                add(r, False)

        for t in reads:
            add(t.w, True)
        for t in writes:
            add_all(t)
            if t.pre:
                for p in t.pre:
                    add_all(p)
                t.pre = None
        for t in reads:
            if t.frozen:
                continue
            if o.is_dma:
                t.rd.append(o)
            else:
                prev = t.r.get(eng)
                if prev is not None and prev is not o:
                    odeps[id(prev)] = prev
                t.r[eng] = o
        for t in writes:
            assert not t.frozen, t.name
            t.w = o
            t.r = {}
            t.rd = []
        if o.is_dma:
            prev = self.last_dma.get(dma_key)
            if prev is not None:
                deps[id(prev)] = prev
            self.last_dma[dma_key] = o
        if os.environ.get("K_SERIAL") and getattr(self, "last", None) is not None:
            lo = self.last
            if lo.is_dma or lo.eng != eng or o.is_dma:
                deps[id(lo)] = lo
            else:
                odeps[id(lo)] = lo
        self.last = o
        o.deps = list(deps.values())
        o.odeps = list(odeps.values())
        for d in o.deps:
            d.waited = True
        self.ops[eng].append(o)
        if o.is_dma:
            self.dma_keys[dma_key] = self.dma_keys.get(dma_key, 0) + 16 * n_dma
            o.val = self.dma_keys[dma_key]
        return o

    def schedule(self, window=24):
        for e in ENGS:
            for o in self.ops[e]:
                o.done = False
                o.fin = 0.0
        use_bl = os.environ.get("K_BL", "1") == "1"
        if use_bl:
            allops = sorted([o for e in ENGS for o in self.ops[e]], key=lambda o: o.gidx)
            succ = {id(o): [] for o in allops}
            for o in allops:
                for d in o.deps:
                    succ[id(d)].append(o)
                for d in o.odeps:
                    succ[id(d)].append(o)
            bl = {}
            for o in reversed(allops):
                m = 0.0
                for s_ in succ[id(o)]:
                    v = bl[id(s_)]
                    if v > m:
                        m = v
                bl[id(o)] = m + o.cost + (2.0 if o.is_dma else 0.15)
            self.bl = bl
        pending = {e: list(self.ops[e]) for e in ENGS}
        free = {e: 0.0 for e in ENGS}
        new = {e: [] for e in ENGS}
        dma_lat = 2.0
        remaining = sum(len(v) for v in pending.values())
        eps = float(os.environ.get('K_EPS', '0.0'))
        while remaining:
            best = None
            cands = []
            for e in ENGS:
                pl = pending[e]
                if not pl:
                    continue
                lim = min(window, len(pl))
                if e in (SP, POOL):
                    lim = min(int(os.environ.get('K_SPW', '8')), len(pl)) if e == SP else min(int(os.environ.get('K_PLW', '24')), len(pl))
                for i in range(lim):
                    o = pl[i]
                    ok = True
                    st = free[e]
                    for d in o.deps:
                        if not d.done:
                            ok = False
                            break
                        if d.fin > st:
                            st = d.fin
                    if not ok:
                        continue
                    for d in o.odeps:
                        if not d.done:
                            ok = False
                            break
                    if not ok:
                        continue
                    if o.is_dma:
                        blocked = False
                        for j in range(i):
                            if pl[j].is_dma and pl[j].dma_key == o.dma_key:
                                blocked = True
                                break
                        if blocked:
                            continue
                    key = (st, -self.bl[id(o)] if use_bl else o.gidx, o.gidx) if use_bl else (st, o.gidx)
                    if use_bl:
                        cands.append((st, -self.bl[id(o)], o.gidx, e, i, o))
                    if best is None or key < best[0]:
                        best = (key, e, i, o, st)
                    if (not use_bl) and i == 0 and st <= free[e]:
                        break
            assert best is not None, "scheduler deadlock"
            _, e, i, o, st = best
            if use_bl and eps > 0:
                lim_st = st + eps
                c2 = min((c for c in cands if c[0] <= lim_st), key=lambda c: (c[1], c[0], c[2]))
                st, _, _, e, i, o = c2
            pending[e].pop(i)
            o.done = True
            o.st = st
            if o.is_dma:
                free[e] = st + 0.1
                o.fin = st + dma_lat + o.cost
            else:
                free[e] = st + o.cost
                o.fin = st + o.cost + float(os.environ.get('K_LAT', '0.2'))
            new[e].append(o)
            remaining -= 1
        self.ops = new
        self.est = max(free.values())

    def emit(self, stack):
        nc = self.nc
        if os.environ.get("K_SCHED", "1") != "0":
            self.schedule()
        engsems = {e: [stack.enter_context(nc.semaphore(f"s_{e}_{p}")) for p in range(self.n_phase)]
                   for e in ENGS}
        dmasems = {k: stack.enter_context(nc.semaphore(f"d_{k}")) for k in self.dma_keys}
        cum = {}
        for e in ENGS:
            cnt = 0
            for pos, o in enumerate(self.ops[e]):
                o.pos = pos
                if o.is_dma:
                    o.sem = dmasems[o.dma_key]
                    cum[o.dma_key] = cum.get(o.dma_key, 0) + 16 * o.n_dma
                    o.val = cum[o.dma_key]
                elif o.waited:
                    ph = cnt // SEM_LIM
                    assert ph < self.n_phase, f"too many waited ops on {e}"
                    o.sem = engsems[e][ph]
                    o.val = cnt % SEM_LIM + 1
                    cnt += 1
        assert cum == self.dma_keys
        for e in ENGS:
            for o in self.ops[e]:
                best = {}
                keep = []
                for d in o.deps:
                    if d.is_dma:
                        keep.append(d)
                    else:
                        b = best.get(d.eng)
                        if b is None or b.pos < d.pos:
                            best[d.eng] = d
                o.deps = keep + list(best.values())
        final = dict(self.dma_keys)
        self.stats = {}

        def run(eng, h):
            seen = {}
            nwait = 0
            for o in self.ops[eng]:
                for d in o.deps:
                    sid = id(d.sem)
                    if seen.get(sid, 0) < d.val:
                        h.wait_ge(d.sem, d.val)
                        seen[sid] = d.val
                        nwait += 1
                res = o.fn(h)
                if o.is_dma:
                    if not isinstance(res, (list, tuple)):
                        res = [res]
                    assert len(res) == o.n_dma, (len(res), o.n_dma)
                    for ins in res:
                        ins.then_inc(o.sem, 16)
                elif o.waited:
                    res.then_inc(o.sem, 1)
            if eng == SP:
                for k, v in final.items():
                    h.wait_ge(dmasems[k], v)
            self.stats[eng] = (len(self.ops[eng]), nwait)

        with nc.Block() as block:
            @block.tensor
            def _(h):
                run(PE, h)

            @block.scalar
            def _(h):
                run(ACT, h)

            @block.vector
            def _(h):
                run(DVE, h)

            @block.gpsimd
            def _(h):
                run(POOL, h)

            @block.sync
            def _(h):
                run(SP, h)


class Arena:
    def __init__(self, nc, stack, name, ncols):
        self.t = stack.enter_context(nc.sbuf_tensor(name, [128, ncols], BF16))
        self.ncols = ncols
        self.views = []

    def alloc(self, c0, n, ntok=1, name=""):
        assert c0 + n <= self.ncols, (name, c0, n, self.ncols)
        pre = []
        keep = []
        for (a, b, toks) in self.views:
            if a < c0 + n and c0 < b:
                pre.extend(toks)
                if a >= c0 and b <= c0 + n:
                    continue
            keep.append((a, b, toks))
        toks = [Tok(f"{name}{i}", pre=list(pre)) for i in range(ntok)]
        keep.append((c0, c0 + n, toks))
        self.views = keep
        return self.t[:, c0:c0 + n], toks


D = 1024
S = 2048
CTX = 256
T = S + CTX
DFF = 2816
NFF = 22
EPS = 1e-6
TILES = [(0, 256, True), (256, 768, False), (768, 1280, False), (1280, 1792, False), (1792, 2304, False)]
C_C64, C_S64, C_CR, C_SR = 0, 2048, 4096, 6144
C_P64, C_PR, C_ID, C_MA, C_MB = 8192, 8320, 8448, 8576, 8704
NCONST = 8832


def make_consts():
    theta = 10000.0
    n_rows = S // 64
    rows = np.repeat(np.arange(n_rows), 64).astype(np.float32)
    cols = np.tile(np.arange(64), n_rows).astype(np.float32)

    def tab(dim):
        q = dim // 4
        inv = (theta ** (-np.arange(q, dtype=np.float32) / q)).astype(np.float32)
        ang = np.concatenate([rows[:, None] * inv, cols[:, None] * inv], axis=-1).astype(np.float32)
        return np.cos(ang).astype(np.float32), np.sin(ang).astype(np.float32)

    c = np.zeros((128, NCONST), np.float32)
    ch, sh = tab(64)
    cr, sr = tab(32)
    for p in range(128):
        i = (p % 64) % 32
        c[p, C_C64:C_C64 + S] = ch[:, i]
        c[p, C_S64:C_S64 + S] = sh[:, i]
    for p in range(64, 96):
        i = (p - 64) % 16
        c[p, C_CR:C_CR + S] = cr[:, i]
        c[p, C_SR:C_SR + S] = sr[:, i]
    for d in range(128):
        dd = d % 64
        base = d - dd
        if dd < 32:
            c[base + dd + 32, C_P64 + d] = -1.0
        else:
            c[base + dd - 32, C_P64 + d] = 1.0
    for d in range(64, 96):
        dd = d - 64
        if dd < 16:
            c[64 + dd + 16, C_PR + d] = -1.0
        else:
            c[64 + dd - 16, C_PR + d] = 1.0
    c[:, C_ID:C_ID + 128] = np.eye(128, dtype=np.float32)
    jj = np.arange(128)[:, None]
    ii = np.arange(128)[None, :]
    c[:, C_MA:C_MA + 128] = (ii <= jj).astype(np.float32)
    c[:, C_MB:C_MB + 128] = (jj <= ii).astype(np.float32)
    return c


def build_nc(nseq=2, stage=0):
    nc = bass.Bass("TRN2", target_bir_lowering=False)
    dt = lambda name, shape: nc.dram_tensor(name, shape, F32, kind="ExternalInput").ap()
    x_d = dt("x", [2, S, D]); c_d = dt("c", [2, D]); ctx_d = dt("ctx", [2, CTX, D]); cctx_d = dt("c_ctx", [D])
    modw_d = dt("mod_w", [2, D, 6 * D]); modb_d = dt("mod_b", [2, 6 * D])
    n1g_d = dt("norm1_g", [2, D]); n2g_d = dt("norm2_g", [2, D])
    evwin_d = dt("ev_w_in", [1, D, 1184]); evqag_d = dt("ev_qa_g", [1, 64]); evkag_d = dt("ev_ka_g", [1, 64])
    evqlg_d = dt("ev_qlat_g", [1, 256]); evwqup_d = dt("ev_w_q_up", [1, 256, 768]); evkvg_d = dt("ev_kvlat_g", [1, 128])
    evwkv_d = dt("ev_w_kv_up", [1, 128, 1024]); evwout_d = dt("ev_w_out", [1, 1024, D])
    odwin_d = dt("od_w_in", [1, D, 1792]); odlng_d = dt("od_ln_g", [1, 512]); odlnb_d = dt("od_ln_b", [1, 512])
    odsw_d = dt("od_sgu_w", [1, 8, 128, 128]); odsb_d = dt("od_sgu_b", [1, 8, 128]); odsink_d = dt("od_sink", [1, 8])
    odwout_d = dt("od_w_out", [1, 1024, D])
    fup_d = dt("ffn_up", [2, D, 2 * DFF]); fcw_d = dt("ffn_conv_w", [2, 3, 2 * DFF]); fcb_d = dt("ffn_conv_b", [2, 2 * DFF])
    fdn_d = dt("ffn_down", [2, DFF, D]); fing_d = dt("final_g", [D])
    consts_d = dt("consts", [128, NCONST])
    out_d = nc.dram_tensor("out", [2, S, D], F32, kind="ExternalOutput").ap()

    fw = FW(nc)
    st = ExitStack()
    with st:
        sbt = lambda name, shape, d=F32: st.enter_context(nc.sbuf_tensor(name, shape, d))
        resid = sbt("resid", [128, 8, T])
        zbuf = sbt("zbuf", [128, 8, T], BF16)
        WA = Arena(nc, st, "warena", 9216)
        AR = Arena(nc, st, "arena", 21248)
        SC = Arena(nc, st, "scr", 8192)
        sq = sbt("sq", [128, 2, 512], BF16)
        rstd = sbt("rstd", [128, 2, 512])
        tmp32 = sbt("tmp32", [128, 3, 512])
        rd = sbt("rd", [128, 512])
        cst = sbt("cst", [128, 640])
        msk = sbt("msk", [128, 256], BF16)
        onesm = sbt("onesm", [128, 5, 128], BF16)
        epst = sbt("epst", [128, 1])
        modT = sbt("modT", [128, 2, 48, 3])
        scv = sbt("scv", [128, 2, 2, 8, 3])
        csb = sbt("csb", [128, 8, 3])
        gsm = sbt("gsm", [128, 5, 8])
        gq = sbt("gq", [128, 4])
        gql = sbt("gql", [128, 2])
        cvw = sbt("cvw", [128, 2, 4, 44])
        esink = sbt("esink", [128, 8])
        ps = st.enter_context(nc.psum_tensor("ps", [128, 8, 512], F32))
        PB = [Tok(f"pb{i}") for i in range(8)]

        class Rot:
            def __init__(self, idx):
                self.idx = list(idx); self.i = 0
            def next(self):
                v = self.idx[self.i % len(self.idx)]; self.i += 1
                return v
        PAIR_EXP = os.environ.get('K_PAIR', '0') == '1'
        SB_ = Rot([0, 1, 2, 3] if PAIR_EXP else [0, 1, 2]); OB_ = Rot([4, 5] if PAIR_EXP else [3, 4]); GB_ = Rot([6, 7] if PAIR_EXP else [5, 6, 7])
        t_sq = [Tok("sq0"), Tok("sq1")]; t_rstd = [Tok("rstd0"), Tok("rstd1")]
        t_tmp = [Tok("tmp0"), Tok("tmp1"), Tok("tmp2")]; t_rd = Tok("rd")
        sq_i = Rot([0, 1]); rstd_i = Rot([0, 1]); tmp_i = Rot([0, 1, 2])
        t_cst = Tok("cst"); t_small = Tok("small"); t_mod = Tok("mod")
        t_res = [[Tok(f"res{k}_{t}") for t in range(5)] for k in range(8)]
        t_z = [Tok(f"z{t}") for t in range(5)]
        dk = [0]

        def dkey(p="k"):
            dk[0] += 1
            return f"{p}{dk[0] % 8}"

        def fsz(ap):
            n = 1
            for d_ in ap.shape[1:]:
                n *= d_
            return n

        def mmcost(r):
            return fsz(r) / 2400.0 * (4 if r.dtype == F32 else 1) + 0.005

        def dma(eng, out, in_, reads=(), writes=(), key=None, **kw):
            c = fsz(out) * 128 * 4 / 2.0e5
            fw.op(eng, lambda h: h.dma_start(out=out, in_=in_, **kw), reads=reads, writes=writes, dma_key=key or dkey('w' if eng == POOL else 'k'), cost=c)

        def mm(out, pairs, reads, writes):
            def fn(h):
                n = len(pairs); ins = None
                for i, (l, r) in enumerate(pairs):
                    ins = h.matmul(out, lhsT=l, rhs=r, start=(i == 0), stop=(i == n - 1))
                return ins
            fw.op(PE, fn, reads, writes, cost=sum(mmcost(r) for _, r in pairs))

        def ecost(eng, out):
            n = fsz(out)
            if eng == ACT:
                return n / 1200.0 + 0.22
            if eng == DVE:
                return n / 800.0 + 0.1
            return n / 500.0 + 0.2

        def act(out, in_, func, reads, writes, **kw):
            fw.op(ACT, lambda h: h.activation(out=out, in_=in_, func=func, **kw), reads, writes, cost=ecost(ACT, out))

        def stt(eng, out, in0, scalar, in1, op0, op1, reads, writes):
            fw.op(eng, lambda h: h.scalar_tensor_tensor(out=out, in0=in0, scalar=scalar, in1=in1, op0=op0, op1=op1), reads, writes, cost=ecost(eng, out))

        def tt(eng, out, in0, in1, op, reads, writes):
            fw.op(eng, lambda h: h.tensor_tensor(out=out, in0=in0, in1=in1, op=op), reads, writes, cost=ecost(eng, out))

        def cp(eng, out, in_, reads, writes):
            if eng == ACT:
                fw.op(ACT, lambda h: h.copy(out=out, in_=in_), reads, writes, cost=ecost(ACT, out))
            else:
                fw.op(eng, lambda h: h.tensor_copy(out=out, in_=in_), reads, writes, cost=ecost(eng, out))

        dma(SP, cst[:, 0:384], consts_d[:, C_P64:C_P64 + 384], writes=[t_cst])
        dma(POOL, msk[:], consts_d[:, C_MA:C_MA + 256], writes=[t_cst])
        P64 = cst[:, 0:128]; PR = cst[:, 128:256]; IDN = cst[:, 256:384]

        def init_small(h):
            h.memset(onesm[:, 0, :], 1.0 / 1024)
            h.memset(onesm[0:64, 1, 64:128], 0.0)
            h.memset(onesm[64:128, 1, 0:64], 0.0)
            h.memset(onesm[0:64, 1, 0:64], 1.0 / 64)
            h.memset(onesm[64:128, 1, 64:128], 1.0 / 64)
            h.memset(onesm[:, 2, :], 1.0 / 256)
            h.memset(onesm[:, 3, :], 1.0 / 128)
            h.memset(onesm[:, 4, :], 1.0)
            return h.memset(epst[:], EPS)
        fw.op(DVE, init_small, writes=[t_small])
        fm = lambda v: v.rearrange("(k p) -> p k", p=128)
        SKIP = os.environ.get('K_SKIP', '')
        for i, src in enumerate([n1g_d[0], n1g_d[1], n2g_d[0], n2g_d[1], fing_d]):
            dma(SP, gsm[:, i, :], fm(src), writes=[t_small], allow_slow_non_contiguous=True)
        for half in (range(2) if 'a' not in SKIP else []):
            dma(SP, gq[64 * half:64 * half + 64, 0:1], evqag_d[0].rearrange("(p o) -> p o", o=1), writes=[t_small], allow_slow_non_contiguous=True)
            dma(SP, gq[64 * half:64 * half + 64, 1:2], evkag_d[0].rearrange("(p o) -> p o", o=1), writes=[t_small], allow_slow_non_contiguous=True)
        dma(SP, gq[:, 2:3], evkvg_d[0].rearrange("(p o) -> p o", o=1), writes=[t_small], allow_slow_non_contiguous=True)
        dma(SP, gql[:], fm(evqlg_d[0]), writes=[t_small], allow_slow_non_contiguous=True)
        for l in (range(2) if 'b' not in SKIP else []):
            for i in range(3):
                dma(SP, cvw[:, l, i, :], fcw_d[l, i].rearrange("(k p) -> p k", p=128), writes=[t_small], allow_slow_non_contiguous=True)
            dma(SP, cvw[:, l, 3, :], fcb_d[l].rearrange("(k p) -> p k", p=128), writes=[t_small], allow_slow_non_contiguous=True)
        dma(SP, esink[:], odsink_d[0].partition_broadcast(128), writes=[t_small])
        act(esink[:], esink[:], AF.Exp, [t_small], [t_small])
        for b in (range(2) if 'c' not in SKIP else []):
            dma(SP, csb[:, :, b], fm(c_d[b]), writes=[t_mod], allow_slow_non_contiguous=True)
        dma(SP, csb[:, :, 2], fm(cctx_d), writes=[t_mod], allow_slow_non_contiguous=True)
        act(csb[:], csb[:], AF.Silu, [t_mod], [t_mod])

        def load_seq(s):
            stg2_ap, stg2_t = SC.alloc(0, 4096, 2, "instg")
            for i in (range(18) if 'e' not in SKIP else []):
                src = ctx_d[s, i * 128:(i + 1) * 128, :] if i < 2 else x_d[s, (i - 2) * 128:(i - 1) * 128, :]
                bi = i % 2
                sbuf = stg2_ap[:, bi * 2048:(bi + 1) * 2048].bitcast(F32)
                dma(SP, sbuf, src, writes=[stg2_t[bi]])
                col = i * 128
                ti = 0 if i < 2 else 1 + (i - 2) // 4
                for hk in range(2):
                    g = GB_.next()

                    def trf(h, g=g, hk=hk, sbuf=sbuf):
                        ins = None
                        for kk in range(4):
                            k = hk * 4 + kk
                            ins = h.transpose(ps[:, g, kk * 128:(kk + 1) * 128], sbuf[:, k * 128:(k + 1) * 128], IDN)
                        return ins
                    fw.op(PE, trf, [stg2_t[bi], t_cst], [PB[g]], cost=1.2)
                    cp(ACT if hk == 0 else DVE, resid[:, hk * 4:hk * 4 + 4, col:col + 128], ps[:, g, :].rearrange("p (a b) -> p a b", a=4),
                       [PB[g]], [t_res[k][ti] for k in range(hk * 4, hk * 4 + 4)])


        load_seq(0)

        stg_ap, stg_t = AR.alloc(0, 16384, 2, "modstg")
        modv_ap, modv_t = AR.alloc(16384, 4864, 1, "modv")
        AR.views = []
        stg_ap, stg_t = AR.alloc(0, 8192, 1, "modstg")
        modv_ap, modv_t = AR.alloc(8192, 12288, 1, "modv")
        stg = stg_ap.bitcast(F32).rearrange("p (k n) -> p k n", k=8)
        stgb_ap, stgb_t = WA.alloc(0, 8192, 1, "modstg2")
        stgb = stgb_ap.bitcast(F32).rearrange("p (k n) -> p k n", k=8)
        stgs = [(stg, stg_t), (stgb, stgb_t)]
        modv = modv_ap.bitcast(F32)
        modb_ap, modb_t = SC.alloc(0, 8192, 1, "modb")
        for l in (range(2) if 'd' not in SKIP else []):
            for nt in range(12):
                stg_c, stg_ct = stgs[nt % 2]
                dma(SP, stg_c, modw_d[l].rearrange("(k p) n -> p k n", p=128)[:, :, nt * 512:(nt + 1) * 512], writes=stg_ct)
                g = GB_.next()
                mm(ps[0:3, g, :], [(csb[:, k, :], stg_c[:, k, :]) for k in range(8)], [t_mod] + stg_ct, [PB[g]])
                cp(DVE, modv[0:3, nt * 512:(nt + 1) * 512], ps[0:3, g, :], [PB[g]], modv_t)
            g = GB_.next()

            def tr(h, g=g):
                ins = None
                for j in range(48):
                    ins = h.transpose(ps[:, g, j * 3:j * 3 + 3], modv[0:3, j * 128:(j + 1) * 128], IDN[0:3, 0:3])
                return ins
            fw.op(PE, tr, modv_t + [t_cst], [PB[g]])
            cp(DVE, modT[:, l, :, :], ps[:, g, 0:144].rearrange("p (j v) -> p j v", v=3), [PB[g]], [t_mod])
            mbT = modb_ap.bitcast(F32)[:, 0:48]
            dma(SP, mbT, fm(modb_d[l]), writes=modb_t, allow_slow_non_contiguous=True)
            for v in range(3):
                tt(DVE, modT[:, l, :, v], modT[:, l, :, v], mbT, ALU.add, [t_mod] + modb_t, [t_mod])
            for n in range(2):
                for v in range(3):
                    stt(DVE, scv[:, l, n, :, v], modT[:, l, (8 + 24 * n):(16 + 24 * n), v], 1.0, gsm[:, 2 * n + l, :], ALU.add, ALU.mult,
                        [t_mod, t_small], [t_mod])

        def norm_mod(l, n, tiles, vb, gain_only=None, out_fn=None):
            for ti in tiles:
                c0, c1, isc = TILES[ti]
                N = c1 - c0
                v = 2 if isc else vb
                g = GB_.next()
                for k in range(8):
                    si = sq_i.next()
                    act(sq[:, si, 0:N], resid[:, k, c0:c1], AF.Square, [t_res[k][ti]], [t_sq[si]])
                    fw.op(PE, (lambda h, g=g, si=si, k=k, N=N: h.matmul(ps[:, g, 0:N], lhsT=onesm[:, 0, :], rhs=sq[:, si, 0:N], start=(k == 0), stop=(k == 7))),
                          [t_sq[si], t_small], [PB[g]], cost=N / 2400.0 + 0.005)
                ri = rstd_i.next()
                act(rstd[:, ri, 0:N], ps[:, g, 0:N], AF.Ln, [PB[g], t_small], [t_rstd[ri]], bias=epst[:, 0:1])
                act(rstd[:, ri, 0:N], rstd[:, ri, 0:N], AF.Exp, [t_rstd[ri]], [t_rstd[ri]], scale=-0.5)
                for k in range(8):
                    if out_fn is not None:
                        out_fn(ti, k, c0, c1, N, ri)
                        continue
                    mi = tmp_i.next()
                    stt(DVE, tmp32[:, mi, 0:N], resid[:, k, c0:c1], scv[:, l, n, k, v:v + 1], rstd[:, ri, 0:N], ALU.mult, ALU.mult,
                        [t_res[k][ti], t_mod, t_rstd[ri]], [t_tmp[mi]])
                    act(zbuf[:, k, c0:c1], tmp32[:, mi, 0:N], AF.Identity, [t_tmp[mi], t_mod], [t_z[ti]],
                        bias=modT[:, l, 24 * n + k, v:v + 1])

        def load_w(dst3, src2, nk, ncol, toks):
            key = dkey("w")
            fw.op(POOL, lambda h: [h.dma_start(out=dst3[:, k, :], in_=src2[k * 128:(k + 1) * 128, :]) for k in range(nk)],
                  writes=toks, dma_key=key, n_dma=nk)

        def gated_add(pb, mo, c0, c1, ti_list, l, gidx, v):
            N = c1 - c0
            stt(DVE, resid[:, mo, c0:c1], ps[:, pb, 0:N], modT[:, l, gidx + mo, v:v + 1], resid[:, mo, c0:c1], ALU.mult, ALU.add,
                [PB[pb], t_mod] + [t_res[mo][ti] for ti in ti_list], [t_res[mo][ti] for ti in ti_list])

        def rope_out(src32, mi, dst, N, tcol0, Cap, Sap, t_tab, prot, prange, rd_toks, wr_toks):
            p0, p1 = prange
            g = GB_.next()
            fw.op(PE, lambda h: h.matmul(ps[p0:p1, g, 0:N], lhsT=prot[p0:p1, p0:p1], rhs=src32[p0:p1, 0:N], start=True, stop=True),
                  [t_tmp[mi], t_cst], [PB[g]], cost=4 * N / 2400.0 + 0.005)
            m2 = tmp_i.next()
            tt(DVE, tmp32[p0:p1, m2, 0:N], ps[p0:p1, g, 0:N], Sap[p0:p1, tcol0:tcol0 + N], ALU.mult, [PB[g]] + t_tab, [t_tmp[m2]])
            tt(DVE, src32[p0:p1, 0:N], src32[p0:p1, 0:N], Cap[p0:p1, tcol0:tcol0 + N], ALU.mult, [t_tmp[mi]] + t_tab, [t_tmp[mi]])
            tt(POOL, dst, src32[p0:p1, 0:N], tmp32[p0:p1, m2, 0:N], ALU.add, [t_tmp[mi], t_tmp[m2]] + rd_toks, wr_toks)

        def attention(qtiles, nheads, K, q_ap, k_ap, v_ap, scale, den_mode, out_w, out_w_toks, nchunk_o, l, vb,
                      q_toks, k_toks, v_toks, window=False, sink=False, ot_view=None, pt_view=None, k_extra=()):
            OT, ot_t = ot_view
            PT, pt_t = pt_view
            pt_i = Rot(list(range(len(pt_t))))
            pt2_i = Rot([0, 2, 4])
            SP2 = Rot([0, 2])
            for ti in qtiles:
                c0, c1, isc = TILES[ti]
                N = c1 - c0
                v = 2 if isc else vb
                oi = (ti % 2)
                for h in range(nheads):
                    e = h % 2
                    ob = OB_.next()
                    if window:
                        tq = ti - 1
                        kcs = [(0, 0, N, None), (1, 0, N, None)]
                        for c in range(4 * tq - 1, 4 * tq + 5):
                            if c < 0 or c > 15:
                                continue
                            b0 = max(4 * tq, c - 1); b1 = min(4 * tq + 3, c + 1)
                            kcs.append((2 + c, (b0 - 4 * tq) * 128, (b1 - 4 * tq + 1) * 128, c))
                    else:
                        kcs = [(kc, 0, N, None) for kc in (range(2) if isc else range(18))]
                    nk = len(kcs)

                    def emit_s(idx, h=h, ti=ti, c0=c0, N=N):
                        kc, a0, a1, cblk = kcs[idx]
                        sb = SB_.next()
                        kt = 0 if kc < 2 else 1 + (kc - 2) // 4
                        fw.op(PE, (lambda h_, sb=sb, kc=kc, a0=a0, a1=a1, h=h, c0=c0: h_.matmul(ps[:, sb, a0:a1], lhsT=k_ap(h, kc), rhs=q_ap(h, c0 + a0, c0 + a1), start=True, stop=True)),
                              [q_toks[ti], k_toks[kt]] + list(k_extra), [PB[sb]], cost=(a1 - a0) / 2400.0 + 0.005)
                        pi = pt_i.next()
                        act(PT[:, pi, a0:a1], ps[:, sb, a0:a1], AF.Exp, [PB[sb]], [pt_t[pi]], scale=scale)
                        if cblk is not None:
                            tq = ti - 1
                            for qb in range(a0 // 128, a1 // 128):
                                qblk = 4 * tq + qb
                                if qblk == cblk:
                                    continue
                                mcol = 0 if qblk > cblk else 128
                                tt(POOL, PT[:, pi, qb * 128:(qb + 1) * 128], PT[:, pi, qb * 128:(qb + 1) * 128], msk[:, mcol:mcol + 128], ALU.mult,
                                   [pt_t[pi], t_cst], [pt_t[pi]])
                        return pi

                    def emit_s2(idx, h=h, ti=ti, c0=c0, N=N):
                        sb = SP2.next()
                        pi = pt2_i.next()
                        for j_ in range(2):
                            kc = kcs[idx + j_][0]
                            kt = 0 if kc < 2 else 1 + (kc - 2) // 4
                            fw.op(PE, (lambda h_, sb=sb, j_=j_, kc=kc, h=h, c0=c0, N=N: h_.matmul(ps[:, sb + j_, 0:N], lhsT=k_ap(h, kc), rhs=q_ap(h, c0, c0 + N), start=True, stop=True)),
                                  [q_toks[ti], k_toks[kt]] + list(k_extra), [PB[sb + j_]], cost=N / 2400.0 + 0.005)
                        act(PT[:, pi:pi + 2, 0:N], ps[:, sb:sb + 2, 0:N], AF.Exp, [PB[sb], PB[sb + 1]], [pt_t[pi], pt_t[pi + 1]], scale=scale)
                        return pi

                    def emit_pv(idx, pi, h=h, ob=ob, nk=nk):
                        kc, a0, a1, cblk = kcs[idx]
                        kt = 0 if kc < 2 else 1 + (kc - 2) // 4

                        def pv(h_, ob=ob, kc=kc, pi=pi, a0=a0, a1=a1, idx=idx, h=h, nk=nk):
                            if den_mode == "aug":
                                return h_.matmul(ps[:, ob, a0:a1], lhsT=v_ap(h, kc), rhs=PT[:, pi, a0:a1], start=(idx == 0), stop=(idx == nk - 1))
                            h_.matmul(ps[0:64, ob, a0:a1], lhsT=v_ap(h, kc), rhs=PT[:, pi, a0:a1], start=(idx == 0), stop=(idx == nk - 1))
                            return h_.matmul(ps[64:128, ob, a0:a1], lhsT=onesm[:, 4, 0:64], rhs=PT[:, pi, a0:a1], start=(idx == 0), stop=(idx == nk - 1))
                        fw.op(PE, pv, [pt_t[pi], v_toks[kt], t_small], [PB[ob]], cost=((a1 - a0) / 2400.0 + 0.005) * (1 if den_mode == 'aug' else 2))
                    if not window and PAIR_EXP:
                        assert nk % 2 == 0
                        np_ = nk // 2
                        pis = {0: emit_s2(0)}
                        for ip in range(np_):
                            if ip + 1 < np_:
                                pis[ip + 1] = emit_s2(2 * (ip + 1))
                            emit_pv(2 * ip, pis[ip])
                            emit_pv(2 * ip + 1, pis[ip] + 1)
                    else:
                        LA = 2
                        pis = {}
                        for idx in range(min(LA, nk)):
                            pis[idx] = emit_s(idx)
                        for idx in range(nk):
                            if idx + LA < nk:
                                pis[idx + LA] = emit_s(idx + LA)
                            emit_pv(idx, pis[idx])
                    if sink:
                        fw.op(DVE, lambda h_, ob=ob, h=h, N=N: h_.tensor_scalar(out=rd[64:128, 0:N], in0=ps[64:128, ob, 0:N], scalar1=esink[64:128, h:h + 1], scalar2=None, op0=ALU.add),
                              [PB[ob], t_small], [t_rd])
                        fw.op(DVE, lambda h_, N=N: h_.reciprocal(out=rd[0:64, 0:N], in_=rd[64:128, 0:N]), [t_rd], [t_rd])
                    else:
                        fw.op(DVE, lambda h_, ob=ob, N=N: h_.reciprocal(out=rd[0:64, 0:N], in_=ps[64:128, ob, 0:N]), [PB[ob]], [t_rd])
                    if e == 0:
                        tt(DVE, OT[0:64, oi, h // 2, 0:N], ps[0:64, ob, 0:N], rd[0:64, 0:N], ALU.mult, [PB[ob], t_rd], [ot_t[oi]])
                    else:
                        mi = tmp_i.next()
                        tt(DVE, tmp32[0:64, mi, 0:N], ps[0:64, ob, 0:N], rd[0:64, 0:N], ALU.mult, [PB[ob], t_rd], [t_tmp[mi]])
                        cp(ACT, OT[64:128, oi, h // 2, 0:N], tmp32[0:64, mi, 0:N], [t_tmp[mi]], [ot_t[oi]])
                for mo in range(8):
                    g = GB_.next()
                    mm(ps[:, g, 0:N], [(out_w(m)[:, mo * 128:(mo + 1) * 128], OT[:, oi, m, 0:N]) for m in range(nchunk_o)],
                       [ot_t[oi]] + out_w_toks, [PB[g]])
                    gated_add(g, mo, c0, c1, [ti], l, 16, v)

        for t_ in (t_small, t_cst, t_mod):
            t_.frozen = True
        for s in range(nseq):
            vb = s
            if s > 0:
                load_seq(s)
            if stage != 1:
                l = 0
                norm_mod(l, 0, range(5), vb)
                wA_ap, wA_t = WA.alloc(0, 6144, 1, "wA")
                wA = wA_ap.rearrange("p (k n) -> p k n", k=8)
                load_w(wA, evwin_d[0][:, 0:768], 8, 768, wA_t)
                tab_ap, tab_t = SC.alloc(0, 4096, 1, "tabA")
                dma(POOL, tab_ap[:, 0:2048], consts_d[:, C_C64:C_C64 + 2048], writes=tab_t)
                dma(POOL, tab_ap[:, 2048:4096], consts_d[:, C_S64:C_S64 + 2048], writes=tab_t)
                Ct = tab_ap[:, 0:2048]; St = tab_ap[:, 2048:4096]
                QA_ap, QA_t = AR.alloc(0, 9216, 5, "QA"); QA = QA_ap.rearrange("p (m t) -> p m t", m=4)
                KA_ap, KA_t = AR.alloc(9216, 4608, 5, "KA"); KA = KA_ap.rearrange("p (j t) -> p j t", j=2)
                VA_ap, VA_t = AR.alloc(13824, 4608, 5, "VA"); VA = VA_ap.rearrange("p (c j d) -> p c j d", c=18, j=2)
                fw.op(POOL, lambda h: h.memset(VA_ap, 1.0), writes=VA_t)

                def qk_chunk(ti, pairs_fn, gcol, dst, dst_toks, rope, ncost=8):
                    c0, c1, isc = TILES[ti]; N = c1 - c0
                    g = GB_.next()
                    fw.op(PE, lambda h: pairs_fn(h, g, c0, c1, N), [t_z[ti]] + wA_t, [PB[g]], cost=ncost * (N / 2400.0 + 0.005))
                    mi = tmp_i.next()
                    if gcol is not None:
                        si = sq_i.next()
                        act(sq[:, si, 0:N], ps[:, g, 0:N], AF.Square, [PB[g]], [t_sq[si]])
                        g2 = GB_.next()
                        mm(ps[:, g2, 0:N], [(onesm[:, 1, :], sq[:, si, 0:N])], [t_sq[si], t_small], [PB[g2]])
                        ri = rstd_i.next()
                        act(rstd[:, ri, 0:N], ps[:, g2, 0:N], AF.Ln, [PB[g2], t_small], [t_rstd[ri]], bias=epst[:, 0:1])
                        act(rstd[:, ri, 0:N], rstd[:, ri, 0:N], AF.Exp, [t_rstd[ri]], [t_rstd[ri]], scale=-0.5)
                        stt(DVE, tmp32[:, mi, 0:N], ps[:, g, 0:N], gq[:, gcol:gcol + 1], rstd[:, ri, 0:N], ALU.mult, ALU.mult,
                            [PB[g], t_small, t_rstd[ri]], [t_tmp[mi]])
                    else:
                        cp(ACT, tmp32[:, mi, 0:N], ps[:, g, 0:N], [PB[g]], [t_tmp[mi]])
                    if rope and not isc:
                        rope_out(tmp32[:, mi, :], mi, dst, N, c0 - 256, Ct, St, tab_t, P64, (0, 128), [], dst_toks)
                    else:
                        cp(DVE, dst, tmp32[:, mi, 0:N], [t_tmp[mi]], dst_toks)

                def proj_qkv(w3, wtoks, tiles, qnorm, Q, Q_t, Kd, K_t, V, V_t, qtiles):
                    for ti in tiles:
                        c0, c1, isc = TILES[ti]; N = c1 - c0
                        if ti in qtiles:
                            for m in range(4):
                                def pf(h, g, c0, c1, N, m=m):
                                    ins = None
                                    for k in range(8):
                                        ins = h.matmul(ps[:, g, 0:N], lhsT=w3[:, k, m * 128:(m + 1) * 128], rhs=zbuf[:, k, c0:c1], start=(k == 0), stop=(k == 7))
                                    return ins
                                qk_chunk(ti, pf, 0 if qnorm else None, Q[:, m, c0:c1], [Q_t[ti]], True)
                        for j in range(2):
                            def pf(h, g, c0, c1, N, j=j):
                                ins = None
                                for k in range(8):
                                    h.matmul(ps[0:64, g, 0:N], lhsT=w3[:, k, 512 + 64 * j:576 + 64 * j], rhs=zbuf[:, k, c0:c1], start=(k == 0), stop=(k == 7))
                                    ins = h.matmul(ps[64:128, g, 0:N], lhsT=w3[:, k, 512 + 64 * j:576 + 64 * j], rhs=zbuf[:, k, c0:c1], start=(k == 0), stop=(k == 7))
                                return ins
                            qk_chunk(ti, pf, 1 if qnorm else None, Kd[:, j, c0:c1], [K_t[ti]], True, ncost=16)
                        for sc_ in range(N // 128):
                            g = GB_.next()
                            cc0 = c0 + sc_ * 128
                            mm(ps[:, g, 0:128], [(zbuf[:, k, cc0:cc0 + 128], w3[:, k, 640:768]) for k in range(8)], [t_z[ti]] + wtoks, [PB[g]])
                            cp(ACT, V[:, cc0 // 128, :, 0:64], ps[:, g, 0:128].rearrange("p (j d) -> p j d", j=2), [PB[g]], [V_t[ti]])


                def pad_k(KA_ap, KA_t):
                    kz_ap, kz_t = WA.alloc(4608, 4608, 1, "KZ1")
                    fw.op(POOL, lambda h: h.tensor_copy(out=kz_ap[64:128, :], in_=KA_ap[64:128, :]), KA_t, kz_t, cost=10.0)
                    fw.op(DVE, lambda h: h.memset(kz_ap[0:64, :], 0.0), (), kz_t, cost=3.0)
                    fw.op(DVE, lambda h: h.memset(KA_ap[64:128, :], 0.0), (), KA_t, cost=3.0)
                    return kz_ap.rearrange("p (j t) -> p j t", j=2), kz_t
                proj_qkv(wA, wA_t, range(5), True, QA, QA_t, KA, KA_t, VA, VA_t, range(5))
                KZ, KZ_t = pad_k(KA_ap, KA_t)
                woA_ap, woA_t = WA.alloc(0, 4096, 1, "woA")
                woA = woA_ap.rearrange("p (k n) -> p k n", k=4)
                load_w(woA, evwout_d[0][0:512, :], 4, 1024, woA_t)
                OT_ap, OT_t = SC.alloc(0, 4096, 2, "OT"); OT = OT_ap.rearrange("p (o m n) -> p o m n", o=2, m=4)
                PT_ap, PT_t = SC.alloc(4096, 3072, 6, "PT"); PT = PT_ap.rearrange("p (i n) -> p i n", i=6)
                attention(range(5), 8, 64,
                          lambda h, a, b: QA[:, h // 2, a:b],
                          lambda h, kc: (KA if h % 2 == 0 else KZ)[:, h // 4, kc * 128:(kc + 1) * 128],
                          lambda h, kc: VA[:, kc, h // 4, :], 64 ** -0.5, "aug",
                          lambda m: woA[:, m, :], woA_t, 4, l, vb, QA_t, KA_t, VA_t, ot_view=(OT, OT_t), pt_view=(PT, PT_t), k_extra=KZ_t)

                if stage != 2:
                    wB_ap, wB_t = WA.alloc(4608, 3328, 1, "wB"); wB = wB_ap.rearrange("p (k n) -> p k n", k=8)
                    load_w(wB, evwin_d[0][:, 768:1184], 8, 416, wB_t)
                    wq_ap, wq_t = WA.alloc(0, 1536, 1, "wq"); wq = wq_ap.rearrange("p (k n) -> p k n", k=2)
                    load_w(wq, evwqup_d[0], 2, 768, wq_t)
                    wkv_ap, wkv_t = WA.alloc(1536, 1024, 1, "wkv"); wkv = wkv_ap.rearrange("p (k n) -> p k n", k=1)
                    load_w(wkv, evwkv_d[0], 1, 1024, wkv_t)
                    tab_ap, tab_t = SC.alloc(0, 4096, 1, "tabR")
                    dma(POOL, tab_ap[:, 0:2048], consts_d[:, C_CR:C_CR + 2048], writes=tab_t)
                    dma(POOL, tab_ap[:, 2048:4096], consts_d[:, C_SR:C_SR + 2048], writes=tab_t)
                    Cr = tab_ap[:, 0:2048]; Sr = tab_ap[:, 2048:4096]
                    cqn_ap, cqn_t = AR.alloc(0, 4608, 5, "cqn"); cqn = cqn_ap.rearrange("p (m t) -> p m t", m=2)
                    ckv_ap, ckv_t = AR.alloc(4608, 2304, 5, "ckv")
                    KBs = []
                    for bi_ in range(2):
                        kb_ap, kb_t = AR.alloc(6912 + 4608 * bi_, 4608, 5, f"KB{bi_}")
                        fw.op(POOL, lambda h, kb_ap=kb_ap: h.memset(kb_ap[96:128, :], 0.0), writes=kb_t, cost=3.0)
                        KBs.append((kb_ap.rearrange("p (e t) -> p e t", e=2), kb_t))
                    for ti in range(5):
                        c0, c1, isc = TILES[ti]; N = c1 - c0
                        gs = []
                        si_l = []
                        for m in range(2):
                            g = GB_.next(); gs.append(g)
                            mm(ps[:, g, 0:N], [(wB[:, k, m * 128:(m + 1) * 128], zbuf[:, k, c0:c1]) for k in range(8)], [t_z[ti]] + wB_t, [PB[g]])
                        g2 = GB_.next()
                        for m in range(2):
                            si = sq_i.next()
                            act(sq[:, si, 0:N], ps[:, gs[m], 0:N], AF.Square, [PB[gs[m]]], [t_sq[si]])
                            fw.op(PE, (lambda h, g2=g2, si=si, m=m, N=N: h.matmul(ps[:, g2, 0:N], lhsT=onesm[:, 2, :], rhs=sq[:, si, 0:N], start=(m == 0), stop=(m == 1))),
                                  [t_sq[si], t_small], [PB[g2]])
                        ri = rstd_i.next()
                        act(rstd[:, ri, 0:N], ps[:, g2, 0:N], AF.Ln, [PB[g2], t_small], [t_rstd[ri]], bias=epst[:, 0:1])
                        act(rstd[:, ri, 0:N], rstd[:, ri, 0:N], AF.Exp, [t_rstd[ri]], [t_rstd[ri]], scale=-0.5)
                        for m in range(2):
                            stt(DVE, cqn[:, m, c0:c1], ps[:, gs[m], 0:N], gql[:, m:m + 1], rstd[:, ri, 0:N], ALU.mult, ALU.mult,
                                [PB[gs[m]], t_small, t_rstd[ri]], [cqn_t[ti]])
                        g = GB_.next()
                        mm(ps[:, g, 0:N], [(wB[:, k, 256:384], zbuf[:, k, c0:c1]) for k in range(8)], [t_z[ti]] + wB_t, [PB[g]])
                        si = sq_i.next()
                        act(sq[:, si, 0:N], ps[:, g, 0:N], AF.Square, [PB[g]], [t_sq[si]])
                        g2 = GB_.next()
                        mm(ps[:, g2, 0:N], [(onesm[:, 3, :], sq[:, si, 0:N])], [t_sq[si], t_small], [PB[g2]])
                        ri = rstd_i.next()
                        act(rstd[:, ri, 0:N], ps[:, g2, 0:N], AF.Ln, [PB[g2], t_small], [t_rstd[ri]], bias=epst[:, 0:1])
                        act(rstd[:, ri, 0:N], rstd[:, ri, 0:N], AF.Exp, [t_rstd[ri]], [t_rstd[ri]], scale=-0.5)
                        stt(DVE, ckv_ap[:, c0:c1], ps[:, g, 0:N], gq[:, 2:3], rstd[:, ri, 0:N], ALU.mult, ALU.mult,
                            [PB[g], t_small, t_rstd[ri]], [ckv_t[ti]])
                        g = GB_.next()
                        mm(ps[64:96, g, 0:N], [(wB[:, k, 384:416], zbuf[:, k, c0:c1]) for k in range(8)], [t_z[ti]] + wB_t, [PB[g]])
                        kr0 = KBs[0][0][64:96, 0, c0:c1]
                        if isc:
                            cp(ACT, kr0, ps[64:96, g, 0:N], [PB[g]], [KBs[0][1][ti]])
                        else:
                            mi = tmp_i.next()
                            cp(ACT, tmp32[64:96, mi, 0:N], ps[64:96, g, 0:N], [PB[g]], [t_tmp[mi]])
                            rope_out(tmp32[:, mi, :], mi, kr0, N, c0 - 256, Cr, Sr, tab_t, PR, (64, 96), [], [KBs[0][1][ti]])
                        cp(POOL, KBs[0][0][64:96, 1, c0:c1], kr0, [KBs[0][1][ti]], [KBs[0][1][ti]])
                        for e_ in range(2):
                            cp(POOL, KBs[1][0][64:96, e_, c0:c1], kr0, [KBs[0][1][ti]], [KBs[1][1][ti]])
                    QB_ap, QB_t = AR.alloc(16128, 4608, 5, "QB"); QB = QB_ap.rearrange("p (e t) -> p e t", e=2)
                    VB_ap, VB_t = WA.alloc(4608, 4608, 5, "VBa"); VB = VB_ap.rearrange("p (c e d) -> p c e d", c=18, e=2)
                    fw.op(POOL, lambda h, VB_ap=VB_ap: h.memset(VB_ap, 1.0), writes=VB_t, cost=10.0)
                    fw.op(POOL, lambda h, QB_ap=QB_ap: h.memset(QB_ap[96:128, :], 0.0), writes=QB_t, cost=3.0)
                    woB_ap, woB_t = WA.alloc(2560, 2048, 2, "woB"); woB = woB_ap.rearrange("p (i n) -> p i n", i=2)
                    OT_ap, OT_t = SC.alloc(4096, 1024, 2, "OTb"); OTb = OT_ap.rearrange("p (o m n) -> p o m n", o=2, m=1)
                    PT_ap, PT_t = SC.alloc(5120, 3072, 6, "PTb"); PTb = PT_ap.rearrange("p (i n) -> p i n", i=6)
                    for sp in range(4):
                        wi = sp % 2
                        KB, KB_t = KBs[sp % 2]
                        fw.op(POOL, lambda h, sp=sp, wi=wi: [h.dma_start(out=woB[:, wi, :], in_=evwout_d[0][512 + 128 * sp:512 + 128 * (sp + 1), :])],
                              writes=[woB_t[wi]], dma_key=dkey("w"))
                        for ti in range(5):
                            c0, c1, isc = TILES[ti]; N = c1 - c0
                            for e in range(2):
                                hh = 2 * sp + e
                                g = GB_.next()
                                mm(ps[0:96, g, 0:N], [(wq[:, k, hh * 96:(hh + 1) * 96], cqn[:, k, c0:c1]) for k in range(2)], [cqn_t[ti]] + wq_t, [PB[g]])
                                cp(ACT, QB[0:64, e, c0:c1], ps[0:64, g, 0:N], [PB[g]], [QB_t[ti]])
                                if isc:
                                    cp(DVE, QB[64:96, e, c0:c1], ps[64:96, g, 0:N], [PB[g]], [QB_t[ti]])
                                else:
                                    mi = tmp_i.next()
                                    cp(DVE, tmp32[64:96, mi, 0:N], ps[64:96, g, 0:N], [PB[g]], [t_tmp[mi]])
                                    rope_out(tmp32[:, mi, :], mi, QB[64:96, e, c0:c1], N, c0 - 256, Cr, Sr, tab_t, PR, (64, 96), [], [QB_t[ti]])
                                g = GB_.next()
                                mm(ps[0:64, g, 0:N], [(wkv[:, 0, hh * 128:hh * 128 + 64], ckv_ap[:, c0:c1])], [ckv_t[ti]] + wkv_t, [PB[g]])
                                cp(ACT, KB[0:64, e, c0:c1], ps[0:64, g, 0:N], [PB[g]], [KB_t[ti]])
                            for sc_ in range(N // 128):
                                g = GB_.next()
                                cc0 = c0 + sc_ * 128
                                mm(ps[:, g, 0:256], [(ckv_ap[:, cc0:cc0 + 128], wkv[:, 0, sp * 256:(sp + 1) * 256])], [ckv_t[ti]] + wkv_t, [PB[g]])
                                cp(DVE, VB[:, cc0 // 128, :, 0:64], ps[:, g, 0:256].rearrange("p (e x d) -> p e x d", e=2, x=2)[:, :, 1, :], [PB[g]], [VB_t[ti]])
                        attention(range(5), 2, 96,
                                  lambda h, a, b: QB[:, h, a:b],
                                  lambda h, kc, KB=KB: KB[:, h, kc * 128:(kc + 1) * 128],
                                  lambda h, kc: VB[:, kc, h, :], 96 ** -0.5, "aug",
                                  lambda m, wi=wi: woB[:, wi, :], [woB_t[wi]], 1, l, vb, QB_t, KB_t, VB_t, ot_view=(OTb, OT_t), pt_view=(PTb, PT_t))

            def ffn(l, tiles, vb):
                norm_mod(l, 1, tiles, vb)
                GB_ = Rot([0, 1, 2, 3, 4, 5, 6, 7])
                wins = []
                if 0 in tiles:
                    wins.append((0, 256, 0, 256, 2, [0]))
                b = [0, 410, 820, 1230, 1639, 2048]
                for i in range(5):
                    h0 = max(b[i] - 1, 0) + 256; h1 = min(b[i + 1] + 1, 2048) + 256
                    tl = sorted(set([1 + (c - 256) // 512 for c in (b[i] + 256, b[i + 1] + 255)]))
                    wins.append((h0, h1, b[i] + 256, b[i + 1] + 256, vb, tl))
                passes = [list(range(0, 6)), list(range(6, 12)), list(range(12, 17)), list(range(17, 22))]
                act_ap, act_t = AR.alloc(0, 6 * T, 6, "ffact"); actb = act_ap.rearrange("p (c t) -> p c t", c=6)
                dn_ap, dn_t = AR.alloc(6 * T, 6144, 1, "ffdn"); dn = dn_ap.rearrange("p (c n) -> p c n", c=6)
                ub = []
                ub.append(WA.alloc(0, 4096, 1, "ffu0")); ub.append(WA.alloc(4096, 4096, 1, "ffu1"))
                cv_ap, cv_t = SC.alloc(0, 8192, 8, "ffcv"); cvt = cv_ap.bitcast(F32).rearrange("p (i n) -> p i n", i=8)
                cv_i = Rot([0, 1, 2]); sl_i = Rot([6, 7]); blk_i = 0
                pend = []

                def flush():
                    while pend:
                        (ta, xa, tg, xg, ci, o0, o1, No) = pend.pop(0)
                        xs = sl_i.next()
                        act(cvt[:, xs, 0:No], tg[:, 0:No], AF.Silu, [cv_t[xg]], [cv_t[xs]])
                        tt(POOL, actb[:, ci, o0:o1], ta[:, 0:No], cvt[:, xs, 0:No], ALU.mult, [cv_t[xa], cv_t[xs]], [act_t[ci]])
                for pl in passes:
                    npc = len(pl)
                    fw.op(POOL, lambda h, pl=pl, npc=npc: [h.dma_start(out=dn[:, i, :], in_=fdn_d[l][pl[i] * 128:(pl[i] + 1) * 128, :]) for i in range(npc)],
                          writes=dn_t, dma_key=dkey("w"), n_dma=npc)
                    for bi in range(0, npc, 2):
                        pcs = pl[bi:bi + 2]
                        u_ap, u_t = ub[blk_i % 2]; blk_i += 1
                        u3 = u_ap.rearrange("p (k n) -> p k n", k=8)
                        npb = len(pcs)
                        def ld(h, pcs=pcs, npb=npb, u3=u3):
                            r = []
                            for k in range(8):
                                r.append(h.dma_start(out=u3[:, k, 0:128 * npb], in_=fup_d[l][k * 128:(k + 1) * 128, pcs[0] * 128:(pcs[0] + npb) * 128]))
                                r.append(h.dma_start(out=u3[:, k, 256:256 + 128 * npb], in_=fup_d[l][k * 128:(k + 1) * 128, DFF + pcs[0] * 128:DFF + (pcs[0] + npb) * 128]))
                            return r
                        fw.op(POOL, ld, writes=u_t, dma_key=dkey("w"), n_dma=16)
                        for j, c in enumerate(pcs):
                            ci = pl.index(c)
                            for (h0, h1, o0, o1, v, tl) in wins:
                                Nh = h1 - h0; No = o1 - o0; d = o0 - h0
                                seq0 = 0 if o0 < 256 else 256; seq1 = 256 if o0 < 256 else T
                                res = []
                                for half in range(2):
                                    g = GB_.next()
                                    col = half * 256 + j * 128
                                    mm(ps[:, g, 0:Nh], [(u3[:, k, col:col + 128], zbuf[:, k, h0:h1]) for k in range(8)], [t_z[t] for t in tl] + u_t, [PB[g]])
                                    fi = half * NFF + c
                                    xi = (cv_i.next() if half == 0 else xi_a) + 3 * half
                                    xi_a = xi
                                    tb = cvt[:, xi, :]
                                    act(tb[:, 0:No], ps[:, g, d:d + No], AF.Identity, [PB[g], t_small], [cv_t[xi]],
                                        scale=cvw[:, l, 1, fi:fi + 1], bias=cvw[:, l, 3, fi:fi + 1])
                                    lo = 1 if o0 == seq0 else 0
                                    stt(DVE, tb[:, lo:No], ps[:, g, d - 1 + lo:d - 1 + No], cvw[:, l, 0, fi:fi + 1], tb[:, lo:No], ALU.mult, ALU.add,
                                        [PB[g], t_small, cv_t[xi]], [cv_t[xi]])
                                    hi = No - 1 if o1 == seq1 else No
                                    stt(DVE, tb[:, 0:hi], ps[:, g, d + 1:d + 1 + hi], cvw[:, l, 2, fi:fi + 1], tb[:, 0:hi], ALU.mult, ALU.add,
                                        [PB[g], t_small, cv_t[xi]], [cv_t[xi]])
                                    res.append((tb, xi))
                                (ta, xa), (tg, xg) = res
                                flush()
                                pend.append((ta, xa, tg, xg, ci, o0, o1, No))
                    flush()
                    for (h0, h1, o0, o1, v, tl) in wins:
                        No = o1 - o0
                        for mo in range(8):
                            g = GB_.next()
                            mm(ps[:, g, 0:No], [(dn[:, i, mo * 128:(mo + 1) * 128], actb[:, i, o0:o1]) for i in range(npc)],
                               [act_t[i] for i in range(npc)] + dn_t, [PB[g]])
                            gated_add(g, mo, o0, o1, tl, l, 40, v)

            if stage not in (1, 2, 3):
                ffn(0, range(5), vb)

            if stage == 0 or stage >= 5:
                l = 1
                norm_mod(l, 0, range(5), vb)
                wD_ap, wD_t = WA.alloc(0, 6144, 1, "wD")
                wD = wD_ap.rearrange("p (k n) -> p k n", k=8)
                load_w(wD, odwin_d[0][:, 1024:1792], 8, 768, wD_t)
                tab_ap, tab_t = SC.alloc(0, 4096, 1, "tabA")
                dma(POOL, tab_ap[:, 0:2048], consts_d[:, C_C64:C_C64 + 2048], writes=tab_t)
                dma(POOL, tab_ap[:, 2048:4096], consts_d[:, C_S64:C_S64 + 2048], writes=tab_t)
                Ct = tab_ap[:, 0:2048]; St = tab_ap[:, 2048:4096]
                wA_t = wD_t
                QA_ap, QA_t = AR.alloc(0, 9216, 5, "QD"); QA = QA_ap.rearrange("p (m t) -> p m t", m=4)
                KA_ap, KA_t = AR.alloc(9216, 4608, 5, "KD"); KA = KA_ap.rearrange("p (j t) -> p j t", j=2)
                VA_ap, VA_t = AR.alloc(13824, 4608, 5, "VD"); VA = VA_ap.rearrange("p (c j d) -> p c j d", c=18, j=2)
                fw.op(POOL, lambda h: h.memset(VA_ap, 1.0), writes=VA_t)
                proj_qkv(wD, wD_t, range(5), False, QA, QA_t, KA, KA_t, VA, VA_t, range(1, 5))
                KZ, KZ_t = pad_k(KA_ap, KA_t)
                woA_ap, woA_t = WA.alloc(0, 4096, 1, "woD")
                woA = woA_ap.rearrange("p (k n) -> p k n", k=4)
                load_w(woA, odwout_d[0][512:1024, :], 4, 1024, woA_t)
                OT_ap, OT_t = SC.alloc(0, 4096, 2, "OT"); OT = OT_ap.rearrange("p (o m n) -> p o m n", o=2, m=4)
                PT_ap, PT_t = SC.alloc(4096, 3072, 6, "PT"); PT = PT_ap.rearrange("p (i n) -> p i n", i=6)
                attention(range(1, 5), 8, 64,
                          lambda h, a, b: QA[:, h // 2, a:b],
                          lambda h, kc: (KA if h % 2 == 0 else KZ)[:, h // 4, kc * 128:(kc + 1) * 128],
                          lambda h, kc: VA[:, kc, h // 4, :], 64 ** -0.5, "aug",
                          lambda m: woA[:, m, :], woA_t, 4, l, vb, QA_t, KA_t, VA_t, window=True, sink=True,
                          ot_view=(OT, OT_t), pt_view=(PT, PT_t), k_extra=KZ_t)
                if stage != 5:
                    wC_ap, wC_t = WA.alloc(0, 8192, 1, "wC")
                    wC = wC_ap.rearrange("p (k n) -> p k n", k=8)
                    load_w(wC, odwin_d[0][:, 0:1024], 8, 1024, wC_t)
                    ug_ap, ug_t = AR.alloc(0, 8192, 4, "ug"); ug = ug_ap.rearrange("p (m t) -> p m t", m=4)
                    vl_ap, vl_t = AR.alloc(8192, 8192, 16, "vln"); vln = vl_ap.rearrange("p (n c) -> p n c", n=16)
                    wst_ap, wst_t = AR.alloc(16384, 1024, 1, "wst"); wst = wst_ap.rearrange("p (g q) -> p g q", g=8)
                    bt_ap, bt_t = AR.alloc(17408, 1024, 1, "btab"); btab = bt_ap.bitcast(F32).rearrange("p (c q) -> p c q", c=4)
                    lg_ap, lg_t = AR.alloc(18432, 2048, 1, "lgb"); lgb = lg_ap.bitcast(F32).rearrange("p (i c) -> p i c", i=2)
                    st_ap, st_t = SC.alloc(0, 2048, 2, "sgstg")
                    dma(SP, lgb[:, 0, :], odlng_d[0].partition_broadcast(128), writes=lg_t)
                    dma(SP, lgb[:, 1, :], odlnb_d[0].partition_broadcast(128), writes=lg_t)
                    for g_ in range(8):
                        dma(SP, btab[64 * (g_ % 2):64 * (g_ % 2) + 64, g_ // 2, :], odsb_d[0, g_].partition_broadcast(64), writes=bt_t)
                        wstg = st_ap[:, (g_ % 2) * 1024:(g_ % 2) * 1024 + 256].bitcast(F32)
                        dma(SP, wstg, odsw_d[0, g_], writes=[st_t[g_ % 2]])
                        g = GB_.next()
                        fw.op(PE, lambda h, g=g, wstg=wstg: h.transpose(ps[:, g, 0:128], wstg, IDN), [st_t[g_ % 2], t_cst], [PB[g]])
                        cp(ACT, wst[:, g_, :], ps[:, g, 0:128], [PB[g]], wst_t)
                    stat = rd
                    for ti in range(1, 5):
                        c0, c1, isc = TILES[ti]; N = c1 - c0
                        for m in range(4):
                            g = GB_.next()
                            mm(ps[:, g, 0:N], [(wC[:, k, m * 128:(m + 1) * 128], zbuf[:, k, c0:c1]) for k in range(8)], [t_z[ti]] + wC_t, [PB[g]])
                            act(ug[:, m, c0 - 256:c1 - 256], ps[:, g, 0:N], AF.Gelu, [PB[g]], [ug_t[ti - 1]])
                        for sc_ in range(4):
                            n = (ti - 1) * 4 + sc_
                            cc0 = c0 + sc_ * 128
                            g = GB_.next()
                            mm(ps[:, g, :], [(zbuf[:, k, cc0:cc0 + 128], wC[:, k, 512:1024]) for k in range(8)], [t_z[ti]] + wC_t, [PB[g]])
                            mi = tmp_i.next(); m2 = tmp_i.next()
                            act(tmp32[:, mi, :], ps[:, g, :], AF.Gelu, [PB[g]], [t_tmp[mi], t_rd], accum_out=stat[:, 0:1])
                            act(tmp32[:, m2, :], tmp32[:, mi, :], AF.Square, [t_tmp[mi]], [t_tmp[m2], t_rd], accum_out=stat[:, 1:2])

                            def stats(h):
                                h.tensor_scalar(out=stat[:, 2:3], in0=stat[:, 0:1], scalar1=1.0 / 512, scalar2=None, op0=ALU.mult)
                                h.tensor_tensor(out=stat[:, 3:4], in0=stat[:, 2:3], in1=stat[:, 2:3], op=ALU.mult)
                                return h.scalar_tensor_tensor(out=stat[:, 4:5], in0=stat[:, 1:2], scalar=1.0 / 512, in1=stat[:, 3:4], op0=ALU.mult, op1=ALU.subtract)
                            fw.op(DVE, lambda h: h.tensor_scalar(out=stat[:, 2:3], in0=stat[:, 0:1], scalar1=1.0 / 512, scalar2=None, op0=ALU.mult), [t_rd], [t_rd])
                            fw.op(DVE, lambda h: h.tensor_tensor(out=stat[:, 3:4], in0=stat[:, 2:3], in1=stat[:, 2:3], op=ALU.mult), [t_rd], [t_rd])
                            fw.op(DVE, lambda h: h.scalar_tensor_tensor(out=stat[:, 4:5], in0=stat[:, 1:2], scalar=1.0 / 512, in1=stat[:, 3:4], op0=ALU.mult, op1=ALU.subtract), [t_rd], [t_rd])
                            act(stat[:, 5:6], stat[:, 4:5], AF.Ln, [t_rd, t_small], [t_rd], bias=epst[:, 0:1])
                            act(stat[:, 5:6], stat[:, 5:6], AF.Exp, [t_rd], [t_rd], scale=-0.5)
                            fw.op(DVE, lambda h, mi=mi: h.tensor_scalar(out=tmp32[:, mi, :], in0=tmp32[:, mi, :], scalar1=stat[:, 2:3], scalar2=stat[:, 5:6], op0=ALU.subtract, op1=ALU.mult),
                                  [t_tmp[mi], t_rd], [t_tmp[mi]])
                            tt(DVE, tmp32[:, mi, :], tmp32[:, mi, :], lgb[:, 0, :], ALU.mult, [t_tmp[mi]] + lg_t, [t_tmp[mi]])
                            tt(POOL, vln[:, n, :], tmp32[:, mi, :], lgb[:, 1, :], ALU.add, [t_tmp[mi]] + lg_t, [vl_t[n]])
                        for cc in range(4):
                            g = GB_.next()

                            def sp_(h, g=g, cc=cc, ti=ti):
                                ins = None
                                for e in range(2):
                                    for sc_ in range(4):
                                        n = (ti - 1) * 4 + sc_
                                        gg = 2 * cc + e
                                        ins = h.matmul(ps[64 * e:64 * e + 64, g, sc_ * 128:(sc_ + 1) * 128], lhsT=vln[:, n, gg * 64:(gg + 1) * 64], rhs=wst[:, gg, :], start=True, stop=True)
                                return ins
                            fw.op(PE, sp_, [vl_t[(ti - 1) * 4 + i] for i in range(4)] + wst_t, [PB[g]], cost=8 * 0.1)
                            mi = tmp_i.next()
                            for sc_ in range(4):
                                tt(DVE, tmp32[:, mi, sc_ * 128:(sc_ + 1) * 128], ps[:, g, sc_ * 128:(sc_ + 1) * 128], btab[:, cc, :], ALU.add, [PB[g]] + bt_t, [t_tmp[mi]])
                            tt(POOL, ug[:, cc, c0 - 256:c1 - 256], ug[:, cc, c0 - 256:c1 - 256], tmp32[:, mi, :], ALU.mult, [t_tmp[mi], ug_t[ti - 1]], [ug_t[ti - 1]])
                    woC_ap, woC_t = WA.alloc(0, 4096, 1, "woC")
                    woC = woC_ap.rearrange("p (k n) -> p k n", k=4)
                    load_w(woC, odwout_d[0][0:512, :], 4, 1024, woC_t)
                    for ti in range(1, 5):
                        c0, c1, isc = TILES[ti]; N = c1 - c0
                        for mo in range(8):
                            g = GB_.next()
                            mm(ps[:, g, 0:N], [(woC[:, m, mo * 128:(mo + 1) * 128], ug[:, m, c0 - 256:c1 - 256]) for m in range(4)], [ug_t[ti - 1]] + woC_t, [PB[g]])
                            gated_add(g, mo, c0, c1, [ti], l, 16, vb)
                    if stage != 6:
                        ffn(1, range(1, 5), vb)

            ost_ap, ost_t = SC.alloc(0, 4096, 2, "ostg")
            fin_ap, fin_t = SC.alloc(4096, 4096, 1, "fin")
            for ti in (range(1, 5) if 'f' not in SKIP else []):
                c0, c1, isc = TILES[ti]; N = c1 - c0
                if stage == 0:
                    g = GB_.next()
                    for k in range(8):
                        si = sq_i.next()
                        act(sq[:, si, 0:N], resid[:, k, c0:c1], AF.Square, [t_res[k][ti]], [t_sq[si]])
                        fw.op(PE, (lambda h, g=g, si=si, k=k, N=N: h.matmul(ps[:, g, 0:N], lhsT=onesm[:, 0, :], rhs=sq[:, si, 0:N], start=(k == 0), stop=(k == 7))),
                              [t_sq[si], t_small], [PB[g]])
                    ri = rstd_i.next()
                    act(rstd[:, ri, 0:N], ps[:, g, 0:N], AF.Ln, [PB[g], t_small], [t_rstd[ri]], bias=epst[:, 0:1])
                    act(rstd[:, ri, 0:N], rstd[:, ri, 0:N], AF.Exp, [t_rstd[ri]], [t_rstd[ri]], scale=-0.5)
                    for k in range(8):
                        stt(DVE, resid[:, k, c0:c1], resid[:, k, c0:c1], gsm[:, 4, k:k + 1], rstd[:, ri, 0:N], ALU.mult, ALU.mult,
                            [t_res[k][ti], t_small, t_rstd[ri]], [t_res[k][ti]])
                for sc_ in range(4):
                    cc0 = c0 + sc_ * 128
                    bi = sc_ % 2
                    obuf = ost_ap[:, bi * 2048:(bi + 1) * 2048].bitcast(F32)
                    for hk in range(2):
                        g = GB_.next()

                        def trb(h, g=g, hk=hk, cc0=cc0):
                            ins = None
                            for kk in range(4):
                                k = hk * 4 + kk
                                ins = h.transpose(ps[:, g, kk * 128:(kk + 1) * 128], resid[:, k, cc0:cc0 + 128], IDN)
                            return ins
                        fw.op(PE, trb, [t_res[k][ti] for k in range(hk * 4, hk * 4 + 4)] + [t_cst], [PB[g]], cost=1.2)
                        cp(ACT if hk == 0 else DVE, obuf[:, hk * 512:(hk + 1) * 512], ps[:, g, :], [PB[g]], [ost_t[bi]])
                    dma(SP, out_d[s, cc0 - 256:cc0 - 128, :], obuf, reads=[ost_t[bi]], writes=[Tok("o")], key=f"o{bi}")
        fw.emit(st)
    return nc, fw


_CACHE = {}


def kernel(**inputs):
    nseq = int(os.environ.get("K_NSEQ", "2"))
    stage = int(os.environ.get("K_STAGE", "0"))
    key = (nseq, stage)
    if key not in _CACHE:
        _CACHE[key] = build_nc(nseq, stage)
    nc, fw = _CACHE[key]
    consts = make_consts()
    f = lambda a: np.ascontiguousarray(np.asarray(a, dtype=np.float32))
    shared = {k: f(v) for k, v in inputs.items() if k not in ("x", "c", "ctx")}
    shared["consts"] = consts
    x = f(inputs["x"]); c = f(inputs["c"]); ctx = f(inputs["ctx"])
    in_maps = []
    for i in range(8):
        m = dict(shared)
        m["x"] = x[2 * i:2 * i + 2]
        m["c"] = c[2 * i:2 * i + 2]
        m["ctx"] = ctx[2 * i:2 * i + 2]
        in_maps.append(m)
    res = run_bass_kernel_spmd(nc, in_maps, core_ids=list(range(8)))
    out = np.concatenate([np.asarray(r["out"], dtype=np.float32) for r in res.results], axis=0)
    return out
```

```python
import os
import numpy as np
from contextlib import ExitStack
import concourse.bass as bass
import concourse.mybir as mybir
from concourse.bass_utils import run_bass_kernel_spmd

F32 = mybir.dt.float32
BF16 = mybir.dt.bfloat16
AF = mybir.ActivationFunctionType
ALU = mybir.AluOpType
PE, ACT, DVE, POOL, SP = "pe", "act", "dve", "pool", "sp"
ENGS = (PE, ACT, DVE, POOL, SP)
SEM_LIM = 30000
STRICT = os.environ.get('K_STRICT', '1') == '1'


class Tok:
    __slots__ = ("name", "w", "r", "rd", "pre", "frozen")

    def __init__(self, name="", pre=None):
        self.name = name
        self.w = None
        self.r = {}
        self.rd = []
        self.pre = pre
        self.frozen = False


class Op:
    __slots__ = ("eng", "fn", "deps", "odeps", "sem", "val", "dma_key", "n_dma", "waited", "idx", "is_dma",
                 "cost", "gidx", "pos", "fin", "done", "st", "tag")


class FW:
    def __init__(self, nc, n_phase=2):
        self.nc = nc
        self.ops = {e: [] for e in ENGS}
        self.n_phase = n_phase
        self.dma_keys = {}
        self.last_dma = {}
        self.nops = 0

    def op(self, eng, fn, reads=(), writes=(), dma_key=None, n_dma=1, cost=0.3):
        o = Op()
        o.eng = eng
        o.fn = fn
        o.is_dma = dma_key is not None
        o.dma_key = dma_key
        o.n_dma = n_dma
        o.waited = False
        o.sem = None
        o.val = 0
        o.cost = cost
        o.idx = len(self.ops[eng])
        o.gidx = self.nops
        self.nops += 1
        deps = {}
        odeps = {}

        def add(d, raw):
            if d is None or d is o:
                return
            if (not d.is_dma) and (not o.is_dma) and d.eng == eng:
                if eng == PE or (not raw and not STRICT):
                    odeps[id(d)] = d
                    return
            deps[id(d)] = d

        def add_all(t):
            add(t.w, False)
            for r in t.r.values():
                add(r, False)
            for r in t.rd:
                add(r, False)

        for t in reads:
            add(t.w, True)
        for t in writes:
            add_all(t)
            if t.pre:
                for p in t.pre:
                    add_all(p)
                t.pre = None
        for t in reads:
            if t.frozen:
                continue
            if o.is_dma:
                t.rd.append(o)
            else:
                prev = t.r.get(eng)
                if prev is not None and prev is not o:
                    odeps[id(prev)] = prev
                t.r[eng] = o
        for t in writes:
            assert not t.frozen, t.name
            t.w = o
            t.r = {}
            t.rd = []
        if o.is_dma:
            prev = self.last_dma.get(dma_key)
            if prev is not None:
                deps[id(prev)] = prev
            self.last_dma[dma_key] = o
        if os.environ.get("K_SERIAL") and getattr(self, "last", None) is not None:
            lo = self.last
            if lo.is_dma or lo.eng != eng or o.is_dma:
                deps[id(lo)] = lo
            else:
                odeps[id(lo)] = lo
        self.last = o
        o.deps = list(deps.values())
        o.odeps = list(odeps.values())
        for d in o.deps:
            d.waited = True
        self.ops[eng].append(o)
        if o.is_dma:
            self.dma_keys[dma_key] = self.dma_keys.get(dma_key, 0) + 16 * n_dma
            o.val = self.dma_keys[dma_key]
        return o

    def schedule(self, window=24):
        for e in ENGS:
            for o in self.ops[e]:
                o.done = False
                o.fin = 0.0
        use_bl = os.environ.get("K_BL", "1") == "1"
        if use_bl:
            allops = sorted([o for e in ENGS for o in self.ops[e]], key=lambda o: o.gidx)
            succ = {id(o): [] for o in allops}
            for o in allops:
                for d in o.deps:
                    succ[id(d)].append(o)
                for d in o.odeps:
                    succ[id(d)].append(o)
            bl = {}
            for o in reversed(allops):
                m = 0.0
                for s_ in succ[id(o)]:
                    v = bl[id(s_)]
                    if v > m:
                        m = v
                bl[id(o)] = m + o.cost + (2.0 if o.is_dma else 0.15)
            self.bl = bl
        pending = {e: list(self.ops[e]) for e in ENGS}
        free = {e: 0.0 for e in ENGS}
        new = {e: [] for e in ENGS}
        dma_lat = 2.0
        remaining = sum(len(v) for v in pending.values())
        eps = float(os.environ.get('K_EPS', '0.5'))
        while remaining:
            best = None
            cands = []
            for e in ENGS:
                pl = pending[e]
                if not pl:
                    continue
                lim = min(window, len(pl))
                if e in (SP, POOL):
                    lim = min(int(os.environ.get('K_SPW', '8')), len(pl)) if e == SP else min(int(os.environ.get('K_PLW', '24')), len(pl))
                for i in range(lim):
                    o = pl[i]
                    ok = True
                    st = free[e]
                    for d in o.deps:
                        if not d.done:
                            ok = False
                            break
                        if d.fin > st:
                            st = d.fin
                    if not ok:
                        continue
                    for d in o.odeps:
                        if not d.done:
                            ok = False
                            break
                    if not ok:
                        continue
                    if o.is_dma:
                        blocked = False
                        for j in range(i):
                            if pl[j].is_dma and pl[j].dma_key == o.dma_key:
                                blocked = True
                                break
                        if blocked:
                            continue
                    key = (st, -self.bl[id(o)] if use_bl else o.gidx, o.gidx) if use_bl else (st, o.gidx)
                    if use_bl:
                        cands.append((st, -self.bl[id(o)], o.gidx, e, i, o))
                    if best is None or key < best[0]:
                        best = (key, e, i, o, st)
                    if (not use_bl) and i == 0 and st <= free[e]:
                        break
            assert best is not None, "scheduler deadlock"
            _, e, i, o, st = best
            if use_bl and eps > 0:
                lim_st = st + eps
                c2 = min((c for c in cands if c[0] <= lim_st), key=lambda c: (c[1], c[0], c[2]))
                st, _, _, e, i, o = c2
            pending[e].pop(i)
            o.done = True
            o.st = st
            if o.is_dma:
                free[e] = st + 0.1
                o.fin = st + dma_lat + o.cost
            else:
                free[e] = st + o.cost
                o.fin = st + o.cost + float(os.environ.get('K_LAT', '0.15'))
            new[e].append(o)
            remaining -= 1
        self.ops = new
        self.est = max(free.values())

    def emit(self, stack):
        nc = self.nc
        if os.environ.get("K_SCHED", "1") != "0":
            self.schedule()
        engsems = {e: [stack.enter_context(nc.semaphore(f"s_{e}_{p}")) for p in range(self.n_phase)]
                   for e in ENGS}
        dmasems = {k: stack.enter_context(nc.semaphore(f"d_{k}")) for k in self.dma_keys}
        cum = {}
        for e in ENGS:
            cnt = 0
            for pos, o in enumerate(self.ops[e]):
                o.pos = pos
                if o.is_dma:
                    o.sem = dmasems[o.dma_key]
                    cum[o.dma_key] = cum.get(o.dma_key, 0) + 16 * o.n_dma
                    o.val = cum[o.dma_key]
                elif o.waited:
                    ph = cnt // SEM_LIM
                    assert ph < self.n_phase, f"too many waited ops on {e}"
                    o.sem = engsems[e][ph]
                    o.val = cnt % SEM_LIM + 1
                    cnt += 1
        assert cum == self.dma_keys
        for e in ENGS:
            for o in self.ops[e]:
                best = {}
                keep = []
                for d in o.deps:
                    if d.is_dma:
                        keep.append(d)
                    else:
                        b = best.get(d.eng)
                        if b is None or b.pos < d.pos:
                            best[d.eng] = d
                o.deps = keep + list(best.values())
        final = dict(self.dma_keys)
        self.stats = {}

        def run(eng, h):
            seen = {}
            nwait = 0
            for o in self.ops[eng]:
                for d in o.deps:
                    sid = id(d.sem)
                    if seen.get(sid, 0) < d.val:
                        h.wait_ge(d.sem, d.val)
                        seen[sid] = d.val
                        nwait += 1
                res = o.fn(h)
                if o.is_dma:
                    if not isinstance(res, (list, tuple)):
                        res = [res]
                    assert len(res) == o.n_dma, (len(res), o.n_dma)
                    for ins in res:
                        ins.then_inc(o.sem, 16)
                elif o.waited:
                    res.then_inc(o.sem, 1)
            if eng == SP:
                for k, v in final.items():
                    h.wait_ge(dmasems[k], v)
            self.stats[eng] = (len(self.ops[eng]), nwait)

        with nc.Block() as block:
            @block.tensor
            def _(h):
                run(PE, h)

            @block.scalar
            def _(h):
                run(ACT, h)

            @block.vector
            def _(h):
                run(DVE, h)

            @block.gpsimd
            def _(h):
                run(POOL, h)

            @block.sync
            def _(h):
                run(SP, h)


class Arena:
    def __init__(self, nc, stack, name, ncols):
        self.t = stack.enter_context(nc.sbuf_tensor(name, [128, ncols], BF16))
        self.ncols = ncols
        self.views = []

    def alloc(self, c0, n, ntok=1, name=""):
        assert c0 + n <= self.ncols, (name, c0, n, self.ncols)
        pre = []
        keep = []
        for (a, b, toks) in self.views:
            if a < c0 + n and c0 < b:
                pre.extend(toks)
                if a >= c0 and b <= c0 + n:
                    continue
            keep.append((a, b, toks))
        toks = [Tok(f"{name}{i}", pre=list(pre)) for i in range(ntok)]
        keep.append((c0, c0 + n, toks))
        self.views = keep
        return self.t[:, c0:c0 + n], toks


D = 1024
S = 2048
CTX = 256
T = S + CTX
DFF = 2816
NFF = 22
EPS = 1e-6
TILES = [(0, 256, True), (256, 768, False), (768, 1280, False), (1280, 1792, False), (1792, 2304, False)]
C_C64, C_S64, C_CR, C_SR = 0, 2048, 4096, 6144
C_P64, C_PR, C_ID, C_MA, C_MB = 8192, 8320, 8448, 8576, 8704
NCONST = 8832


def make_consts():
    theta = 10000.0
    n_rows = S // 64
    rows = np.repeat(np.arange(n_rows), 64).astype(np.float32)
    cols = np.tile(np.arange(64), n_rows).astype(np.float32)

    def tab(dim):
        q = dim // 4
        inv = (theta ** (-np.arange(q, dtype=np.float32) / q)).astype(np.float32)
        ang = np.concatenate([rows[:, None] * inv, cols[:, None] * inv], axis=-1).astype(np.float32)
        return np.cos(ang).astype(np.float32), np.sin(ang).astype(np.float32)

    c = np.zeros((128, NCONST), np.float32)
    ch, sh = tab(64)
    cr, sr = tab(32)
    for p in range(128):
        i = (p % 64) % 32
        c[p, C_C64:C_C64 + S] = ch[:, i]
        c[p, C_S64:C_S64 + S] = sh[:, i]
    for p in range(64, 96):
        i = (p - 64) % 16
        c[p, C_CR:C_CR + S] = cr[:, i]
        c[p, C_SR:C_SR + S] = sr[:, i]
    for d in range(128):
        dd = d % 64
        base = d - dd
        if dd < 32:
            c[base + dd + 32, C_P64 + d] = -1.0
        else:
            c[base + dd - 32, C_P64 + d] = 1.0
    for d in range(64, 96):
        dd = d - 64
        if dd < 16:
            c[64 + dd + 16, C_PR + d] = -1.0
        else:
            c[64 + dd - 16, C_PR + d] = 1.0
    c[:, C_ID:C_ID + 128] = np.eye(128, dtype=np.float32)
    jj = np.arange(128)[:, None]
    ii = np.arange(128)[None, :]
    c[:, C_MA:C_MA + 128] = (ii <= jj).astype(np.float32)
    c[:, C_MB:C_MB + 128] = (jj <= ii).astype(np.float32)
    return c


def build_nc(nseq=2, stage=0):
    nc = bass.Bass("TRN2", target_bir_lowering=False)
    dt = lambda name, shape: nc.dram_tensor(name, shape, F32, kind="ExternalInput").ap()
    x_d = dt("x", [2, S, D]); c_d = dt("c", [2, D]); ctx_d = dt("ctx", [2, CTX, D]); cctx_d = dt("c_ctx", [D])
    modw_d = dt("mod_w", [2, D, 6 * D]); modb_d = dt("mod_b", [2, 6 * D])
    n1g_d = dt("norm1_g", [2, D]); n2g_d = dt("norm2_g", [2, D])
    evwin_d = dt("ev_w_in", [1, D, 1184]); evqag_d = dt("ev_qa_g", [1, 64]); evkag_d = dt("ev_ka_g", [1, 64])
    evqlg_d = dt("ev_qlat_g", [1, 256]); evwqup_d = dt("ev_w_q_up", [1, 256, 768]); evkvg_d = dt("ev_kvlat_g", [1, 128])
    evwkv_d = dt("ev_w_kv_up", [1, 128, 1024]); evwout_d = dt("ev_w_out", [1, 1024, D])
    odwin_d = dt("od_w_in", [1, D, 1792]); odlng_d = dt("od_ln_g", [1, 512]); odlnb_d = dt("od_ln_b", [1, 512])
    odsw_d = dt("od_sgu_w", [1, 8, 128, 128]); odsb_d = dt("od_sgu_b", [1, 8, 128]); odsink_d = dt("od_sink", [1, 8])
    odwout_d = dt("od_w_out", [1, 1024, D])
    fup_d = dt("ffn_up", [2, D, 2 * DFF]); fcw_d = dt("ffn_conv_w", [2, 3, 2 * DFF]); fcb_d = dt("ffn_conv_b", [2, 2 * DFF])
    fdn_d = dt("ffn_down", [2, DFF, D]); fing_d = dt("final_g", [D])
    consts_d = dt("consts", [128, NCONST])
    out_d = nc.dram_tensor("out", [2, S, D], F32, kind="ExternalOutput").ap()

    fw = FW(nc)
    st = ExitStack()
    with st:
        sbt = lambda name, shape, d=F32: st.enter_context(nc.sbuf_tensor(name, shape, d))
        resid = sbt("resid", [128, 8, T])
        zbuf = sbt("zbuf", [128, 8, T], BF16)
        WA = Arena(nc, st, "warena", 9216)
        AR = Arena(nc, st, "arena", 21248)
        SC = Arena(nc, st, "scr", 8192)
        sq = sbt("sq", [128, 2, 512], BF16)
        rstd = sbt("rstd", [128, 2, 512])
        tmp32 = sbt("tmp32", [128, 3, 512])
        rd = sbt("rd", [128, 512])
        cst = sbt("cst", [128, 640])
        msk = sbt("msk", [128, 256], BF16)
        onesm = sbt("onesm", [128, 5, 128], BF16)
        epst = sbt("epst", [128, 1])
        modT = sbt("modT", [128, 2, 48, 3])
        scv = sbt("scv", [128, 2, 2, 8, 3])
        csb = sbt("csb", [128, 8, 3])
        gsm = sbt("gsm", [128, 5, 8])
        gq = sbt("gq", [128, 4])
        gql = sbt("gql", [128, 2])
        cvw = sbt("cvw", [128, 2, 4, 44])
        esink = sbt("esink", [128, 8])
        ps = st.enter_context(nc.psum_tensor("ps", [128, 8, 512], F32))
        PB = [Tok(f"pb{i}") for i in range(8)]

        class Rot:
            def __init__(self, idx):
                self.idx = list(idx); self.i = 0
            def next(self):
                v = self.idx[self.i % len(self.idx)]; self.i += 1
                return v
        PAIR_EXP = os.environ.get('K_PAIR', '0') == '1'
        SB_ = Rot([0, 1, 2, 3] if PAIR_EXP else [0, 1, 2]); OB_ = Rot([4, 5] if PAIR_EXP else [3, 4]); GB_ = Rot([6, 7] if PAIR_EXP else [5, 6, 7])
        t_sq = [Tok("sq0"), Tok("sq1")]; t_rstd = [Tok("rstd0"), Tok("rstd1")]
        t_tmp = [Tok("tmp0"), Tok("tmp1"), Tok("tmp2")]; t_rd = Tok("rd")
        sq_i = Rot([0, 1]); rstd_i = Rot([0, 1]); tmp_i = Rot([0, 1, 2])
        t_cst = Tok("cst"); t_small = Tok("small"); t_mod = Tok("mod")
        t_res = [[Tok(f"res{k}_{t}") for t in range(5)] for k in range(8)]
        t_z = [Tok(f"z{t}") for t in range(5)]
        dk = [0]

        def dkey(p="k"):
            dk[0] += 1
            return f"{p}{dk[0] % 8}"

        def fsz(ap):
            n = 1
            for d_ in ap.shape[1:]:
                n *= d_
            return n

        def mmcost(r):
            return fsz(r) / 2400.0 * (4 if r.dtype == F32 else 1) + 0.005

        def dma(eng, out, in_, reads=(), writes=(), key=None, **kw):
            c = fsz(out) * 128 * 4 / 2.0e5
            fw.op(eng, lambda h: h.dma_start(out=out, in_=in_, **kw), reads=reads, writes=writes, dma_key=key or dkey('w' if eng == POOL else 'k'), cost=c)

        def mm(out, pairs, reads, writes):
            def fn(h):
                n = len(pairs); ins = None
                for i, (l, r) in enumerate(pairs):
                    ins = h.matmul(out, lhsT=l, rhs=r, start=(i == 0), stop=(i == n - 1))
                return ins
            fw.op(PE, fn, reads, writes, cost=sum(mmcost(r) for _, r in pairs))

        def ecost(eng, out):
            n = fsz(out)
            if eng == ACT:
                return n / 1200.0 + 0.22
            if eng == DVE:
                return n / 800.0 + 0.1
            return n / 500.0 + 0.2

        def act(out, in_, func, reads, writes, **kw):
            fw.op(ACT, lambda h: h.activation(out=out, in_=in_, func=func, **kw), reads, writes, cost=ecost(ACT, out))

        def stt(eng, out, in0, scalar, in1, op0, op1, reads, writes):
            fw.op(eng, lambda h: h.scalar_tensor_tensor(out=out, in0=in0, scalar=scalar, in1=in1, op0=op0, op1=op1), reads, writes, cost=ecost(eng, out))

        def tt(eng, out, in0, in1, op, reads, writes):
            fw.op(eng, lambda h: h.tensor_tensor(out=out, in0=in0, in1=in1, op=op), reads, writes, cost=ecost(eng, out))

        def cp(eng, out, in_, reads, writes):
            if eng == ACT:
                fw.op(ACT, lambda h: h.copy(out=out, in_=in_), reads, writes, cost=ecost(ACT, out))
            else:
                fw.op(eng, lambda h: h.tensor_copy(out=out, in_=in_), reads, writes, cost=ecost(eng, out))

        dma(SP, cst[:, 0:384], consts_d[:, C_P64:C_P64 + 384], writes=[t_cst])
        dma(POOL, msk[:], consts_d[:, C_MA:C_MA + 256], writes=[t_cst])
        P64 = cst[:, 0:128]; PR = cst[:, 128:256]; IDN = cst[:, 256:384]

        def init_small(h):
            h.memset(onesm[:, 0, :], 1.0 / 1024)
            h.memset(onesm[0:64, 1, 64:128], 0.0)
            h.memset(onesm[64:128, 1, 0:64], 0.0)
            h.memset(onesm[0:64, 1, 0:64], 1.0 / 64)
            h.memset(onesm[64:128, 1, 64:128], 1.0 / 64)
            h.memset(onesm[:, 2, :], 1.0 / 256)
            h.memset(onesm[:, 3, :], 1.0 / 128)
            h.memset(onesm[:, 4, :], 1.0)
            return h.memset(epst[:], EPS)
        fw.op(DVE, init_small, writes=[t_small])
        fm = lambda v: v.rearrange("(k p) -> p k", p=128)
        SKIP = os.environ.get('K_SKIP', '')
        for i, src in enumerate([n1g_d[0], n1g_d[1], n2g_d[0], n2g_d[1], fing_d]):
            dma(SP, gsm[:, i, :], fm(src), writes=[t_small], allow_slow_non_contiguous=True)
        for half in (range(2) if 'a' not in SKIP else []):
            dma(SP, gq[64 * half:64 * half + 64, 0:1], evqag_d[0].rearrange("(p o) -> p o", o=1), writes=[t_small], allow_slow_non_contiguous=True)
            dma(SP, gq[64 * half:64 * half + 64, 1:2], evkag_d[0].rearrange("(p o) -> p o", o=1), writes=[t_small], allow_slow_non_contiguous=True)
        dma(SP, gq[:, 2:3], evkvg_d[0].rearrange("(p o) -> p o", o=1), writes=[t_small], allow_slow_non_contiguous=True)
        dma(SP, gql[:], fm(evqlg_d[0]), writes=[t_small], allow_slow_non_contiguous=True)
        for l in (range(2) if 'b' not in SKIP else []):
            for i in range(3):
                dma(SP, cvw[:, l, i, :], fcw_d[l, i].rearrange("(k p) -> p k", p=128), writes=[t_small], allow_slow_non_contiguous=True)
            dma(SP, cvw[:, l, 3, :], fcb_d[l].rearrange("(k p) -> p k", p=128), writes=[t_small], allow_slow_non_contiguous=True)
        dma(SP, esink[:], odsink_d[0].partition_broadcast(128), writes=[t_small])
        act(esink[:], esink[:], AF.Exp, [t_small], [t_small])
        for b in (range(2) if 'c' not in SKIP else []):
            dma(SP, csb[:, :, b], fm(c_d[b]), writes=[t_mod], allow_slow_non_contiguous=True)
        dma(SP, csb[:, :, 2], fm(cctx_d), writes=[t_mod], allow_slow_non_contiguous=True)
        act(csb[:], csb[:], AF.Silu, [t_mod], [t_mod])

        def load_seq(s):
            stg2_ap, stg2_t = SC.alloc(0, 4096, 2, "instg")
            for i in (range(18) if 'e' not in SKIP else []):
                src = ctx_d[s, i * 128:(i + 1) * 128, :] if i < 2 else x_d[s, (i - 2) * 128:(i - 1) * 128, :]
                bi = i % 2
                sbuf = stg2_ap[:, bi * 2048:(bi + 1) * 2048].bitcast(F32)
                dma(SP, sbuf, src, writes=[stg2_t[bi]])
                col = i * 128
                ti = 0 if i < 2 else 1 + (i - 2) // 4
                for hk in range(2):
                    g = GB_.next()

                    def trf(h, g=g, hk=hk, sbuf=sbuf):
                        ins = None
                        for kk in range(4):
                            k = hk * 4 + kk
                            ins = h.transpose(ps[:, g, kk * 128:(kk + 1) * 128], sbuf[:, k * 128:(k + 1) * 128], IDN)
                        return ins
                    fw.op(PE, trf, [stg2_t[bi], t_cst], [PB[g]], cost=1.2)
                    cp(ACT if hk == 0 else DVE, resid[:, hk * 4:hk * 4 + 4, col:col + 128], ps[:, g, :].rearrange("p (a b) -> p a b", a=4),
                       [PB[g]], [t_res[k][ti] for k in range(hk * 4, hk * 4 + 4)])


        load_seq(0)

        stg_ap, stg_t = AR.alloc(0, 16384, 2, "modstg")
        modv_ap, modv_t = AR.alloc(16384, 4864, 1, "modv")
        AR.views = []
        stg_ap, stg_t = AR.alloc(0, 8192, 1, "modstg")
        modv_ap, modv_t = AR.alloc(8192, 12288, 1, "modv")
        stg = stg_ap.bitcast(F32).rearrange("p (k n) -> p k n", k=8)
        stgb_ap, stgb_t = WA.alloc(0, 8192, 1, "modstg2")
        stgb = stgb_ap.bitcast(F32).rearrange("p (k n) -> p k n", k=8)
        stgs = [(stg, stg_t), (stgb, stgb_t)]
        modv = modv_ap.bitcast(F32)
        modb_ap, modb_t = SC.alloc(0, 8192, 1, "modb")
        for l in (range(2) if 'd' not in SKIP else []):
            for nt in range(12):
                stg_c, stg_ct = stgs[nt % 2]
                dma(SP, stg_c, modw_d[l].rearrange("(k p) n -> p k n", p=128)[:, :, nt * 512:(nt + 1) * 512], writes=stg_ct)
                g = GB_.next()
                mm(ps[0:3, g, :], [(csb[:, k, :], stg_c[:, k, :]) for k in range(8)], [t_mod] + stg_ct, [PB[g]])
                cp(DVE, modv[0:3, nt * 512:(nt + 1) * 512], ps[0:3, g, :], [PB[g]], modv_t)
            g = GB_.next()

            def tr(h, g=g):
                ins = None
                for j in range(48):
                    ins = h.transpose(ps[:, g, j * 3:j * 3 + 3], modv[0:3, j * 128:(j + 1) * 128], IDN[0:3, 0:3])
                return ins
            fw.op(PE, tr, modv_t + [t_cst], [PB[g]])
            cp(DVE, modT[:, l, :, :], ps[:, g, 0:144].rearrange("p (j v) -> p j v", v=3), [PB[g]], [t_mod])
            mbT = modb_ap.bitcast(F32)[:, 0:48]
            dma(SP, mbT, fm(modb_d[l]), writes=modb_t, allow_slow_non_contiguous=True)
            for v in range(3):
                tt(DVE, modT[:, l, :, v], modT[:, l, :, v], mbT, ALU.add, [t_mod] + modb_t, [t_mod])
            for n in range(2):
                for v in range(3):
                    stt(DVE, scv[:, l, n, :, v], modT[:, l, (8 + 24 * n):(16 + 24 * n), v], 1.0, gsm[:, 2 * n + l, :], ALU.add, ALU.mult,
                        [t_mod, t_small], [t_mod])

        def norm_mod(l, n, tiles, vb, gain_only=None, out_fn=None):
            for ti in tiles:
                c0, c1, isc = TILES[ti]
                N = c1 - c0
                v = 2 if isc else vb
                g = GB_.next()
                for k in range(8):
                    si = sq_i.next()
                    act(sq[:, si, 0:N], resid[:, k, c0:c1], AF.Square, [t_res[k][ti]], [t_sq[si]])
                    fw.op(PE, (lambda h, g=g, si=si, k=k, N=N: h.matmul(ps[:, g, 0:N], lhsT=onesm[:, 0, :], rhs=sq[:, si, 0:N], start=(k == 0), stop=(k == 7))),
                          [t_sq[si], t_small], [PB[g]], cost=N / 2400.0 + 0.005)
                ri = rstd_i.next()
                act(rstd[:, ri, 0:N], ps[:, g, 0:N], AF.Ln, [PB[g], t_small], [t_rstd[ri]], bias=epst[:, 0:1])
                act(rstd[:, ri, 0:N], rstd[:, ri, 0:N], AF.Exp, [t_rstd[ri]], [t_rstd[ri]], scale=-0.5)
                for k in range(8):
                    if out_fn is not None:
                        out_fn(ti, k, c0, c1, N, ri)
                        continue
                    mi = tmp_i.next()
                    stt(DVE, tmp32[:, mi, 0:N], resid[:, k, c0:c1], scv[:, l, n, k, v:v + 1], rstd[:, ri, 0:N], ALU.mult, ALU.mult,
                        [t_res[k][ti], t_mod, t_rstd[ri]], [t_tmp[mi]])
                    act(zbuf[:, k, c0:c1], tmp32[:, mi, 0:N], AF.Identity, [t_tmp[mi], t_mod], [t_z[ti]],
                        bias=modT[:, l, 24 * n + k, v:v + 1])

        def load_w(dst3, src2, nk, ncol, toks):
            key = dkey("w")
            fw.op(POOL, lambda h: [h.dma_start(out=dst3[:, k, :], in_=src2[k * 128:(k + 1) * 128, :]) for k in range(nk)],
                  writes=toks, dma_key=key, n_dma=nk)

        def gated_add(pb, mo, c0, c1, ti_list, l, gidx, v):
            N = c1 - c0
            stt(DVE, resid[:, mo, c0:c1], ps[:, pb, 0:N], modT[:, l, gidx + mo, v:v + 1], resid[:, mo, c0:c1], ALU.mult, ALU.add,
                [PB[pb], t_mod] + [t_res[mo][ti] for ti in ti_list], [t_res[mo][ti] for ti in ti_list])

        def rope_out(src32, mi, dst, N, tcol0, Cap, Sap, t_tab, prot, prange, rd_toks, wr_toks):
            p0, p1 = prange
            g = GB_.next()
            fw.op(PE, lambda h: h.matmul(ps[p0:p1, g, 0:N], lhsT=prot[p0:p1, p0:p1], rhs=src32[p0:p1, 0:N], start=True, stop=True),
                  [t_tmp[mi], t_cst], [PB[g]], cost=4 * N / 2400.0 + 0.005)
            m2 = tmp_i.next()
            tt(DVE, tmp32[p0:p1, m2, 0:N], ps[p0:p1, g, 0:N], Sap[p0:p1, tcol0:tcol0 + N], ALU.mult, [PB[g]] + t_tab, [t_tmp[m2]])
            tt(DVE, src32[p0:p1, 0:N], src32[p0:p1, 0:N], Cap[p0:p1, tcol0:tcol0 + N], ALU.mult, [t_tmp[mi]] + t_tab, [t_tmp[mi]])
            tt(POOL, dst, src32[p0:p1, 0:N], tmp32[p0:p1, m2, 0:N], ALU.add, [t_tmp[mi], t_tmp[m2]] + rd_toks, wr_toks)

        def attention(qtiles, nheads, K, q_ap, k_ap, v_ap, scale, den_mode, out_w, out_w_toks, nchunk_o, l, vb,
                      q_toks, k_toks, v_toks, window=False, sink=False, ot_view=None, pt_view=None, k_extra=()):
            OT, ot_t = ot_view
            PT, pt_t = pt_view
            pt_i = Rot(list(range(len(pt_t))))
            pt2_i = Rot([0, 2, 4])
            SP2 = Rot([0, 2])
            for ti in qtiles:
                c0, c1, isc = TILES[ti]
                N = c1 - c0
                v = 2 if isc else vb
                oi = (ti % 2)
                for h in range(nheads):
                    e = h % 2
                    ob = OB_.next()
                    if window:
                        tq = ti - 1
                        kcs = [(0, 0, N, None), (1, 0, N, None)]
                        for c in range(4 * tq - 1, 4 * tq + 5):
                            if c < 0 or c > 15:
                                continue
                            b0 = max(4 * tq, c - 1); b1 = min(4 * tq + 3, c + 1)
                            kcs.append((2 + c, (b0 - 4 * tq) * 128, (b1 - 4 * tq + 1) * 128, c))
                    else:
                        kcs = [(kc, 0, N, None) for kc in (range(2) if isc else range(18))]
                    nk = len(kcs)

                    def emit_s(idx, h=h, ti=ti, c0=c0, N=N):
                        kc, a0, a1, cblk = kcs[idx]
                        sb = SB_.next()
                        kt = 0 if kc < 2 else 1 + (kc - 2) // 4
                        fw.op(PE, (lambda h_, sb=sb, kc=kc, a0=a0, a1=a1, h=h, c0=c0: h_.matmul(ps[:, sb, a0:a1], lhsT=k_ap(h, kc), rhs=q_ap(h, c0 + a0, c0 + a1), start=True, stop=True)),
                              [q_toks[ti], k_toks[kt]] + list(k_extra), [PB[sb]], cost=(a1 - a0) / 2400.0 + 0.005)
                        pi = pt_i.next()
                        act(PT[:, pi, a0:a1], ps[:, sb, a0:a1], AF.Exp, [PB[sb]], [pt_t[pi]], scale=scale)
                        if cblk is not None:
                            tq = ti - 1
                            for qb in range(a0 // 128, a1 // 128):
                                qblk = 4 * tq + qb
                                if qblk == cblk:
                                    continue
                                mcol = 0 if qblk > cblk else 128
                                tt(POOL, PT[:, pi, qb * 128:(qb + 1) * 128], PT[:, pi, qb * 128:(qb + 1) * 128], msk[:, mcol:mcol + 128], ALU.mult,
                                   [pt_t[pi], t_cst], [pt_t[pi]])
                        return pi

                    def emit_s2(idx, h=h, ti=ti, c0=c0, N=N):
                        sb = SP2.next()
                        pi = pt2_i.next()
                        for j_ in range(2):
                            kc = kcs[idx + j_][0]
                            kt = 0 if kc < 2 else 1 + (kc - 2) // 4
                            fw.op(PE, (lambda h_, sb=sb, j_=j_, kc=kc, h=h, c0=c0, N=N: h_.matmul(ps[:, sb + j_, 0:N], lhsT=k_ap(h, kc), rhs=q_ap(h, c0, c0 + N), start=True, stop=True)),
                                  [q_toks[ti], k_toks[kt]] + list(k_extra), [PB[sb + j_]], cost=N / 2400.0 + 0.005)
                        act(PT[:, pi:pi + 2, 0:N], ps[:, sb:sb + 2, 0:N], AF.Exp, [PB[sb], PB[sb + 1]], [pt_t[pi], pt_t[pi + 1]], scale=scale)
                        return pi

                    def emit_pv(idx, pi, h=h, ob=ob, nk=nk):
                        kc, a0, a1, cblk = kcs[idx]
                        kt = 0 if kc < 2 else 1 + (kc - 2) // 4

                        def pv(h_, ob=ob, kc=kc, pi=pi, a0=a0, a1=a1, idx=idx, h=h, nk=nk):
                            if den_mode == "aug":
                                return h_.matmul(ps[:, ob, a0:a1], lhsT=v_ap(h, kc), rhs=PT[:, pi, a0:a1], start=(idx == 0), stop=(idx == nk - 1))
                            h_.matmul(ps[0:64, ob, a0:a1], lhsT=v_ap(h, kc), rhs=PT[:, pi, a0:a1], start=(idx == 0), stop=(idx == nk - 1))
                            return h_.matmul(ps[64:128, ob, a0:a1], lhsT=onesm[:, 4, 0:64], rhs=PT[:, pi, a0:a1], start=(idx == 0), stop=(idx == nk - 1))
                        fw.op(PE, pv, [pt_t[pi], v_toks[kt], t_small], [PB[ob]], cost=((a1 - a0) / 2400.0 + 0.005) * (1 if den_mode == 'aug' else 2))
                    if not window and PAIR_EXP:
                        assert nk % 2 == 0
                        np_ = nk // 2
                        pis = {0: emit_s2(0)}
                        for ip in range(np_):
                            if ip + 1 < np_:
                                pis[ip + 1] = emit_s2(2 * (ip + 1))
                            emit_pv(2 * ip, pis[ip])
                            emit_pv(2 * ip + 1, pis[ip] + 1)
                    else:
                        LA = 2
                        pis = {}
                        for idx in range(min(LA, nk)):
                            pis[idx] = emit_s(idx)
                        for idx in range(nk):
                            if idx + LA < nk:
                                pis[idx + LA] = emit_s(idx + LA)
                            emit_pv(idx, pis[idx])
                    if sink:
                        fw.op(DVE, lambda h_, ob=ob, h=h, N=N: h_.tensor_scalar(out=rd[64:128, 0:N], in0=ps[64:128, ob, 0:N], scalar1=esink[64:128, h:h + 1], scalar2=None, op0=ALU.add),
                              [PB[ob], t_small], [t_rd])
                        fw.op(DVE, lambda h_, N=N: h_.reciprocal(out=rd[0:64, 0:N], in_=rd[64:128, 0:N]), [t_rd], [t_rd])
                    else:
                        fw.op(DVE, lambda h_, ob=ob, N=N: h_.reciprocal(out=rd[0:64, 0:N], in_=ps[64:128, ob, 0:N]), [PB[ob]], [t_rd])
                    if e == 0:
                        tt(DVE, OT[0:64, oi, h // 2, 0:N], ps[0:64, ob, 0:N], rd[0:64, 0:N], ALU.mult, [PB[ob], t_rd], [ot_t[oi]])
                    else:
                        mi = tmp_i.next()
                        tt(DVE, tmp32[0:64, mi, 0:N], ps[0:64, ob, 0:N], rd[0:64, 0:N], ALU.mult, [PB[ob], t_rd], [t_tmp[mi]])
                        cp(ACT, OT[64:128, oi, h // 2, 0:N], tmp32[0:64, mi, 0:N], [t_tmp[mi]], [ot_t[oi]])
                for mo in range(8):
                    g = GB_.next()
                    mm(ps[:, g, 0:N], [(out_w(m)[:, mo * 128:(mo + 1) * 128], OT[:, oi, m, 0:N]) for m in range(nchunk_o)],
                       [ot_t[oi]] + out_w_toks, [PB[g]])
                    gated_add(g, mo, c0, c1, [ti], l, 16, v)

        for t_ in (t_small, t_cst, t_mod):
            t_.frozen = True
        for s in range(nseq):
            vb = s
            if s > 0:
                load_seq(s)
            if stage != 1:
                l = 0
                norm_mod(l, 0, range(5), vb)
                wA_ap, wA_t = WA.alloc(0, 6144, 1, "wA")
                wA = wA_ap.rearrange("p (k n) -> p k n", k=8)
                load_w(wA, evwin_d[0][:, 0:768], 8, 768, wA_t)
                tab_ap, tab_t = SC.alloc(0, 4096, 1, "tabA")
                dma(POOL, tab_ap[:, 0:2048], consts_d[:, C_C64:C_C64 + 2048], writes=tab_t)
                dma(POOL, tab_ap[:, 2048:4096], consts_d[:, C_S64:C_S64 + 2048], writes=tab_t)
                Ct = tab_ap[:, 0:2048]; St = tab_ap[:, 2048:4096]
                QA_ap, QA_t = AR.alloc(0, 9216, 5, "QA"); QA = QA_ap.rearrange("p (m t) -> p m t", m=4)
                KA_ap, KA_t = AR.alloc(9216, 4608, 5, "KA"); KA = KA_ap.rearrange("p (j t) -> p j t", j=2)
                VA_ap, VA_t = AR.alloc(13824, 4608, 5, "VA"); VA = VA_ap.rearrange("p (c j d) -> p c j d", c=18, j=2)
                fw.op(POOL, lambda h: h.memset(VA_ap, 1.0), writes=VA_t)

                def qk_chunk(ti, pairs_fn, gcol, dst, dst_toks, rope, ncost=8):
                    c0, c1, isc = TILES[ti]; N = c1 - c0
                    g = GB_.next()
                    fw.op(PE, lambda h: pairs_fn(h, g, c0, c1, N), [t_z[ti]] + wA_t, [PB[g]], cost=ncost * (N / 2400.0 + 0.005))
                    mi = tmp_i.next()
                    if gcol is not None:
                        si = sq_i.next()
                        act(sq[:, si, 0:N], ps[:, g, 0:N], AF.Square, [PB[g]], [t_sq[si]])
                        g2 = GB_.next()
                        mm(ps[:, g2, 0:N], [(onesm[:, 1, :], sq[:, si, 0:N])], [t_sq[si], t_small], [PB[g2]])
                        ri = rstd_i.next()
                        act(rstd[:, ri, 0:N], ps[:, g2, 0:N], AF.Ln, [PB[g2], t_small], [t_rstd[ri]], bias=epst[:, 0:1])
                        act(rstd[:, ri, 0:N], rstd[:, ri, 0:N], AF.Exp, [t_rstd[ri]], [t_rstd[ri]], scale=-0.5)
                        stt(DVE, tmp32[:, mi, 0:N], ps[:, g, 0:N], gq[:, gcol:gcol + 1], rstd[:, ri, 0:N], ALU.mult, ALU.mult,
                            [PB[g], t_small, t_rstd[ri]], [t_tmp[mi]])
                    else:
                        cp(ACT, tmp32[:, mi, 0:N], ps[:, g, 0:N], [PB[g]], [t_tmp[mi]])
                    if rope and not isc:
                        rope_out(tmp32[:, mi, :], mi, dst, N, c0 - 256, Ct, St, tab_t, P64, (0, 128), [], dst_toks)
                    else:
                        cp(DVE, dst, tmp32[:, mi, 0:N], [t_tmp[mi]], dst_toks)

                def proj_qkv(w3, wtoks, tiles, qnorm, Q, Q_t, Kd, K_t, V, V_t, qtiles):
                    for ti in tiles:
                        c0, c1, isc = TILES[ti]; N = c1 - c0
                        if ti in qtiles:
                            for m in range(4):
                                def pf(h, g, c0, c1, N, m=m):
                                    ins = None
                                    for k in range(8):
                                        ins = h.matmul(ps[:, g, 0:N], lhsT=w3[:, k, m * 128:(m + 1) * 128], rhs=zbuf[:, k, c0:c1], start=(k == 0), stop=(k == 7))
                                    return ins
                                qk_chunk(ti, pf, 0 if qnorm else None, Q[:, m, c0:c1], [Q_t[ti]], True)
                        for j in range(2):
                            def pf(h, g, c0, c1, N, j=j):
                                ins = None
                                for k in range(8):
                                    h.matmul(ps[0:64, g, 0:N], lhsT=w3[:, k, 512 + 64 * j:576 + 64 * j], rhs=zbuf[:, k, c0:c1], start=(k == 0), stop=(k == 7))
                                    ins = h.matmul(ps[64:128, g, 0:N], lhsT=w3[:, k, 512 + 64 * j:576 + 64 * j], rhs=zbuf[:, k, c0:c1], start=(k == 0), stop=(k == 7))
                                return ins
                            qk_chunk(ti, pf, 1 if qnorm else None, Kd[:, j, c0:c1], [K_t[ti]], True, ncost=16)
                        for sc_ in range(N // 128):
                            g = GB_.next()
                            cc0 = c0 + sc_ * 128
                            mm(ps[:, g, 0:128], [(zbuf[:, k, cc0:cc0 + 128], w3[:, k, 640:768]) for k in range(8)], [t_z[ti]] + wtoks, [PB[g]])
                            cp(ACT, V[:, cc0 // 128, :, 0:64], ps[:, g, 0:128].rearrange("p (j d) -> p j d", j=2), [PB[g]], [V_t[ti]])


                def pad_k(KA_ap, KA_t):
                    kz_ap, kz_t = WA.alloc(4608, 4608, 1, "KZ1")
                    fw.op(POOL, lambda h: h.tensor_copy(out=kz_ap[64:128, :], in_=KA_ap[64:128, :]), KA_t, kz_t, cost=10.0)
                    fw.op(DVE, lambda h: h.memset(kz_ap[0:64, :], 0.0), (), kz_t, cost=3.0)
                    fw.op(DVE, lambda h: h.memset(KA_ap[64:128, :], 0.0), (), KA_t, cost=3.0)
                    return kz_ap.rearrange("p (j t) -> p j t", j=2), kz_t
                proj_qkv(wA, wA_t, range(5), True, QA, QA_t, KA, KA_t, VA, VA_t, range(5))
                KZ, KZ_t = pad_k(KA_ap, KA_t)
                woA_ap, woA_t = WA.alloc(0, 4096, 1, "woA")
                woA = woA_ap.rearrange("p (k n) -> p k n", k=4)
                load_w(woA, evwout_d[0][0:512, :], 4, 1024, woA_t)
                OT_ap, OT_t = SC.alloc(0, 4096, 2, "OT"); OT = OT_ap.rearrange("p (o m n) -> p o m n", o=2, m=4)
                PT_ap, PT_t = SC.alloc(4096, 3072, 6, "PT"); PT = PT_ap.rearrange("p (i n) -> p i n", i=6)
                attention(range(5), 8, 64,
                          lambda h, a, b: QA[:, h // 2, a:b],
                          lambda h, kc: (KA if h % 2 == 0 else KZ)[:, h // 4, kc * 128:(kc + 1) * 128],
                          lambda h, kc: VA[:, kc, h // 4, :], 64 ** -0.5, "aug",
                          lambda m: woA[:, m, :], woA_t, 4, l, vb, QA_t, KA_t, VA_t, ot_view=(OT, OT_t), pt_view=(PT, PT_t), k_extra=KZ_t)

                if stage != 2:
                    wB_ap, wB_t = WA.alloc(4608, 3328, 1, "wB"); wB = wB_ap.rearrange("p (k n) -> p k n", k=8)
                    load_w(wB, evwin_d[0][:, 768:1184], 8, 416, wB_t)
                    wq_ap, wq_t = WA.alloc(0, 1536, 1, "wq"); wq = wq_ap.rearrange("p (k n) -> p k n", k=2)
                    load_w(wq, evwqup_d[0], 2, 768, wq_t)
                    wkv_ap, wkv_t = WA.alloc(1536, 1024, 1, "wkv"); wkv = wkv_ap.rearrange("p (k n) -> p k n", k=1)
                    load_w(wkv, evwkv_d[0], 1, 1024, wkv_t)
                    tab_ap, tab_t = SC.alloc(0, 4096, 1, "tabR")
                    dma(POOL, tab_ap[:, 0:2048], consts_d[:, C_CR:C_CR + 2048], writes=tab_t)
                    dma(POOL, tab_ap[:, 2048:4096], consts_d[:, C_SR:C_SR + 2048], writes=tab_t)
                    Cr = tab_ap[:, 0:2048]; Sr = tab_ap[:, 2048:4096]
                    cqn_ap, cqn_t = AR.alloc(0, 4608, 5, "cqn"); cqn = cqn_ap.rearrange("p (m t) -> p m t", m=2)
                    ckv_ap, ckv_t = AR.alloc(4608, 2304, 5, "ckv")
                    KBs = []
                    for bi_ in range(2):
                        kb_ap, kb_t = AR.alloc(6912 + 4608 * bi_, 4608, 5, f"KB{bi_}")
                        fw.op(POOL, lambda h, kb_ap=kb_ap: h.memset(kb_ap[96:128, :], 0.0), writes=kb_t, cost=3.0)
                        KBs.append((kb_ap.rearrange("p (e t) -> p e t", e=2), kb_t))
                    for ti in range(5):
                        c0, c1, isc = TILES[ti]; N = c1 - c0
                        gs = []
                        si_l = []
                        for m in range(2):
                            g = GB_.next(); gs.append(g)
                            mm(ps[:, g, 0:N], [(wB[:, k, m * 128:(m + 1) * 128], zbuf[:, k, c0:c1]) for k in range(8)], [t_z[ti]] + wB_t, [PB[g]])
                        g2 = GB_.next()
                        for m in range(2):
                            si = sq_i.next()
                            act(sq[:, si, 0:N], ps[:, gs[m], 0:N], AF.Square, [PB[gs[m]]], [t_sq[si]])
                            fw.op(PE, (lambda h, g2=g2, si=si, m=m, N=N: h.matmul(ps[:, g2, 0:N], lhsT=onesm[:, 2, :], rhs=sq[:, si, 0:N], start=(m == 0), stop=(m == 1))),
                                  [t_sq[si], t_small], [PB[g2]])
                        ri = rstd_i.next()
                        act(rstd[:, ri, 0:N], ps[:, g2, 0:N], AF.Ln, [PB[g2], t_small], [t_rstd[ri]], bias=epst[:, 0:1])
                        act(rstd[:, ri, 0:N], rstd[:, ri, 0:N], AF.Exp, [t_rstd[ri]], [t_rstd[ri]], scale=-0.5)
                        for m in range(2):
                            stt(DVE, cqn[:, m, c0:c1], ps[:, gs[m], 0:N], gql[:, m:m + 1], rstd[:, ri, 0:N], ALU.mult, ALU.mult,
                                [PB[gs[m]], t_small, t_rstd[ri]], [cqn_t[ti]])
                        g = GB_.next()
                        mm(ps[:, g, 0:N], [(wB[:, k, 256:384], zbuf[:, k, c0:c1]) for k in range(8)], [t_z[ti]] + wB_t, [PB[g]])
                        si = sq_i.next()
                        act(sq[:, si, 0:N], ps[:, g, 0:N], AF.Square, [PB[g]], [t_sq[si]])
                        g2 = GB_.next()
                        mm(ps[:, g2, 0:N], [(onesm[:, 3, :], sq[:, si, 0:N])], [t_sq[si], t_small], [PB[g2]])
                        ri = rstd_i.next()
                        act(rstd[:, ri, 0:N], ps[:, g2, 0:N], AF.Ln, [PB[g2], t_small], [t_rstd[ri]], bias=epst[:, 0:1])
                        act(rstd[:, ri, 0:N], rstd[:, ri, 0:N], AF.Exp, [t_rstd[ri]], [t_rstd[ri]], scale=-0.5)
                        stt(DVE, ckv_ap[:, c0:c1], ps[:, g, 0:N], gq[:, 2:3], rstd[:, ri, 0:N], ALU.mult, ALU.mult,
                            [PB[g], t_small, t_rstd[ri]], [ckv_t[ti]])
                        g = GB_.next()
                        mm(ps[64:96, g, 0:N], [(wB[:, k, 384:416], zbuf[:, k, c0:c1]) for k in range(8)], [t_z[ti]] + wB_t, [PB[g]])
                        kr0 = KBs[0][0][64:96, 0, c0:c1]
                        if isc:
                            cp(ACT, kr0, ps[64:96, g, 0:N], [PB[g]], [KBs[0][1][ti]])
                        else:
                            mi = tmp_i.next()
                            cp(ACT, tmp32[64:96, mi, 0:N], ps[64:96, g, 0:N], [PB[g]], [t_tmp[mi]])
                            rope_out(tmp32[:, mi, :], mi, kr0, N, c0 - 256, Cr, Sr, tab_t, PR, (64, 96), [], [KBs[0][1][ti]])
                        cp(POOL, KBs[0][0][64:96, 1, c0:c1], kr0, [KBs[0][1][ti]], [KBs[0][1][ti]])
                        for e_ in range(2):
                            cp(POOL, KBs[1][0][64:96, e_, c0:c1], kr0, [KBs[0][1][ti]], [KBs[1][1][ti]])
                    QB_ap, QB_t = AR.alloc(16128, 4608, 5, "QB"); QB = QB_ap.rearrange("p (e t) -> p e t", e=2)
                    VB_ap, VB_t = WA.alloc(4608, 4608, 5, "VBa"); VB = VB_ap.rearrange("p (c e d) -> p c e d", c=18, e=2)
                    fw.op(POOL, lambda h, VB_ap=VB_ap: h.memset(VB_ap, 1.0), writes=VB_t, cost=10.0)
                    fw.op(POOL, lambda h, QB_ap=QB_ap: h.memset(QB_ap[96:128, :], 0.0), writes=QB_t, cost=3.0)
                    woB_ap, woB_t = WA.alloc(2560, 2048, 2, "woB"); woB = woB_ap.rearrange("p (i n) -> p i n", i=2)
                    OT_ap, OT_t = SC.alloc(4096, 1024, 2, "OTb"); OTb = OT_ap.rearrange("p (o m n) -> p o m n", o=2, m=1)
                    PT_ap, PT_t = SC.alloc(5120, 3072, 6, "PTb"); PTb = PT_ap.rearrange("p (i n) -> p i n", i=6)
                    for sp in range(4):
                        wi = sp % 2
                        KB, KB_t = KBs[sp % 2]
                        fw.op(POOL, lambda h, sp=sp, wi=wi: [h.dma_start(out=woB[:, wi, :], in_=evwout_d[0][512 + 128 * sp:512 + 128 * (sp + 1), :])],
                              writes=[woB_t[wi]], dma_key=dkey("w"))
                        for ti in range(5):
                            c0, c1, isc = TILES[ti]; N = c1 - c0
                            for e in range(2):
                                hh = 2 * sp + e
                                g = GB_.next()
                                mm(ps[0:96, g, 0:N], [(wq[:, k, hh * 96:(hh + 1) * 96], cqn[:, k, c0:c1]) for k in range(2)], [cqn_t[ti]] + wq_t, [PB[g]])
                                cp(ACT, QB[0:64, e, c0:c1], ps[0:64, g, 0:N], [PB[g]], [QB_t[ti]])
                                if isc:
                                    cp(DVE, QB[64:96, e, c0:c1], ps[64:96, g, 0:N], [PB[g]], [QB_t[ti]])
                                else:
                                    mi = tmp_i.next()
                                    cp(DVE, tmp32[64:96, mi, 0:N], ps[64:96, g, 0:N], [PB[g]], [t_tmp[mi]])
                                    rope_out(tmp32[:, mi, :], mi, QB[64:96, e, c0:c1], N, c0 - 256, Cr, Sr, tab_t, PR, (64, 96), [], [QB_t[ti]])
                                g = GB_.next()
                                mm(ps[0:64, g, 0:N], [(wkv[:, 0, hh * 128:hh * 128 + 64], ckv_ap[:, c0:c1])], [ckv_t[ti]] + wkv_t, [PB[g]])
                                cp(ACT, KB[0:64, e, c0:c1], ps[0:64, g, 0:N], [PB[g]], [KB_t[ti]])
                            for sc_ in range(N // 128):
                                g = GB_.next()
                                cc0 = c0 + sc_ * 128
                                mm(ps[:, g, 0:256], [(ckv_ap[:, cc0:cc0 + 128], wkv[:, 0, sp * 256:(sp + 1) * 256])], [ckv_t[ti]] + wkv_t, [PB[g]])
                                cp(DVE, VB[:, cc0 // 128, :, 0:64], ps[:, g, 0:256].rearrange("p (e x d) -> p e x d", e=2, x=2)[:, :, 1, :], [PB[g]], [VB_t[ti]])
                        attention(range(5), 2, 96,
                                  lambda h, a, b: QB[:, h, a:b],
                                  lambda h, kc, KB=KB: KB[:, h, kc * 128:(kc + 1) * 128],
                                  lambda h, kc: VB[:, kc, h, :], 96 ** -0.5, "aug",
                                  lambda m, wi=wi: woB[:, wi, :], [woB_t[wi]], 1, l, vb, QB_t, KB_t, VB_t, ot_view=(OTb, OT_t), pt_view=(PTb, PT_t))

            def ffn(l, tiles, vb):
                norm_mod(l, 1, tiles, vb)
                GB_ = Rot([0, 1, 2, 3, 4, 5, 6, 7])
                wins = []
                if 0 in tiles:
                    wins.append((0, 256, 0, 256, 2, [0]))
                b = [0, 410, 820, 1230, 1639, 2048]
                for i in range(5):
                    h0 = max(b[i] - 1, 0) + 256; h1 = min(b[i + 1] + 1, 2048) + 256
                    tl = sorted(set([1 + (c - 256) // 512 for c in (b[i] + 256, b[i + 1] + 255)]))
                    wins.append((h0, h1, b[i] + 256, b[i + 1] + 256, vb, tl))
                passes = [list(range(0, 6)), list(range(6, 12)), list(range(12, 17)), list(range(17, 22))]
                act_ap, act_t = AR.alloc(0, 6 * T, 6, "ffact"); actb = act_ap.rearrange("p (c t) -> p c t", c=6)
                dn_ap, dn_t = AR.alloc(6 * T, 6144, 1, "ffdn"); dn = dn_ap.rearrange("p (c n) -> p c n", c=6)
                ub = []
                ub.append(WA.alloc(0, 4096, 1, "ffu0")); ub.append(WA.alloc(4096, 4096, 1, "ffu1"))
                cv_ap, cv_t = SC.alloc(0, 8192, 8, "ffcv"); cvt = cv_ap.bitcast(F32).rearrange("p (i n) -> p i n", i=8)
                cv_i = Rot([0, 1, 2]); sl_i = Rot([6, 7]); blk_i = 0
                pend = []

                def flush():
                    while pend:
                        (ta, xa, tg, xg, ci, o0, o1, No) = pend.pop(0)
                        xs = sl_i.next()
                        act(cvt[:, xs, 0:No], tg[:, 0:No], AF.Silu, [cv_t[xg]], [cv_t[xs]])
                        tt(POOL, actb[:, ci, o0:o1], ta[:, 0:No], cvt[:, xs, 0:No], ALU.mult, [cv_t[xa], cv_t[xs]], [act_t[ci]])
                for pl in passes:
                    npc = len(pl)
                    fw.op(POOL, lambda h, pl=pl, npc=npc: [h.dma_start(out=dn[:, i, :], in_=fdn_d[l][pl[i] * 128:(pl[i] + 1) * 128, :]) for i in range(npc)],
                          writes=dn_t, dma_key=dkey("w"), n_dma=npc)
                    for bi in range(0, npc, 2):
                        pcs = pl[bi:bi + 2]
                        u_ap, u_t = ub[blk_i % 2]; blk_i += 1
                        u3 = u_ap.rearrange("p (k n) -> p k n", k=8)
                        npb = len(pcs)
                        def ld(h, pcs=pcs, npb=npb, u3=u3):
                            r = []
                            for k in range(8):
                                r.append(h.dma_start(out=u3[:, k, 0:128 * npb], in_=fup_d[l][k * 128:(k + 1) * 128, pcs[0] * 128:(pcs[0] + npb) * 128]))
                                r.append(h.dma_start(out=u3[:, k, 256:256 + 128 * npb], in_=fup_d[l][k * 128:(k + 1) * 128, DFF + pcs[0] * 128:DFF + (pcs[0] + npb) * 128]))
                            return r
                        fw.op(POOL, ld, writes=u_t, dma_key=dkey("w"), n_dma=16)
                        for j, c in enumerate(pcs):
                            ci = pl.index(c)
                            for (h0, h1, o0, o1, v, tl) in wins:
                                Nh = h1 - h0; No = o1 - o0; d = o0 - h0
                                seq0 = 0 if o0 < 256 else 256; seq1 = 256 if o0 < 256 else T
                                res = []
                                for half in range(2):
                                    g = GB_.next()
                                    col = half * 256 + j * 128
                                    mm(ps[:, g, 0:Nh], [(u3[:, k, col:col + 128], zbuf[:, k, h0:h1]) for k in range(8)], [t_z[t] for t in tl] + u_t, [PB[g]])
                                    fi = half * NFF + c
                                    xi = (cv_i.next() if half == 0 else xi_a) + 3 * half
                                    xi_a = xi
                                    tb = cvt[:, xi, :]
                                    act(tb[:, 0:No], ps[:, g, d:d + No], AF.Identity, [PB[g], t_small], [cv_t[xi]],
                                        scale=cvw[:, l, 1, fi:fi + 1], bias=cvw[:, l, 3, fi:fi + 1])
                                    lo = 1 if o0 == seq0 else 0
                                    stt(DVE, tb[:, lo:No], ps[:, g, d - 1 + lo:d - 1 + No], cvw[:, l, 0, fi:fi + 1], tb[:, lo:No], ALU.mult, ALU.add,
                                        [PB[g], t_small, cv_t[xi]], [cv_t[xi]])
                                    hi = No - 1 if o1 == seq1 else No
                                    stt(DVE, tb[:, 0:hi], ps[:, g, d + 1:d + 1 + hi], cvw[:, l, 2, fi:fi + 1], tb[:, 0:hi], ALU.mult, ALU.add,
                                        [PB[g], t_small, cv_t[xi]], [cv_t[xi]])
                                    res.append((tb, xi))
                                (ta, xa), (tg, xg) = res
                                flush()
                                pend.append((ta, xa, tg, xg, ci, o0, o1, No))
                    flush()
                    for (h0, h1, o0, o1, v, tl) in wins:
                        No = o1 - o0
                        for mo in range(8):
                            g = GB_.next()
                            mm(ps[:, g, 0:No], [(dn[:, i, mo * 128:(mo + 1) * 128], actb[:, i, o0:o1]) for i in range(npc)],
                               [act_t[i] for i in range(npc)] + dn_t, [PB[g]])
                            gated_add(g, mo, o0, o1, tl, l, 40, v)

            if stage not in (1, 2, 3):
                ffn(0, range(5), vb)

            if stage == 0 or stage >= 5:
                l = 1
                norm_mod(l, 0, range(5), vb)
                wD_ap, wD_t = WA.alloc(0, 6144, 1, "wD")
                wD = wD_ap.rearrange("p (k n) -> p k n", k=8)
                load_w(wD, odwin_d[0][:, 1024:1792], 8, 768, wD_t)
                tab_ap, tab_t = SC.alloc(0, 4096, 1, "tabA")
                dma(POOL, tab_ap[:, 0:2048], consts_d[:, C_C64:C_C64 + 2048], writes=tab_t)
                dma(POOL, tab_ap[:, 2048:4096], consts_d[:, C_S64:C_S64 + 2048], writes=tab_t)
                Ct = tab_ap[:, 0:2048]; St = tab_ap[:, 2048:4096]
                wA_t = wD_t
                QA_ap, QA_t = AR.alloc(0, 9216, 5, "QD"); QA = QA_ap.rearrange("p (m t) -> p m t", m=4)
                KA_ap, KA_t = AR.alloc(9216, 4608, 5, "KD"); KA = KA_ap.rearrange("p (j t) -> p j t", j=2)
                VA_ap, VA_t = AR.alloc(13824, 4608, 5, "VD"); VA = VA_ap.rearrange("p (c j d) -> p c j d", c=18, j=2)
                fw.op(POOL, lambda h: h.memset(VA_ap, 1.0), writes=VA_t)
                proj_qkv(wD, wD_t, range(5), False, QA, QA_t, KA, KA_t, VA, VA_t, range(1, 5))
                KZ, KZ_t = pad_k(KA_ap, KA_t)
                woA_ap, woA_t = WA.alloc(0, 4096, 1, "woD")
                woA = woA_ap.rearrange("p (k n) -> p k n", k=4)
                load_w(woA, odwout_d[0][512:1024, :], 4, 1024, woA_t)
                OT_ap, OT_t = SC.alloc(0, 4096, 2, "OT"); OT = OT_ap.rearrange("p (o m n) -> p o m n", o=2, m=4)
                PT_ap, PT_t = SC.alloc(4096, 3072, 6, "PT"); PT = PT_ap.rearrange("p (i n) -> p i n", i=6)
                attention(range(1, 5), 8, 64,
                          lambda h, a, b: QA[:, h // 2, a:b],
                          lambda h, kc: (KA if h % 2 == 0 else KZ)[:, h // 4, kc * 128:(kc + 1) * 128],
                          lambda h, kc: VA[:, kc, h // 4, :], 64 ** -0.5, "aug",
                          lambda m: woA[:, m, :], woA_t, 4, l, vb, QA_t, KA_t, VA_t, window=True, sink=True,
                          ot_view=(OT, OT_t), pt_view=(PT, PT_t), k_extra=KZ_t)
                if stage != 5:
                    wC_ap, wC_t = WA.alloc(0, 8192, 1, "wC")
                    wC = wC_ap.rearrange("p (k n) -> p k n", k=8)
                    load_w(wC, odwin_d[0][:, 0:1024], 8, 1024, wC_t)
                    ug_ap, ug_t = AR.alloc(0, 8192, 4, "ug"); ug = ug_ap.rearrange("p (m t) -> p m t", m=4)
                    vl_ap, vl_t = AR.alloc(8192, 8192, 16, "vln"); vln = vl_ap.rearrange("p (n c) -> p n c", n=16)
                    wst_ap, wst_t = AR.alloc(16384, 1024, 1, "wst"); wst = wst_ap.rearrange("p (g q) -> p g q", g=8)
                    bt_ap, bt_t = AR.alloc(17408, 1024, 1, "btab"); btab = bt_ap.bitcast(F32).rearrange("p (c q) -> p c q", c=4)
                    lg_ap, lg_t = AR.alloc(18432, 2048, 1, "lgb"); lgb = lg_ap.bitcast(F32).rearrange("p (i c) -> p i c", i=2)
                    st_ap, st_t = SC.alloc(0, 2048, 2, "sgstg")
                    dma(SP, lgb[:, 0, :], odlng_d[0].partition_broadcast(128), writes=lg_t)
                    dma(SP, lgb[:, 1, :], odlnb_d[0].partition_broadcast(128), writes=lg_t)
                    for g_ in range(8):
                        dma(SP, btab[64 * (g_ % 2):64 * (g_ % 2) + 64, g_ // 2, :], odsb_d[0, g_].partition_broadcast(64), writes=bt_t)
                        wstg = st_ap[:, (g_ % 2) * 1024:(g_ % 2) * 1024 + 256].bitcast(F32)
                        dma(SP, wstg, odsw_d[0, g_], writes=[st_t[g_ % 2]])
                        g = GB_.next()
                        fw.op(PE, lambda h, g=g, wstg=wstg: h.transpose(ps[:, g, 0:128], wstg, IDN), [st_t[g_ % 2], t_cst], [PB[g]])
                        cp(ACT, wst[:, g_, :], ps[:, g, 0:128], [PB[g]], wst_t)
                    stat = rd
                    for ti in range(1, 5):
                        c0, c1, isc = TILES[ti]; N = c1 - c0
                        for m in range(4):
                            g = GB_.next()
                            mm(ps[:, g, 0:N], [(wC[:, k, m * 128:(m + 1) * 128], zbuf[:, k, c0:c1]) for k in range(8)], [t_z[ti]] + wC_t, [PB[g]])
                            act(ug[:, m, c0 - 256:c1 - 256], ps[:, g, 0:N], AF.Gelu, [PB[g]], [ug_t[ti - 1]])
                        for sc_ in range(4):
                            n = (ti - 1) * 4 + sc_
                            cc0 = c0 + sc_ * 128
                            g = GB_.next()
                            mm(ps[:, g, :], [(zbuf[:, k, cc0:cc0 + 128], wC[:, k, 512:1024]) for k in range(8)], [t_z[ti]] + wC_t, [PB[g]])
                            mi = tmp_i.next(); m2 = tmp_i.next()
                            act(tmp32[:, mi, :], ps[:, g, :], AF.Gelu, [PB[g]], [t_tmp[mi], t_rd], accum_out=stat[:, 0:1])
                            act(tmp32[:, m2, :], tmp32[:, mi, :], AF.Square, [t_tmp[mi]], [t_tmp[m2], t_rd], accum_out=stat[:, 1:2])

                            def stats(h):
                                h.tensor_scalar(out=stat[:, 2:3], in0=stat[:, 0:1], scalar1=1.0 / 512, scalar2=None, op0=ALU.mult)
                                h.tensor_tensor(out=stat[:, 3:4], in0=stat[:, 2:3], in1=stat[:, 2:3], op=ALU.mult)
                                return h.scalar_tensor_tensor(out=stat[:, 4:5], in0=stat[:, 1:2], scalar=1.0 / 512, in1=stat[:, 3:4], op0=ALU.mult, op1=ALU.subtract)
                            fw.op(DVE, lambda h: h.tensor_scalar(out=stat[:, 2:3], in0=stat[:, 0:1], scalar1=1.0 / 512, scalar2=None, op0=ALU.mult), [t_rd], [t_rd])
                            fw.op(DVE, lambda h: h.tensor_tensor(out=stat[:, 3:4], in0=stat[:, 2:3], in1=stat[:, 2:3], op=ALU.mult), [t_rd], [t_rd])
                            fw.op(DVE, lambda h: h.scalar_tensor_tensor(out=stat[:, 4:5], in0=stat[:, 1:2], scalar=1.0 / 512, in1=stat[:, 3:4], op0=ALU.mult, op1=ALU.subtract), [t_rd], [t_rd])
                            act(stat[:, 5:6], stat[:, 4:5], AF.Ln, [t_rd, t_small], [t_rd], bias=epst[:, 0:1])
                            act(stat[:, 5:6], stat[:, 5:6], AF.Exp, [t_rd], [t_rd], scale=-0.5)
                            fw.op(DVE, lambda h, mi=mi: h.tensor_scalar(out=tmp32[:, mi, :], in0=tmp32[:, mi, :], scalar1=stat[:, 2:3], scalar2=stat[:, 5:6], op0=ALU.subtract, op1=ALU.mult),
                                  [t_tmp[mi], t_rd], [t_tmp[mi]])
                            tt(DVE, tmp32[:, mi, :], tmp32[:, mi, :], lgb[:, 0, :], ALU.mult, [t_tmp[mi]] + lg_t, [t_tmp[mi]])
                            tt(POOL, vln[:, n, :], tmp32[:, mi, :], lgb[:, 1, :], ALU.add, [t_tmp[mi]] + lg_t, [vl_t[n]])
                        for cc in range(4):
                            g = GB_.next()

                            def sp_(h, g=g, cc=cc, ti=ti):
                                ins = None
                                for e in range(2):
                                    for sc_ in range(4):
                                        n = (ti - 1) * 4 + sc_
                                        gg = 2 * cc + e
                                        ins = h.matmul(ps[64 * e:64 * e + 64, g, sc_ * 128:(sc_ + 1) * 128], lhsT=vln[:, n, gg * 64:(gg + 1) * 64], rhs=wst[:, gg, :], start=True, stop=True)
                                return ins
                            fw.op(PE, sp_, [vl_t[(ti - 1) * 4 + i] for i in range(4)] + wst_t, [PB[g]], cost=8 * 0.1)
                            mi = tmp_i.next()
                            for sc_ in range(4):
                                tt(DVE, tmp32[:, mi, sc_ * 128:(sc_ + 1) * 128], ps[:, g, sc_ * 128:(sc_ + 1) * 128], btab[:, cc, :], ALU.add, [PB[g]] + bt_t, [t_tmp[mi]])
                            tt(POOL, ug[:, cc, c0 - 256:c1 - 256], ug[:, cc, c0 - 256:c1 - 256], tmp32[:, mi, :], ALU.mult, [t_tmp[mi], ug_t[ti - 1]], [ug_t[ti - 1]])
                    woC_ap, woC_t = WA.alloc(0, 4096, 1, "woC")
                    woC = woC_ap.rearrange("p (k n) -> p k n", k=4)
                    load_w(woC, odwout_d[0][0:512, :], 4, 1024, woC_t)
                    for ti in range(1, 5):
                        c0, c1, isc = TILES[ti]; N = c1 - c0
                        for mo in range(8):
                            g = GB_.next()
                            mm(ps[:, g, 0:N], [(woC[:, m, mo * 128:(mo + 1) * 128], ug[:, m, c0 - 256:c1 - 256]) for m in range(4)], [ug_t[ti - 1]] + woC_t, [PB[g]])
                            gated_add(g, mo, c0, c1, [ti], l, 16, vb)
                    if stage != 6:
                        ffn(1, range(1, 5), vb)

            ost_ap, ost_t = SC.alloc(0, 4096, 2, "ostg")
            fin_ap, fin_t = SC.alloc(4096, 4096, 1, "fin")
            for ti in (range(1, 5) if 'f' not in SKIP else []):
                c0, c1, isc = TILES[ti]; N = c1 - c0
                if stage == 0:
                    g = GB_.next()
                    for k in range(8):
                        si = sq_i.next()
                        act(sq[:, si, 0:N], resid[:, k, c0:c1], AF.Square, [t_res[k][ti]], [t_sq[si]])
                        fw.op(PE, (lambda h, g=g, si=si, k=k, N=N: h.matmul(ps[:, g, 0:N], lhsT=onesm[:, 0, :], rhs=sq[:, si, 0:N], start=(k == 0), stop=(k == 7))),
                              [t_sq[si], t_small], [PB[g]])
                    ri = rstd_i.next()
                    act(rstd[:, ri, 0:N], ps[:, g, 0:N], AF.Ln, [PB[g], t_small], [t_rstd[ri]], bias=epst[:, 0:1])
                    act(rstd[:, ri, 0:N], rstd[:, ri, 0:N], AF.Exp, [t_rstd[ri]], [t_rstd[ri]], scale=-0.5)
                    for k in range(8):
                        stt(DVE, resid[:, k, c0:c1], resid[:, k, c0:c1], gsm[:, 4, k:k + 1], rstd[:, ri, 0:N], ALU.mult, ALU.mult,
                            [t_res[k][ti], t_small, t_rstd[ri]], [t_res[k][ti]])
                for sc_ in range(4):
                    cc0 = c0 + sc_ * 128
                    bi = sc_ % 2
                    obuf = ost_ap[:, bi * 2048:(bi + 1) * 2048].bitcast(F32)
                    for hk in range(2):
                        g = GB_.next()

                        def trb(h, g=g, hk=hk, cc0=cc0):
                            ins = None
                            for kk in range(4):
                                k = hk * 4 + kk
                                ins = h.transpose(ps[:, g, kk * 128:(kk + 1) * 128], resid[:, k, cc0:cc0 + 128], IDN)
                            return ins
                        fw.op(PE, trb, [t_res[k][ti] for k in range(hk * 4, hk * 4 + 4)] + [t_cst], [PB[g]], cost=1.2)
                        cp(ACT if hk == 0 else DVE, obuf[:, hk * 512:(hk + 1) * 512], ps[:, g, :], [PB[g]], [ost_t[bi]])
                    dma(SP, out_d[s, cc0 - 256:cc0 - 128, :], obuf, reads=[ost_t[bi]], writes=[Tok("o")], key=f"o{bi}")
        fw.emit(st)
    return nc, fw


_CACHE = {}


def kernel(**inputs):
    nseq = int(os.environ.get("K_NSEQ", "2"))
    stage = int(os.environ.get("K_STAGE", "0"))
    key = (nseq, stage)
    if key not in _CACHE:
        _CACHE[key] = build_nc(nseq, stage)
    nc, fw = _CACHE[key]
    consts = make_consts()
    f = lambda a: np.ascontiguousarray(np.asarray(a, dtype=np.float32))
    shared = {k: f(v) for k, v in inputs.items() if k not in ("x", "c", "ctx")}
    shared["consts"] = consts
    x = f(inputs["x"]); c = f(inputs["c"]); ctx = f(inputs["ctx"])
    in_maps = []
    for i in range(8):
        m = dict(shared)
        m["x"] = x[2 * i:2 * i + 2]
        m["c"] = c[2 * i:2 * i + 2]
        m["ctx"] = ctx[2 * i:2 * i + 2]
        in_maps.append(m)
    res = run_bass_kernel_spmd(nc, in_maps, core_ids=list(range(8)))
    out = np.concatenate([np.asarray(r["out"], dtype=np.float32) for r in res.results], axis=0)
    return out
```
